# Optimizing a Trainium2 kernel written in Bass

```python
import math
import jax, jax.numpy as jnp
from jax import lax
import numpy as np

D_MODEL = 1024
BATCH = 8
SEQ = 4096
DEPTH = 1

N_META = 16
NORM_EPS = 1e-6
SSM_WIDTH = D_MODEL // 2
SSM_GROUP = 16
SSM_GROUPS = SSM_WIDTH // SSM_GROUP
SSM_STATE = 64
ML_WIDTH = D_MODEL
ML_HEADS = 4
ML_HEAD_DIM = ML_WIDTH // ML_HEADS
ML_CHUNK = 64
ML_CONV = 4
ML_QKV_BLOCK = 4
PEER_HEADS = 8
PEER_NKEYS = 128
PEER_EXPERTS = PEER_NKEYS * PEER_NKEYS
PEER_TOPK = 16
PEER_QDIM = 256
PEER_HALF = PEER_QDIM // 2
PEER_TOKEN_BLOCK = 256
IN_COLS = SSM_WIDTH + 2 * ML_WIDTH + 2 * D_MODEL

kernel_name = "hybrid_s5_mlstm_peer_block"


def rmsnorm(x, g):
    x32 = x.astype(jnp.float32)
    y = x32 * lax.rsqrt(jnp.mean(x32 * x32, axis=-1, keepdims=True) + NORM_EPS)
    return (y * g.astype(jnp.float32)).astype(x.dtype)


def s5_branch(u, a_re, a_im, log_dt, b_re, b_im, c_re, c_im, d, w_glu, b_glu):
    f32 = jnp.float32
    bn, L, _ = u.shape
    u32 = u.astype(f32).reshape(bn, L, SSM_GROUPS, SSM_GROUP)
    lam = lax.complex(a_re.astype(f32), a_im.astype(f32))
    dt = jnp.exp(log_dt.astype(f32))[:, None]
    a_bar = jnp.exp(lam * dt)
    b_bar = ((a_bar - 1.0) / lam)[..., None] * lax.complex(b_re.astype(f32), b_im.astype(f32))
    bu = jnp.einsum('blgc,gpc->blgp', u32.astype(jnp.complex64), b_bar)
    a_seq = jnp.broadcast_to(a_bar, (1, L) + a_bar.shape)

    def combine(e1, e2):
        a1, b1 = e1
        a2, b2 = e2
        return a1 * a2, a2 * b1 + b2

    _, states = lax.associative_scan(combine, (a_seq, bu), axis=1)
    c = lax.complex(c_re.astype(f32), c_im.astype(f32))
    y = jnp.real(jnp.einsum('blgp,gcp->blgc', states, c)) + d.astype(f32).reshape(SSM_GROUPS, SSM_GROUP) * u32
    y = jax.nn.gelu(y.reshape(bn, L, SSM_WIDTH))
    y = y * jax.nn.sigmoid(y @ w_glu.astype(f32) + b_glu.astype(f32))
    return y.astype(u.dtype)


def causal_depthwise_conv(x, w, b):
    K = w.shape[0]
    L = x.shape[1]
    xp = jnp.pad(x, ((0, 0), (K - 1, 0), (0, 0)))
    return sum(xp[:, j:j + L] * w[j] for j in range(K)) + b


def blockdiag(x, w):
    sh = x.shape
    xb = x.reshape(sh[:-1] + (w.shape[0], w.shape[1]))
    return jnp.einsum('blnc,ncd->blnd', xb, w).reshape(sh)


def mlstm_chunk_step(carry, xs):
    c_mat, n_vec, m_st = carry
    q, k, v, log_i, log_f = xs
    b = jnp.cumsum(log_f, axis=-1)
    g = b[..., -1]
    causal = jnp.tril(jnp.ones((ML_CHUNK, ML_CHUNK), dtype=bool))
    d_log = jnp.where(causal, b[..., :, None] - b[..., None, :] + log_i[..., None, :], -jnp.inf)
    inter_log = b + m_st[..., None]
    m_t = jnp.maximum(inter_log, jnp.max(d_log, axis=-1))
    w_intra = jnp.exp(d_log - m_t[..., None])
    s = jnp.einsum('bhtd,bhsd->bhts', q, k) * w_intra
    w_inter = jnp.exp(inter_log - m_t)
    num = jnp.einsum('bhts,bhsd->bhtd', s, v) + w_inter[..., None] * jnp.einsum('bhed,bhtd->bhte', c_mat, q)
    den = jnp.sum(s, axis=-1) + w_inter * jnp.einsum('bhd,bhtd->bht', n_vec, q)
    h = num / jnp.maximum(jnp.abs(den), jnp.exp(-m_t))[..., None]
    a = g[..., None] - b + log_i
    m_new = jnp.maximum(g + m_st, jnp.max(a, axis=-1))
    w_state = jnp.exp(a - m_new[..., None])
    decay = jnp.exp(g + m_st - m_new)
    c_new = decay[..., None, None] * c_mat + jnp.einsum('bhs,bhse,bhsd->bhed', w_state, v, k)
    n_new = decay[..., None] * n_vec + jnp.einsum('bhs,bhsd->bhd', w_state, k)
    return (c_new, n_new, m_new), h


def mlstm_branch(xm, z, conv_w, conv_b, wq, wk, wv, w_if, b_if, norm_g, skip):
    f32 = jnp.float32
    bn, L, _ = xm.shape
    H, dh = ML_HEADS, ML_HEAD_DIM
    xc = jax.nn.silu(causal_depthwise_conv(xm, conv_w, conv_b))
    q = blockdiag(xc, wq)
    k = blockdiag(xc, wk)
    v = blockdiag(xm, wv)
    gates = (jnp.concatenate([q, k, v], axis=-1) @ w_if + b_if).astype(f32)
    log_i = gates[..., :H]
    log_f = jax.nn.log_sigmoid(gates[..., H:])
    k = k * (dh ** -0.5)
    pad_f = (-N_META) % ML_CHUNK
    pad_b = (-(L + pad_f)) % ML_CHUNK
    n_chunks = (L + pad_f + pad_b) // ML_CHUNK

    def heads_to_chunks(t):
        t = jnp.pad(t.astype(f32).reshape(bn, L, H, dh), ((0, 0), (pad_f, pad_b), (0, 0), (0, 0)))
        return t.reshape(bn, n_chunks, ML_CHUNK, H, dh).transpose(1, 0, 3, 2, 4)

    def gate_to_chunks(t):
        t = jnp.pad(t, ((0, 0), (pad_f, pad_b), (0, 0)))
        return t.reshape(bn, n_chunks, ML_CHUNK, H).transpose(1, 0, 3, 2)

    init = (jnp.zeros((bn, H, dh, dh), f32), jnp.zeros((bn, H, dh), f32), jnp.zeros((bn, H), f32))
    _, hc = lax.scan(mlstm_chunk_step, init,
                     (heads_to_chunks(q), heads_to_chunks(k), heads_to_chunks(v),
                      gate_to_chunks(log_i), gate_to_chunks(log_f)))
    h = hc.transpose(1, 0, 3, 2, 4).reshape(bn, n_chunks * ML_CHUNK, H, dh)[:, pad_f:pad_f + L]
    h = jax.nn.sigmoid(z.astype(f32)).reshape(bn, L, H, dh) * h
    h = h * lax.rsqrt(jnp.mean(h * h, axis=-1, keepdims=True) + NORM_EPS)
    h = h.reshape(bn, L, ML_WIDTH) * norm_g.astype(f32) + skip.astype(f32) * xc.astype(f32)
    return h.astype(xm.dtype)


def peer_ffn(xn, w_q, sub_keys, u_tab, v_tab):
    f32 = jnp.float32
    bn, L, D = xn.shape
    T = bn * L
    n_blk = -(-T // PEER_TOKEN_BLOCK)
    xt = jnp.pad(xn.reshape(T, D), ((0, n_blk * PEER_TOKEN_BLOCK - T), (0, 0)))
    xt = xt.reshape(n_blk, PEER_TOKEN_BLOCK, D)
    keys32 = sub_keys.astype(f32)

    def block(xb):
        q = (xb @ w_q).astype(f32).reshape(PEER_TOKEN_BLOCK, PEER_HEADS, 2, PEER_HALF)
        s = jnp.einsum('thid,hikd->thik', q, keys32)
        sv, si = lax.top_k(s, PEER_TOPK)
        cand = sv[..., 0, :, None] + sv[..., 1, None, :]
        cidx = si[..., 0, :, None] * PEER_NKEYS + si[..., 1, None, :]
        cand = cand.reshape(PEER_TOKEN_BLOCK, PEER_HEADS, PEER_TOPK * PEER_TOPK)
        cidx = cidx.reshape(PEER_TOKEN_BLOCK, PEER_HEADS, PEER_TOPK * PEER_TOPK)
        cs, ci = lax.top_k(cand, PEER_TOPK)
        eidx = jnp.take_along_axis(cidx, ci, axis=-1)
        gate = jax.nn.softmax(cs, axis=-1)
        ug = u_tab[eidx]
        act = jax.nn.gelu(jnp.einsum('td,thkd->thk', xb, ug).astype(f32))
        vg = v_tab[eidx]
        return jnp.einsum('thk,thkd->td', (gate * act).astype(vg.dtype), vg)

    out = lax.map(block, xt).reshape(n_blk * PEER_TOKEN_BLOCK, D)[:T]
    return out.reshape(bn, L, D).astype(xn.dtype)


def setup_inputs(seed: int = 0) -> dict:
    key = jax.random.key(seed)
    ks = list(jax.random.split(key, 40))
    f32 = jnp.float32
    it = iter(ks)

    def nrm(shape, scale):
        return jax.random.normal(next(it), shape, f32) * scale

    def gain(shape):
        return 1.0 + nrm(shape, 0.02)

    Dp = DEPTH
    x = nrm((BATCH, SEQ, D_MODEL), 1.0)
    meta_tokens = nrm((N_META, D_MODEL), 1.0)
    norm1_g = gain((Dp, D_MODEL))
    w_in = nrm((Dp, D_MODEL, IN_COLS), D_MODEL ** -0.5)
    ssm_a_re = -0.5 * jnp.exp(nrm((Dp, SSM_GROUPS, SSM_STATE), 0.05))
    ssm_a_im = jnp.pi * jnp.arange(SSM_STATE, dtype=f32) + nrm((Dp, SSM_GROUPS, SSM_STATE), 0.01)
    ssm_log_dt = jax.random.uniform(next(it), (Dp, SSM_GROUPS), f32, math.log(1e-3), math.log(1e-1))
    ssm_b_re = nrm((Dp, SSM_GROUPS, SSM_STATE, SSM_GROUP), (2 * SSM_GROUP) ** -0.5)
    ssm_b_im = nrm((Dp, SSM_GROUPS, SSM_STATE, SSM_GROUP), (2 * SSM_GROUP) ** -0.5)
    ssm_c_re = nrm((Dp, SSM_GROUPS, SSM_GROUP, SSM_STATE), (2 * SSM_STATE) ** -0.5)
    ssm_c_im = nrm((Dp, SSM_GROUPS, SSM_GROUP, SSM_STATE), (2 * SSM_STATE) ** -0.5)
    ssm_d = nrm((Dp, SSM_WIDTH), 0.5)
    ssm_w_glu = nrm((Dp, SSM_WIDTH, SSM_WIDTH), SSM_WIDTH ** -0.5)
    ssm_b_glu = nrm((Dp, SSM_WIDTH), 0.02)
    w_ssm_out = nrm((Dp, SSM_WIDTH, D_MODEL), SSM_WIDTH ** -0.5)
    ml_conv_w = nrm((Dp, ML_CONV, ML_WIDTH), 0.5)
    ml_conv_b = nrm((Dp, ML_WIDTH), 0.02)
    nb = ML_WIDTH // ML_QKV_BLOCK
    ml_wq = nrm((Dp, nb, ML_QKV_BLOCK, ML_QKV_BLOCK), ML_QKV_BLOCK ** -0.5)
    ml_wk = nrm((Dp, nb, ML_QKV_BLOCK, ML_QKV_BLOCK), ML_QKV_BLOCK ** -0.5)
    ml_wv = nrm((Dp, nb, ML_QKV_BLOCK, ML_QKV_BLOCK), ML_QKV_BLOCK ** -0.5)
    ml_w_if = nrm((Dp, 3 * ML_WIDTH, 2 * ML_HEADS), (3 * ML_WIDTH) ** -0.5)
    ml_b_if = jnp.concatenate([nrm((Dp, ML_HEADS), 0.1),
                               3.0 + 3.0 * jax.random.uniform(next(it), (Dp, ML_HEADS), f32)], axis=-1)
    ml_norm_g = gain((Dp, ML_WIDTH))
    ml_skip = 1.0 + nrm((Dp, ML_WIDTH), 0.1)
    w_ml_out = nrm((Dp, ML_WIDTH, D_MODEL), ML_WIDTH ** -0.5)
    w_out = nrm((Dp, D_MODEL, D_MODEL), D_MODEL ** -0.5)
    norm2_g = gain((Dp, D_MODEL))
    peer_w_q = nrm((Dp, D_MODEL, PEER_HEADS * PEER_QDIM), D_MODEL ** -0.5)
    peer_sub_keys = nrm((Dp, PEER_HEADS, 2, PEER_NKEYS, PEER_HALF), PEER_HALF ** -0.5)
    peer_u = nrm((Dp, PEER_EXPERTS, D_MODEL), D_MODEL ** -0.5)
    peer_v = nrm((Dp, PEER_EXPERTS, D_MODEL), 0.5 * PEER_HEADS ** -0.5)
    final_norm_g = gain((D_MODEL,))
    return {"x": x, "meta_tokens": meta_tokens, "norm1_g": norm1_g, "w_in": w_in,
            "ssm_a_re": ssm_a_re, "ssm_a_im": ssm_a_im, "ssm_log_dt": ssm_log_dt,
            "ssm_b_re": ssm_b_re, "ssm_b_im": ssm_b_im, "ssm_c_re": ssm_c_re, "ssm_c_im": ssm_c_im,
            "ssm_d": ssm_d, "ssm_w_glu": ssm_w_glu, "ssm_b_glu": ssm_b_glu, "w_ssm_out": w_ssm_out,
            "ml_conv_w": ml_conv_w, "ml_conv_b": ml_conv_b, "ml_wq": ml_wq, "ml_wk": ml_wk, "ml_wv": ml_wv,
            "ml_w_if": ml_w_if, "ml_b_if": ml_b_if, "ml_norm_g": ml_norm_g, "ml_skip": ml_skip,
            "w_ml_out": w_ml_out, "w_out": w_out, "norm2_g": norm2_g, "peer_w_q": peer_w_q,
            "peer_sub_keys": peer_sub_keys, "peer_u": peer_u, "peer_v": peer_v,
            "final_norm_g": final_norm_g}


def reference(x, meta_tokens, norm1_g, w_in, ssm_a_re, ssm_a_im, ssm_log_dt, ssm_b_re, ssm_b_im,
              ssm_c_re, ssm_c_im, ssm_d, ssm_w_glu, ssm_b_glu, w_ssm_out, ml_conv_w, ml_conv_b,
              ml_wq, ml_wk, ml_wv, ml_w_if, ml_b_if, ml_norm_g, ml_skip, w_ml_out, w_out, norm2_g,
              peer_w_q, peer_sub_keys, peer_u, peer_v, final_norm_g):
    bn = x.shape[0]
    meta = jnp.broadcast_to(meta_tokens.astype(x.dtype)[None], (bn, N_META, D_MODEL))
    h = jnp.concatenate([meta, x], axis=1)
    splits = [SSM_WIDTH, SSM_WIDTH + ML_WIDTH, SSM_WIDTH + 2 * ML_WIDTH,
              SSM_WIDTH + 2 * ML_WIDTH + D_MODEL]
    for l in range(DEPTH):
        xn = rmsnorm(h, norm1_g[l])
        proj = xn @ w_in[l]
        u_s, x_m, z_m, g_s, g_m = jnp.split(proj, splits, axis=-1)
        y_s = s5_branch(u_s, ssm_a_re[l], ssm_a_im[l], ssm_log_dt[l], ssm_b_re[l], ssm_b_im[l],
                        ssm_c_re[l], ssm_c_im[l], ssm_d[l], ssm_w_glu[l], ssm_b_glu[l]) @ w_ssm_out[l]
        y_m = mlstm_branch(x_m, z_m, ml_conv_w[l], ml_conv_b[l], ml_wq[l], ml_wk[l], ml_wv[l],
                           ml_w_if[l], ml_b_if[l], ml_norm_g[l], ml_skip[l]) @ w_ml_out[l]
        mixed = jax.nn.sigmoid(g_s) * y_s + jax.nn.sigmoid(g_m) * y_m
        h = h + mixed @ w_out[l]
        h = h + peer_ffn(rmsnorm(h, norm2_g[l]), peer_w_q[l], peer_sub_keys[l], peer_u[l], peer_v[l])
    return rmsnorm(h, final_norm_g)[:, N_META:]
```

```python
import math
import types
import numpy as np
from contextlib import ExitStack
import concourse.bass as bass
import concourse.mybir as mybir
from concourse.bass_utils import run_bass_kernel_spmd

F32 = mybir.dt.float32
BF16 = mybir.dt.bfloat16
I32 = mybir.dt.int32
U32 = mybir.dt.uint32
AF = mybir.ActivationFunctionType
ALU = mybir.AluOpType
AX = mybir.AxisListType

ENGINES = ("pe", "dve", "act", "pool", "sp")
EPOCH = 20000
N_DMA_SEMS = 24
RING_SIZES = {"cv": 6, "gq": 12}
L = 4112
NMETA = 16
D = 1024
EPS = 1e-6


class Sched:
    def __init__(self, nc, stack):
        self.nc = nc
        self.stack = stack
        self.ops = []
        self.buf = {}
        self.sem_cache = {}
        self.cnt = {e: 0 for e in ENGINES}
        self.dma_cnt = [0] * N_DMA_SEMS
        self.dma_rr = 0
        self.ring_cnt = {}
        self.ring_rr = {}
        self.emitted = 0
        self.fence_tickets = []

    def _sem(self, name):
        if name not in self.sem_cache:
            self.sem_cache[name] = self.stack.enter_context(self.nc.semaphore(name))
        return self.sem_cache[name]

    @staticmethod
    def _snap(fn):
        if fn.__closure__ is None:
            return fn
        cells = []
        for c in fn.__closure__:
            try:
                cells.append(types.CellType(c.cell_contents))
            except ValueError:
                cells.append(c)
        return types.FunctionType(fn.__code__, fn.__globals__, fn.__name__, fn.__defaults__, tuple(cells))

    def op(self, eng, fn, reads=(), writes=(), dma=False, detached=False, ring="dma"):
        fn = self._snap(fn)
        deps = {}
        for k in reads:
            st = self.buf.get(k)
            if st is not None and st["w"] is not None:
                deps[st["w"]] = True
        for k in writes:
            st = self.buf.get(k)
            if st is not None:
                if st["w"] is not None:
                    deps.setdefault(st["w"], False)
                for r_ in st["r"]:
                    deps.setdefault(r_, False)
        oid = len(self.ops)
        self.ops.append(dict(eng=eng, fn=fn, deps=deps, dma=dma, signal=dma, ticket=None, detached=detached, ring=ring))
        for k in reads:
            st = self.buf.setdefault(k, {"w": None, "r": []})
            st["r"].append(oid)
        for k in writes:
            self.buf[k] = {"w": oid, "r": []}
        return oid

    def emit(self, final=False):
        ops = self.ops
        start = self.emitted
        new = ops[start:]
        for o in new:
            live = set()
            for d, raw in o["deps"].items():
                if d < start and not (ops[d]["detached"] and ops[d]["dma"]):
                    continue
                od = ops[d]
                if od["eng"] == o["eng"] and not od["dma"] and not o["dma"] and (o["eng"] == "pe" or (RELAX and not raw)):
                    continue
                od["signal"] = True
                live.add(d)
            o["deps"] = live
        last = {}
        for o in new:
            if not o["dma"] and not o["detached"]:
                last[o["eng"]] = o
        for o in last.values():
            o["signal"] = True
        for o in new:
            if o["dma"] and o["ring"] != "dma":
                rn = o["ring"]
                cntl = self.ring_cnt.setdefault(rn, [0] * RING_SIZES[rn])
                j = self.ring_rr.get(rn, 0) % RING_SIZES[rn]
                self.ring_rr[rn] = self.ring_rr.get(rn, 0) + 1
                o["prev_dma"] = (f"{rn}{j}", cntl[j])
                cntl[j] += 16
                o["ticket"] = (f"{rn}{j}", cntl[j])
            elif o["dma"]:
                j = self.dma_rr % N_DMA_SEMS
                self.dma_rr += 1
                o["prev_dma"] = (f"dma{j}", self.dma_cnt[j])
                self.dma_cnt[j] += 16
                o["ticket"] = (f"dma{j}", self.dma_cnt[j])
            elif o["signal"]:
                e = o["eng"]
                self.cnt[e] += 1
                ep = (self.cnt[e] - 1) // EPOCH
                o["ticket"] = (f"c_{e}_{ep}", (self.cnt[e] - 1) % EPOCH + 1)
        for o in new:
            if o["ticket"] is not None:
                self._sem(o["ticket"][0])
        per_eng = {e: [] for e in ENGINES}
        for o in new:
            per_eng[o["eng"]].append(o)
        fence_in = list(self.fence_tickets)
        fence_out = [o["ticket"] for o in last.values()] + [(f"dma{j}", c) for j, c in enumerate(self.dma_cnt) if c > 0]
        fence_out += [(f"gq{j}", c) for j, c in enumerate(self.ring_cnt.get("gq", [])) if c > 0]

        def replay(engname):
            def body(eng):
                waited = {}
                for s, v in fence_in:
                    eng.wait_ge(self._sem(s), v)
                    waited[s] = max(waited.get(s, 0), v)
                for o in per_eng[engname]:
                    need = {}
                    for d in o["deps"]:
                        s, v = ops[d]["ticket"]
                        if need.get(s, 0) < v:
                            need[s] = v
                    if o["dma"]:
                        s, v = o["prev_dma"]
                        if v > 0 and need.get(s, 0) < v:
                            need[s] = v
                    for s, v in need.items():
                        if waited.get(s, 0) >= v:
                            continue
                        eng.wait_ge(self._sem(s), v)
                        waited[s] = v
                    ins = o["fn"](eng)
                    if o["signal"]:
                        s, v = o["ticket"]
                        ins.then_inc(self._sem(s), 16 if o["dma"] else 1)
                if final:
                    for s, v in fence_out:
                        if waited.get(s, 0) < v:
                            eng.wait_ge(self._sem(s), v)
            return body

        with self.nc.Block() as block:
            block.tensor(replay("pe"))
            block.vector(replay("dve"))
            block.scalar(replay("act"))
            block.gpsimd(replay("pool"))
            block.sync(replay("sp"))
        self.fence_tickets = fence_out
        self.emitted = len(ops)


GRING = "dma"
RELAX = False
TWO_PI = 2.0 * math.pi
CW1 = 6.28125
CW2 = TWO_PI - CW1


def build(passes=("A", "B", "C"), debug=False, nblkA=9, nblkB=17, ntileC=32):
    nc = bass.Bass("TRN2", target_bir_lowering=False)

    def din(name, shape, dt=F32):
        return nc.dram_tensor(name, list(shape), dt, kind="ExternalInput").ap()

    xh = din("xh", [L, D])
    w_in = din("w_in", [D, 4608])
    w_glu = din("w_glu", [512, 512])
    w_sso = din("w_sso", [512, D])
    w_mlo = din("w_mlo", [D, D])
    w_out = din("w_out", [D, D])
    w_pq = din("w_pq", [D, 2048])
    n1g = din("n1g", [128, D])
    mlng = din("mlng", [128, D])
    n2g = din("n2g", [128, D])
    fng = din("fng", [128, D])
    s5a = din("s5a", [128, 3, 16])
    s5bw = din("s5bw", [128, 2, 16, 128])
    s5cw = din("s5cw", [128, 2, 16, 128])
    s5v = din("s5v", [128, 2, 4])
    mlv = din("mlv", [128, 6, 8])
    mlbd = din("mlbd", [128, 5, 8, 128])
    mlbdv = din("mlbdv", [128, 8, 128])
    mlwif = din("mlwif", [128, 24, 8])
    mlbif = din("mlbif", [128, 8])
    keysT = din("keysT", [128, 16, 128])
    u_tab = din("u_tab", [16384, D])
    v_tab = din("v_tab", [16384, D])
    cst = din("cst", [128, 4, 128])
    kind_s = "ExternalOutput" if debug else "Internal"
    mixs = nc.dram_tensor("mixs", [D, L], BF16, kind=kind_s).ap()
    h1 = nc.dram_tensor("h1", [4096, D], F32, kind=kind_s).ap()
    out = nc.dram_tensor("out", [4096, D], F32, kind="ExternalOutput").ap()
    uvb = nc.dram_tensor("uvb", [16384, 2 * D], BF16, kind="Internal").ap()

    with ExitStack() as top:
        S = Sched(nc, top)
        V = lambda fn, r, w: S.op("dve", fn, reads=r, writes=w)
        A = lambda fn, r, w: S.op("act", fn, reads=r, writes=w)
        G = lambda fn, r, w: S.op("pool", fn, reads=r, writes=w)
        T = lambda fn, r, w: S.op("pe", fn, reads=r, writes=w)
        DM = lambda fn, r, w: S.op("sp", fn, reads=r, writes=w, dma=True)

        ps = [top.enter_context(nc.psum_tensor(f"ps{i}", [128, 512], F32)) for i in range(3)]
        psS = top.enter_context(nc.psum_tensor("psS", [128, 1024], F32))
        ps += [psS[:, 0:512], psS[:, 512:1024]]
        ps += [top.enter_context(nc.psum_tensor(f"ps{i}", [128, 512], F32)) for i in range(5, 7)]
        psb = top.enter_context(nc.psum_tensor("psb", [128, 1024], BF16))

        def sbt(stack, name, shape, dt):
            return stack.enter_context(nc.sbuf_tensor("sb_" + name, list(shape), dt))

        cstf = sbt(top, "cstf", [128, 4, 128], F32)
        identb = sbt(top, "identb", [128, 128], BF16)
        DM(lambda e: e.dma_start(out=cstf[:], in_=cst), [], ["cstf"])
        V(lambda e: e.tensor_copy(out=identb[:], in_=cstf[:, 0, :]), ["cstf"], ["identb"])
        identf = cstf[:, 0, :]
        maskT = cstf[:, 1, :]
        iota_f = cstf[:, 2, :]
        ones_f = cstf[:, 3, :]

        def load_weight(stack, name, src, rows, c0, c1, dst, dcol0, stg, tagn):
            nk = rows // 128
            w = c1 - c0
            i = 0
            for k in range(nk):
                SW = stg.shape[2]
                for s0 in range(0, w, SW):
                    s1 = min(w, s0 + SW)
                    b = load_weight.ctr % 2
                    load_weight.ctr += 1
                    DM(lambda e, b=b, k=k, s0=s0, s1=s1: e.dma_start(out=stg[:, b, 0:s1 - s0], in_=src[k * 128:(k + 1) * 128, c0 + s0:c0 + s1]),
                       [], [f"stg{b}"])
                    fn = lambda e, b=b, k=k, s0=s0, s1=s1: e.tensor_copy(out=dst[:, k, dcol0 + s0:dcol0 + s1], in_=stg[:, b, 0:s1 - s0])
                    if i % 2 == 0:
                        A(lambda e, b=b, k=k, s0=s0, s1=s1: e.activation(out=dst[:, k, dcol0 + s0:dcol0 + s1], in_=stg[:, b, 0:s1 - s0], func=AF.Copy), [f"stg{b}"], [name])
                    else:
                        V(fn, [f"stg{b}"], [name])
                    i += 1
        load_weight.ctr = 0

        def rms_rows(P, x_ap, g_ap, outs, kx, tag, scr):
            junk, ssq, rs = scr["junk"], scr["ssq"], scr["rs"]
            tag = scr["tag"]
            A(lambda e: e.activation(out=junk[:P, :], in_=x_ap, func=AF.Square, accum_out=ssq[:P, 0:1]), [kx], [tag + "junk", tag + "ssq"])
            A(lambda e: e.activation(out=rs[:P, 1:2], in_=ssq[:P, 0:1], func=AF.Ln, scale=1.0 / D, bias=EPS), [tag + "ssq"], [tag + "rs1"])
            A(lambda e: e.activation(out=rs[:P, 2:3], in_=rs[:P, 1:2], func=AF.Exp, scale=-0.5), [tag + "rs1"], [tag + "rs2"])
            for (o_ap, ko) in outs:
                V(lambda e, o_ap=o_ap: e.scalar_tensor_tensor(out=o_ap, in0=x_ap, scalar=rs[:P, 2:3], in1=g_ap, op0=ALU.mult, op1=ALU.mult),
                  [kx, tag + "rs2"], [ko])

        def transpose_rows(P, xn_bf_ap, kxn, dst_ap, kdst):
            for c in range(8):
                T(lambda e, c=c: e.transpose(out=psb[:, c * 128:c * 128 + P], in_=xn_bf_ap[:, c * 128:(c + 1) * 128], identity=identb[:P, :P]),
                  [kxn, "identb"], ["psb"])
            A(lambda e: e.activation(out=dst_ap, in_=psb[:, :].rearrange("p (c t) -> p c t", c=8)[:, :, :P], func=AF.Copy), ["psb"], [kdst])

        conv_keys = []
        if "C" in passes:
            CW = 1024
            cvf = [sbt(top, f"cvf{i}", [128, CW], F32) for i in range(2)]
            cvb = [sbt(top, f"cvb{i}", [128, CW], BF16) for i in range(2)]
            chunks = [(tsrc, half, c) for (tsrc, half) in ((u_tab, 0), (v_tab, 1)) for c in range(131072 // CW)]
            def cv_in(i):
                tsrc, half, c = chunks[i]
                S.op("pool", lambda e, i=i, tsrc=tsrc, c=c: e.dma_start(out=cvf[i % 2][:, :], in_=tsrc.rearrange("(p r) d -> p (r d)", p=128)[:, c * CW:(c + 1) * CW]),
                     reads=[], writes=[f"cvf{i % 2}"], dma=True, detached=True, ring="cv")
            cvs = {"i": 0}

            def conv_batch(n):
                for _ in range(n):
                    i = cvs["i"]
                    if i >= len(chunks):
                        return
                    if i == 0:
                        cv_in(0)
                    if i + 1 < len(chunks):
                        cv_in(i + 1)
                    tsrc, half, c = chunks[i]
                    S.op("pool", lambda e, i=i: e.tensor_copy(out=cvb[i % 2][:, :], in_=cvf[i % 2][:, :]), reads=[f"cvf{i % 2}"], writes=[f"cvb{i % 2}"], detached=True)
                    S.op("pool", lambda e, i=i, half=half, c=c: e.dma_start(out=uvb.rearrange("(p r) c -> p r c", p=128)[:, c, half * D:(half + 1) * D], in_=cvb[i % 2][:, :]),
                         reads=[f"cvb{i % 2}"], writes=[f"cvk{i}"], dma=True, detached=True, ring="cv")
                    conv_keys.append(f"cvk{i}")
                    cvs["i"] = i + 1
        else:
            def conv_batch(n):
                return

        if "A" in passes:
            with ExitStack() as sa:
                WinA = sbt(sa, "WinA", [128, 8, 1536], BF16)
                Wglu = sbt(sa, "Wglu", [128, 4, 512], BF16)
                Wsso = sbt(sa, "Wsso", [128, 4, 1024], BF16)
                n1g_t = sbt(sa, "n1gA", [128, D], F32)
                sv_ = sbt(sa, "s5v", [128, 2, 4], F32)
                sm = sbt(sa, "s5sm", [128, 24, 16], F32)
                cosT = sbt(sa, "cosT", [128, 16, 128], F32)
                sinT = sbt(sa, "sinT", [128, 16, 128], F32)
                BwT = sbt(sa, "BwT", [128, 2, 16, 128], BF16)
                CwT = sbt(sa, "CwT", [128, 2, 16, 128], BF16)
                DEC = sbt(sa, "DEC", [128, 16, 128], F32)
                sa_outer = sa
                sa = ExitStack()
                sa.__enter__()
                stg = sbt(sa, "stgA", [128, 2, 2048], F32)
                load_weight(sa, "WinA", w_in, 1024, 0, 512, WinA, 0, stg, "a")
                load_weight(sa, "WinA", w_in, 1024, 2560, 3584, WinA, 512, stg, "a")
                load_weight(sa, "Wglu", w_glu, 512, 0, 512, Wglu, 0, stg, "a")
                load_weight(sa, "Wsso", w_sso, 512, 0, 1024, Wsso, 0, stg, "a")
                DM(lambda e: e.dma_start(out=n1g_t[:], in_=n1g), [], ["n1g"])
                pa = sbt(sa, "s5a", [128, 3, 16], F32)
                bw = sbt(sa, "s5bw", [128, 2, 16, 128], F32)
                cw = sbt(sa, "s5cw", [128, 2, 16, 128], F32)
                DM(lambda e: e.dma_start(out=pa[:], in_=s5a), [], ["pa"])
                DM(lambda e: e.dma_start(out=bw[:], in_=s5bw), [], ["bw"])
                DM(lambda e: e.dma_start(out=cw[:], in_=s5cw), [], ["cw"])
                DM(lambda e: e.dma_start(out=sv_[:], in_=s5v), [], ["s5v"])
                DT_, LR, TH, RHO, STH, CTH, AR_, AI_, NR, DEN, FRE, FIM, T1, T2, CR128, SR128, CR16, SR16, TH128, TH16, FIMN = range(21)
                are = pa[:, 0, :]
                aim = pa[:, 1, :]
                A(lambda e: e.activation(out=sm[:, DT_, :], in_=pa[:, 2, :], func=AF.Exp), ["pa"], ["sm_dt"])
                V(lambda e: e.tensor_tensor(out=sm[:, LR, :], in0=are, in1=sm[:, DT_, :], op=ALU.mult), ["pa", "sm_dt"], ["sm_lr"])
                V(lambda e: e.tensor_tensor(out=sm[:, TH, :], in0=aim, in1=sm[:, DT_, :], op=ALU.mult), ["pa", "sm_dt"], ["sm_th"])
                A(lambda e: e.activation(out=sm[:, RHO, :], in_=sm[:, LR, :], func=AF.Exp), ["sm_lr"], ["sm_rho"])
                V(lambda e: e.tensor_scalar(out=sm[:, TH128, :], in0=sm[:, TH, :], scalar1=128.0, scalar2=None, op0=ALU.mult), ["sm_th"], ["sm_th128"])
                V(lambda e: e.tensor_scalar(out=sm[:, TH16, :], in0=sm[:, TH, :], scalar1=16.0, scalar2=None, op0=ALU.mult), ["sm_th"], ["sm_th16"])

                thT = sbt(sa, "thT", [128, 16, 128], F32)
                scs = [sbt(sa, f"scs{i}", [128, 2048], F32) for i in range(4)]
                sci = sbt(sa, "sci", [128, 2048], I32)

                def sincos(x_ap, kx, Fn, sin_ap, ksin, cos_ap, kcos, tag):
                    k_ = scs[0][:, :Fn]; y_ = scs[1][:, :Fn]; s1 = scs[2][:, :Fn]; s2 = scs[3][:, :Fn]; ki = sci[:, :Fn]
                    shp = list(x_ap.shape)
                    def vw(ap):
                        if len(shp) == 3:
                            return ap.rearrange("p (a b) -> p a b", a=shp[1])
                        return ap
                    V(lambda e: e.tensor_scalar(out=vw(k_), in0=x_ap, scalar1=1.0 / TWO_PI, scalar2=None, op0=ALU.mult), [kx], ["scs0"])
                    V(lambda e: e.tensor_copy(out=ki, in_=k_), ["scs0"], ["sci"])
                    V(lambda e: e.tensor_copy(out=k_, in_=ki), ["sci"], ["scs0"])
                    V(lambda e: e.scalar_tensor_tensor(out=vw(y_), in0=vw(k_), scalar=-CW1, in1=x_ap, op0=ALU.mult, op1=ALU.add), ["scs0", kx], ["scs1"])
                    V(lambda e: e.scalar_tensor_tensor(out=y_, in0=k_, scalar=-CW2, in1=y_, op0=ALU.mult, op1=ALU.add), ["scs0", "scs1"], ["scs1"])
                    A(lambda e: e.activation(out=s1, in_=y_, func=AF.Sin, scale=0.5), ["scs1"], ["scs2"])
                    A(lambda e: e.activation(out=s2, in_=y_, func=AF.Sin, scale=0.25), ["scs1"], ["scs3"])
                    V(lambda e: e.tensor_tensor(out=k_, in0=s2, in1=s2, op=ALU.mult), ["scs3"], ["scs0"])
                    V(lambda e: e.tensor_scalar(out=k_, in0=k_, scalar1=-2.0, scalar2=1.0, op0=ALU.mult, op1=ALU.add), ["scs0"], ["scs0"])
                    V(lambda e: e.scalar_tensor_tensor(out=sin_ap, in0=vw(s1), scalar=2.0, in1=vw(k_), op0=ALU.mult, op1=ALU.mult), ["scs2", "scs0"], [ksin])
                    V(lambda e: e.tensor_tensor(out=y_, in0=s1, in1=s1, op=ALU.mult), ["scs2"], ["scs1"])
                    V(lambda e: e.tensor_scalar(out=cos_ap, in0=vw(y_), scalar1=-2.0, scalar2=1.0, op0=ALU.mult, op1=ALU.add), ["scs1"], [kcos])

                sincos(sm[:, TH, :], "sm_th", 16, sm[:, STH, :], "sm_sth", sm[:, CTH, :], "sm_cth", "a")
                sincos(sm[:, TH128, :], "sm_th128", 16, sm[:, SR128, :], "sm_sr128", sm[:, CR128, :], "sm_cr128", "b")
                sincos(sm[:, TH16, :], "sm_th16", 16, sm[:, SR16, :], "sm_sr16", sm[:, CR16, :], "sm_cr16", "c")
                for j in range(16):
                    V(lambda e, j=j: e.tensor_scalar(out=thT[:, j, :], in0=iota_f, scalar1=sm[:, TH, j:j + 1], scalar2=None, op0=ALU.mult), ["cstf", "sm_th"], ["thT"])
                sincos(thT[:, :, :], "thT", 2048, sinT[:, :, :], "sinT", cosT[:, :, :], "cosT", "d")
                V(lambda e: e.tensor_tensor(out=sm[:, AR_, :], in0=sm[:, RHO, :], in1=sm[:, CTH, :], op=ALU.mult), ["sm_rho", "sm_cth"], ["sm_ar"])
                V(lambda e: e.tensor_tensor(out=sm[:, AI_, :], in0=sm[:, RHO, :], in1=sm[:, STH, :], op=ALU.mult), ["sm_rho", "sm_sth"], ["sm_ai"])
                V(lambda e: e.tensor_scalar(out=sm[:, NR, :], in0=sm[:, AR_, :], scalar1=-1.0, scalar2=None, op0=ALU.add), ["sm_ar"], ["sm_nr"])
                V(lambda e: e.tensor_tensor(out=sm[:, T1, :], in0=are, in1=are, op=ALU.mult), ["pa"], ["sm_t1"])
                V(lambda e: e.tensor_tensor(out=sm[:, T2, :], in0=aim, in1=aim, op=ALU.mult), ["pa"], ["sm_t2"])
                V(lambda e: e.tensor_tensor(out=sm[:, DEN, :], in0=sm[:, T1, :], in1=sm[:, T2, :], op=ALU.add), ["sm_t1", "sm_t2"], ["sm_den"])
                V(lambda e: e.reciprocal(out=sm[:, DEN, :], in_=sm[:, DEN, :]), ["sm_den"], ["sm_den"])
                V(lambda e: e.tensor_tensor(out=sm[:, T1, :], in0=sm[:, NR, :], in1=are, op=ALU.mult), ["sm_nr", "pa", "sm_den"], ["sm_t1"])
                V(lambda e: e.tensor_tensor(out=sm[:, T2, :], in0=sm[:, AI_, :], in1=aim, op=ALU.mult), ["sm_ai", "pa", "sm_den"], ["sm_t2"])
                V(lambda e: e.tensor_tensor(out=sm[:, FRE, :], in0=sm[:, T1, :], in1=sm[:, T2, :], op=ALU.add), ["sm_t1", "sm_t2"], ["sm_fre"])
                V(lambda e: e.tensor_tensor(out=sm[:, FRE, :], in0=sm[:, FRE, :], in1=sm[:, DEN, :], op=ALU.mult), ["sm_fre", "sm_den"], ["sm_fre"])
                V(lambda e: e.tensor_tensor(out=sm[:, T1, :], in0=sm[:, AI_, :], in1=are, op=ALU.mult), ["sm_ai", "pa", "sm_fre"], ["sm_t1"])
                V(lambda e: e.tensor_tensor(out=sm[:, T2, :], in0=sm[:, NR, :], in1=aim, op=ALU.mult), ["sm_nr", "pa", "sm_fre"], ["sm_t2"])
                V(lambda e: e.tensor_tensor(out=sm[:, FIM, :], in0=sm[:, T1, :], in1=sm[:, T2, :], op=ALU.subtract), ["sm_t1", "sm_t2"], ["sm_fim"])
                V(lambda e: e.tensor_tensor(out=sm[:, FIM, :], in0=sm[:, FIM, :], in1=sm[:, DEN, :], op=ALU.mult), ["sm_fim", "sm_den"], ["sm_fim"])
                bb = sbt(sa, "bb", [128, 3, 128], F32)
                for j in range(16):
                    V(lambda e, j=j: e.tensor_scalar(out=bb[:, 2, :], in0=bw[:, 1, j, :], scalar1=sm[:, FIM, j:j + 1], scalar2=None, op0=ALU.mult), ["bw", "sm_fim"], ["bb2"])
                    V(lambda e, j=j: e.scalar_tensor_tensor(out=bb[:, 0, :], in0=bw[:, 0, j, :], scalar=sm[:, FRE, j:j + 1], in1=bb[:, 2, :], op0=ALU.mult, op1=ALU.subtract), ["bw", "sm_fre", "bb2"], ["bb0"])
                    V(lambda e, j=j: e.tensor_scalar(out=bb[:, 2, :], in0=bw[:, 1, j, :], scalar1=sm[:, FRE, j:j + 1], scalar2=None, op0=ALU.mult), ["bw", "sm_fre", "bb0"], ["bb2"])
                    V(lambda e, j=j: e.scalar_tensor_tensor(out=bb[:, 1, :], in0=bw[:, 0, j, :], scalar=sm[:, FIM, j:j + 1], in1=bb[:, 2, :], op0=ALU.mult, op1=ALU.add), ["bw", "sm_fim", "bb2"], ["bb1"])
                    for ri in range(2):
                        T(lambda e, ri=ri: e.transpose(out=ps[0][:, ri * 128:(ri + 1) * 128], in_=bb[:, ri, :], identity=identf), [f"bb{ri}", "cstf"], ["ps0"])
                    A(lambda e, j=j: e.activation(out=BwT[:, :, j, :], in_=ps[0][:, 0:256].rearrange("p (r m) -> p r m", r=2), func=AF.Copy), ["ps0"], ["BwT"])
                V(lambda e: e.tensor_copy(out=CwT[:, 0, :, :], in_=cw[:, 0, :, :]), ["cw"], ["CwT"])
                V(lambda e: e.tensor_scalar(out=CwT[:, 1, :, :], in0=cw[:, 1, :, :], scalar1=-1.0, scalar2=None, op0=ALU.mult), ["cw"], ["CwT"])
                for j in range(16):
                    V(lambda e, j=j: e.tensor_scalar(out=DEC[:, j, :], in0=ones_f, scalar1=sm[:, RHO, j:j + 1], scalar2=None, op0=ALU.mult), ["cstf", "sm_rho"], ["DEC"])
                V(lambda e: e.memset(DEC[:, :, 0:1], 0.0), ["DEC"], ["DEC"])

                S.emit()
                sa.__exit__(None, None, None)
                sa = sa_outer
                xt = sbt(sa, "xtA", [128, 4, D], F32)
                xnb = sbt(sa, "xnbA", [128, D], BF16)
                xnT = sbt(sa, "xnTA", [128, 8, 512], BF16)
                uT = sbt(sa, "uT", [128, 4, 512], BF16)
                sgs = sbt(sa, "sgs", [128, 8, 512], BF16)
                yg = sbt(sa, "yg", [128, 4, 512], BF16)
                y2 = sbt(sa, "y2", [128, 4, 512], BF16)
                mixT = sbt(sa, "mixTA", [128, 8, 512], BF16)
                scrA = dict(tag="nA", junk=sbt(sa, "junkA", [128, D], F32), ssq=sbt(sa, "ssqA", [128, 1], F32), rs=sbt(sa, "rsA", [128, 3], F32))
                P1f = sbt(sa, "P1f", [128, 1024], F32)
                P2f = sbt(sa, "P2f", [128, 1024], F32)
                Xf = sbt(sa, "Xf", [128, 2, 4, 128], F32)
                RI = sbt(sa, "RI", [128, 2, 16], F32)
                SbB = [sbt(sa, f"SbB{i}", [128, 2, 4, 128], BF16) for i in range(2)]
                P1 = [P1f[:, i * 256:(i + 1) * 256].rearrange("p (r t) -> p r t", r=2) for i in range(2)]
                P2 = [P2f[:, i * 256:(i + 1) * 256].rearrange("p (r t) -> p r t", r=2) for i in range(2)]
                X_ = [Xf[:, :, i, :] for i in range(2)]
                R_ = sbt(sa, "R_", [128, 2, 16, 128], F32)
                Rinit = sbt(sa, "Rinit", [128, 2, 16], F32)
                rt = sbt(sa, "rt", [128, 4, 16], F32)
                Sb = [sbt(sa, f"Sb{i}", [128, 2, 128], BF16) for i in range(2)]
                yv = sbt(sa, "yv", [128, 128], F32)
                sgt = sbt(sa, "sgt", [128, 512], BF16)
                V(lambda e: e.memset(Rinit[:], 0.0), [], ["Rinit"])
                V(lambda e: e.memset(RI[:], 0.0), [], ["RI"])

                for blk in range(nblkA):
                    conv_batch(10)
                    N = 16 if blk == 0 else 512
                    tok0 = 0 if blk == 0 else 16 + 512 * (blk - 1)
                    ntile = 1 if blk == 0 else 4
                    for i in range(ntile):
                        P = min(128, N)
                        r0 = tok0 + i * 128
                        DM(lambda e, i=i, P=P, r0=r0: e.dma_start(out=xt[:P, i, :], in_=xh[r0:r0 + P, :]), [], [f"xtA{i}"])
                        rms_rows(P, xt[:P, i, :], n1g_t[:P, :], [(xnb[:P, :], "xnbA")], f"xtA{i}", "nA", scrA)
                        transpose_rows(P, xnb[:P, :], "xnbA", xnT[:, :, i * 128:i * 128 + P], "xnTA")
                    for q in range(12):
                        bank = ps[1 + q % 2]; kb = f"ps{1 + q % 2}"
                        for k in range(8):
                            T(lambda e, q=q, k=k, bank=bank: e.matmul(bank[:, :N], lhsT=WinA[:, k, q * 128:(q + 1) * 128], rhs=xnT[:, k, :N], start=(k == 0), stop=(k == 7)),
                              ["WinA", "xnTA"], [kb])
                        if q < 4:
                            A(lambda e, q=q, bank=bank: e.activation(out=uT[:, q, :N], in_=bank[:, :N], func=AF.Copy), [kb], ["uT"])
                        else:
                            A(lambda e, q=q, bank=bank: e.activation(out=sgs[:, q - 4, :N], in_=bank[:, :N], func=AF.Sigmoid), [kb], ["sgs"])
                    nsc = 1 if blk == 0 else 4
                    Ts = 16 if blk == 0 else 128
                    allR = [f"R_{j}" for j in range(16)]
                    for sc in range(nsc):
                        t0 = sc * Ts
                        if Ts == 128:
                            for q in range(4):
                                buv = psS[:, :].rearrange("p (j r t) -> p j r t", j=4, r=2)
                                for jj in range(4):
                                    j = 4 * q + jj
                                    for ri in range(2):
                                        T(lambda e, j=j, jj=jj, ri=ri, q=q: e.matmul(psS[:, jj * 256 + ri * 128:jj * 256 + ri * 128 + 128], lhsT=BwT[:, ri, j, :], rhs=uT[:, q, t0:t0 + 128], start=True, stop=True),
                                          ["BwT", "uT"], ["ps3", "ps4"])
                                cosb = cosT[:, 4 * q:4 * q + 4, :].unsqueeze(2).to_broadcast([128, 4, 2, 128])
                                sinb = sinT[:, 4 * q:4 * q + 4, :].unsqueeze(2).to_broadcast([128, 4, 2, 128])
                                P1v = P1f[:, :].rearrange("p (j r t) -> p j r t", j=4, r=2)
                                P2v = P2f[:, :].rearrange("p (j r t) -> p j r t", j=4, r=2)
                                V(lambda e, buv=buv, cosb=cosb, P1v=P1v: e.tensor_tensor(out=P1v, in0=buv, in1=cosb, op=ALU.mult), ["ps3", "ps4", "cosT"], ["P1"])
                                V(lambda e, buv=buv, sinb=sinb, P2v=P2v: e.tensor_tensor(out=P2v, in0=buv, in1=sinb, op=ALU.mult), ["ps3", "ps4", "sinT"], ["P2"])
                                V(lambda e, P1v=P1v, P2v=P2v: e.tensor_tensor(out=Xf[:, 0, :, :], in0=P1v[:, :, 0, :], in1=P2v[:, :, 1, :], op=ALU.add), ["P1", "P2"], ["X"])
                                V(lambda e, P1v=P1v, P2v=P2v: e.tensor_tensor(out=Xf[:, 1, :, :], in0=P1v[:, :, 1, :], in1=P2v[:, :, 0, :], op=ALU.subtract), ["P1", "P2"], ["X"])
                                V(lambda e, q=q: e.tensor_tensor(out=Xf[:, :, :, 0], in0=Xf[:, :, :, 0], in1=RI[:, :, 4 * q:4 * q + 4], op=ALU.add), ["X", "RI"], ["X"])
                                for ri in range(2):
                                    V(lambda e, ri=ri, q=q: e.tensor_tensor_scan(out=R_[:, ri, 4 * q:4 * q + 4, :].rearrange("p j t -> p (j t)"), data0=DEC[:, 4 * q:4 * q + 4, :].rearrange("p j t -> p (j t)"),
                                                                                data1=Xf[:, ri, :, :].rearrange("p j t -> p (j t)"), initial=0.0, op0=ALU.mult, op1=ALU.add),
                                      ["DEC", "X"], [f"R_{4 * q + jj}" for jj in range(4)])
                                Rv = R_[:, :, 4 * q:4 * q + 4, :]
                                cosb2 = cosT[:, 4 * q:4 * q + 4, :].unsqueeze(1).to_broadcast([128, 2, 4, 128])
                                sinb2 = sinT[:, 4 * q:4 * q + 4, :].unsqueeze(1).to_broadcast([128, 2, 4, 128])
                                Q1v = P1f[:, :].rearrange("p (r j t) -> p r j t", r=2, j=4)
                                Q2v = P2f[:, :].rearrange("p (r j t) -> p r j t", r=2, j=4)
                                kR = [f"R_{4 * q + jj}" for jj in range(4)]
                                V(lambda e, Rv=Rv, cosb2=cosb2, Q1v=Q1v: e.tensor_tensor(out=Q1v, in0=Rv, in1=cosb2, op=ALU.mult), kR + ["cosT"], ["P1"])
                                V(lambda e, Rv=Rv, sinb2=sinb2, Q2v=Q2v: e.tensor_tensor(out=Q2v, in0=Rv, in1=sinb2, op=ALU.mult), kR + ["sinT"], ["P2"])
                                sbq = SbB[q % 2]; ksb = f"SbB{q % 2}"
                                V(lambda e, Q1v=Q1v, Q2v=Q2v, sbq=sbq: e.tensor_tensor(out=sbq[:, 0, :, :], in0=Q1v[:, 0, :, :], in1=Q2v[:, 1, :, :], op=ALU.subtract), ["P1", "P2"], [ksb])
                                V(lambda e, Q1v=Q1v, Q2v=Q2v, sbq=sbq: e.tensor_tensor(out=sbq[:, 1, :, :], in0=Q2v[:, 0, :, :], in1=Q1v[:, 1, :, :], op=ALU.add), ["P1", "P2"], [ksb])
                                for jj in range(4):
                                    j = 4 * q + jj
                                    for ri in range(2):
                                        T(lambda e, j=j, jj=jj, ri=ri, q=q, sbq=sbq: e.matmul(ps[5][:, q * 128:(q + 1) * 128], lhsT=CwT[:, ri, j, :], rhs=sbq[:, ri, jj, :], start=(jj == 0 and ri == 0), stop=(jj == 3 and ri == 1)),
                                          ["CwT", ksb], ["ps5"])
                                V(lambda e, q=q: e.scalar_tensor_tensor(out=yv[:, :], in0=uT[:, q, t0:t0 + 128], scalar=sv_[:, 0, q:q + 1], in1=ps[5][:, q * 128:(q + 1) * 128], op0=ALU.mult, op1=ALU.add),
                                  ["uT", "s5v", "ps5"], ["yv"])
                                A(lambda e, q=q: e.activation(out=yg[:, q, t0:t0 + 128], in_=yv[:, :], func=AF.Gelu), ["yv"], ["yg"])
                        else:
                            for j in range(16):
                                q = j // 4
                                pb = j % 2
                                bank = ps[3 + pb]; kb = f"ps{3 + pb}"
                                buv = bank[:, 0:256].rearrange("p (r t) -> p r t", r=2)
                                T(lambda e, j=j, q=q, bank=bank: e.matmul(bank[:, 0:Ts], lhsT=BwT[:, 0, j, :], rhs=uT[:, q, t0:t0 + Ts], start=True, stop=True), ["BwT", "uT"], [kb])
                                T(lambda e, j=j, q=q, bank=bank: e.matmul(bank[:, 128:128 + Ts], lhsT=BwT[:, 1, j, :], rhs=uT[:, q, t0:t0 + Ts], start=True, stop=True), ["BwT", "uT"], [kb])
                                cosb = cosT[:, j, :Ts].unsqueeze(1).to_broadcast([128, 2, Ts])
                                sinb = sinT[:, j, :Ts].unsqueeze(1).to_broadcast([128, 2, Ts])
                                V(lambda e, buv=buv, pb=pb, cosb=cosb: e.tensor_tensor(out=P1[pb][:, :, :Ts], in0=buv[:, :, :Ts], in1=cosb, op=ALU.mult), [kb, "cosT"], ["P1"])
                                V(lambda e, buv=buv, pb=pb, sinb=sinb: e.tensor_tensor(out=P2[pb][:, :, :Ts], in0=buv[:, :, :Ts], in1=sinb, op=ALU.mult), [kb, "sinT"], ["P2"])
                                V(lambda e, pb=pb: e.tensor_tensor(out=X_[pb][:, 0, :Ts], in0=P1[pb][:, 0, :Ts], in1=P2[pb][:, 1, :Ts], op=ALU.add), ["P1", "P2"], ["X"])
                                V(lambda e, pb=pb: e.tensor_tensor(out=X_[pb][:, 1, :Ts], in0=P1[pb][:, 1, :Ts], in1=P2[pb][:, 0, :Ts], op=ALU.subtract), ["P1", "P2"], ["X"])
                                for ri in range(2):
                                    V(lambda e, pb=pb, ri=ri, j=j: e.tensor_tensor_scan(out=R_[:, ri, j, :Ts], data0=sm[:, RHO, j:j + 1].to_broadcast([128, Ts]), data1=X_[pb][:, ri, :Ts],
                                                                                       initial=Rinit[:, ri, j:j + 1], op0=ALU.mult, op1=ALU.add),
                                      ["sm_rho", "X", "Rinit"], [f"R_{j}"])
                                V(lambda e, pb=pb, j=j, cosb=cosb: e.tensor_tensor(out=P1[pb][:, :, :Ts], in0=R_[:, :, j, :Ts], in1=cosb, op=ALU.mult), [f"R_{j}", "cosT"], ["P1"])
                                V(lambda e, pb=pb, j=j, sinb=sinb: e.tensor_tensor(out=P2[pb][:, :, :Ts], in0=R_[:, :, j, :Ts], in1=sinb, op=ALU.mult), [f"R_{j}", "sinT"], ["P2"])
                                V(lambda e, pb=pb: e.tensor_tensor(out=Sb[pb][:, 0, :Ts], in0=P1[pb][:, 0, :Ts], in1=P2[pb][:, 1, :Ts], op=ALU.subtract), ["P1", "P2"], [f"Sb{pb}"])
                                V(lambda e, pb=pb: e.tensor_tensor(out=Sb[pb][:, 1, :Ts], in0=P2[pb][:, 0, :Ts], in1=P1[pb][:, 1, :Ts], op=ALU.add), ["P1", "P2"], [f"Sb{pb}"])
                                T(lambda e, j=j, q=q, pb=pb: e.matmul(ps[5][:, q * 128:q * 128 + Ts], lhsT=CwT[:, 0, j, :], rhs=Sb[pb][:, 0, :Ts], start=(j % 4 == 0), stop=False), ["CwT", f"Sb{pb}"], ["ps5"])
                                T(lambda e, j=j, q=q, pb=pb: e.matmul(ps[5][:, q * 128:q * 128 + Ts], lhsT=CwT[:, 1, j, :], rhs=Sb[pb][:, 1, :Ts], start=False, stop=(j % 4 == 3)), ["CwT", f"Sb{pb}"], ["ps5"])
                                if j % 4 == 3:
                                    V(lambda e, q=q: e.scalar_tensor_tensor(out=yv[:, :Ts], in0=uT[:, q, t0:t0 + Ts], scalar=sv_[:, 0, q:q + 1], in1=ps[5][:, q * 128:q * 128 + Ts], op0=ALU.mult, op1=ALU.add),
                                      ["uT", "s5v", "ps5"], ["yv"])
                                    A(lambda e, q=q: e.activation(out=yg[:, q, t0:t0 + Ts], in_=yv[:, :Ts], func=AF.Gelu), ["yv"], ["yg"])
                        cR = sm[:, CR16 if Ts == 16 else CR128, :]
                        sR = sm[:, SR16 if Ts == 16 else SR128, :]
                        kc = ["sm_cr16", "sm_sr16"] if Ts == 16 else ["sm_cr128", "sm_sr128"]
                        lre = R_[:, 0, :, Ts - 1]
                        lim = R_[:, 1, :, Ts - 1]
                        V(lambda e, cR=cR, lre=lre: e.tensor_tensor(out=rt[:, 0, :], in0=cR, in1=lre, op=ALU.mult), allR + kc, ["rt0"])
                        V(lambda e, sR=sR, lim=lim: e.tensor_tensor(out=rt[:, 1, :], in0=sR, in1=lim, op=ALU.mult), allR + kc, ["rt1"])
                        V(lambda e, sR=sR, lre=lre: e.tensor_tensor(out=rt[:, 2, :], in0=sR, in1=lre, op=ALU.mult), allR + kc, ["rt2"])
                        V(lambda e, cR=cR, lim=lim: e.tensor_tensor(out=rt[:, 3, :], in0=cR, in1=lim, op=ALU.mult), allR + kc, ["rt3"])
                        V(lambda e: e.tensor_tensor(out=Rinit[:, 0, :], in0=rt[:, 0, :], in1=rt[:, 1, :], op=ALU.subtract), ["rt0", "rt1"], ["Rinit"])
                        V(lambda e: e.tensor_tensor(out=Rinit[:, 1, :], in0=rt[:, 2, :], in1=rt[:, 3, :], op=ALU.add), ["rt2", "rt3"], ["Rinit"])
                        V(lambda e: e.tensor_tensor(out=RI[:, :, :], in0=Rinit[:, :, :], in1=sm[:, RHO, :].unsqueeze(1).to_broadcast([128, 2, 16]), op=ALU.mult), ["Rinit", "sm_rho"], ["RI"])
                    for qo in range(4):
                        bank = ps[1 + qo % 2]; kb = f"ps{1 + qo % 2}"
                        for q in range(4):
                            T(lambda e, q=q, qo=qo, bank=bank: e.matmul(bank[:, :N], lhsT=Wglu[:, q, qo * 128:(qo + 1) * 128], rhs=yg[:, q, :N], start=(q == 0), stop=(q == 3)), ["Wglu", "yg"], [kb])
                        A(lambda e, qo=qo, bank=bank: e.activation(out=sgt[:, :N], in_=bank[:, :N], func=AF.Sigmoid, bias=sv_[:, 1, qo:qo + 1]), [kb, "s5v"], ["sgt"])
                        V(lambda e, qo=qo: e.tensor_tensor(out=y2[:, qo, :N], in0=yg[:, qo, :N], in1=sgt[:, :N], op=ALU.mult), ["yg", "sgt"], ["y2"])
                    for o in range(8):
                        bank = ps[1 + o % 2]; kb = f"ps{1 + o % 2}"
                        for q in range(4):
                            T(lambda e, q=q, o=o, bank=bank: e.matmul(bank[:, :N], lhsT=Wsso[:, q, o * 128:(o + 1) * 128], rhs=y2[:, q, :N], start=(q == 0), stop=(q == 3)), ["Wsso", "y2"], [kb])
                        V(lambda e, o=o, bank=bank: e.tensor_tensor(out=mixT[:, o, :N], in0=bank[:, :N], in1=sgs[:, o, :N], op=ALU.mult), [kb, "sgs"], ["mixTA"])
                    DM(lambda e, tok0=tok0, N=N: e.dma_start(out=mixs.rearrange("(o p) t -> p o t", p=128)[:, :, tok0:tok0 + N], in_=mixT[:, :, :N]), ["mixTA"], ["mixs"])
                S.emit(final=(passes[-1] == "A"))

        if "B" in passes:
            with ExitStack() as sbk:
                WinB = sbt(sbk, "WinB", [128, 8, 3072], BF16)
                Wmlo = sbt(sbk, "Wmlo", [128, 8, 1024], BF16)
                Wout = sbt(sbk, "Wout", [128, 8, 1024], BF16)
                n1g_t = sbt(sbk, "n1gB", [128, D], F32)
                mlng_t = sbt(sbk, "mlngB", [128, D], F32)
                mlv_t = sbt(sbk, "mlv", [128, 6, 8], F32)
                bif_t = sbt(sbk, "bif", [128, 8], F32)
                bd = sbt(sbk, "bd", [128, 3, 8, 128], BF16)
                G12 = sbt(sbk, "G12", [128, 2, 8, 8], BF16)
                sbk_outer = sbk
                sbk = ExitStack()
                sbk.__enter__()
                stg = sbt(sbk, "stgB", [128, 2, 1024], F32)
                load_weight(sbk, "WinB", w_in, 1024, 512, 2560, WinB, 0, stg, "b")
                load_weight(sbk, "WinB", w_in, 1024, 3584, 4608, WinB, 2048, stg, "b")
                load_weight(sbk, "Wmlo", w_mlo, 1024, 0, 1024, Wmlo, 0, stg, "b")
                load_weight(sbk, "Wout", w_out, 1024, 0, 1024, Wout, 0, stg, "b")
                DM(lambda e: e.dma_start(out=n1g_t[:], in_=n1g), [], ["n1gB"])
                DM(lambda e: e.dma_start(out=mlng_t[:], in_=mlng), [], ["mlngB"])
                bdf = sbt(sbk, "bdf", [128, 5, 8, 128], F32)
                bdvf = sbt(sbk, "bdvf", [128, 8, 128], F32)
                wiff = sbt(sbk, "wiff", [128, 24, 8], F32)
                DM(lambda e: e.dma_start(out=mlv_t[:], in_=mlv), [], ["mlv"])
                DM(lambda e: e.dma_start(out=bdf[:], in_=mlbd), [], ["bdf"])
                DM(lambda e: e.dma_start(out=bdvf[:], in_=mlbdv), [], ["bdvf"])
                DM(lambda e: e.dma_start(out=wiff[:], in_=mlwif), [], ["wiff"])
                DM(lambda e: e.dma_start(out=bif_t[:], in_=mlbif), [], ["bif"])
                V(lambda e: e.tensor_copy(out=bd[:], in_=bdf[:, 0:3, :, :]), ["bdf"], ["bd"])
                for c in range(8):
                    T(lambda e, c=c: e.matmul(ps[0][:, c * 8:(c + 1) * 8], lhsT=bdf[:, 3, c, :], rhs=wiff[:, c, :], start=True, stop=False), ["bdf", "wiff"], ["ps0"])
                    T(lambda e, c=c: e.matmul(ps[0][:, c * 8:(c + 1) * 8], lhsT=bdf[:, 4, c, :], rhs=wiff[:, 8 + c, :], start=False, stop=True), ["bdf", "wiff"], ["ps0"])
                    T(lambda e, c=c: e.matmul(ps[0][:, 64 + c * 8:64 + (c + 1) * 8], lhsT=bdvf[:, c, :], rhs=wiff[:, 16 + c, :], start=True, stop=True), ["bdvf", "wiff"], ["ps0"])
                V(lambda e: e.tensor_copy(out=G12[:], in_=ps[0][:, 0:128].rearrange("p (a c n) -> p a c n", a=2, c=8)), ["ps0"], ["G12"])
                tri = cstf[:, 1, :]

                S.emit()
                sbk.__exit__(None, None, None)
                sbk = sbk_outer
                NB = 256
                xt = sbt(sbk, "xtB", [128, 2, D], F32)
                xnb = sbt(sbk, "xnbB", [128, D], BF16)
                xnT = sbt(sbk, "xnTB", [128, 8, NB], BF16)
                xmT = sbt(sbk, "xmT", [128, 8, 3 + NB], BF16)
                xcT = sbt(sbk, "xcT", [128, 8, NB], BF16)
                acc = sbt(sbk, "accB", [128, NB], F32)
                sgm = sbt(sbk, "sgm", [128, 8, NB], BF16)
                qT = sbt(sbk, "qT", [128, 8, NB], BF16)
                kT = sbt(sbk, "kT", [128, 8, NB], BF16)
                mixin = sbt(sbk, "mixin", [128, 8, NB], BF16)
                hmT = sbt(sbk, "hmT", [128, 8, NB], BF16)
                mixT = sbt(sbk, "mixTB", [128, 8, NB], BF16)
                tmpf = sbt(sbk, "tmpfB", [128, NB], F32)
                h1t = sbt(sbk, "h1tB", [128, D], F32)
                scrB = dict(tag="nB", junk=sbt(sbk, "junkB", [128, D], F32), ssq=sbt(sbk, "ssqB", [128, 1], F32), rs=sbt(sbk, "rsB", [128, 3], F32))
                sigz2 = [sbt(sbk, f"sigz{i}", [64, D], BF16) for i in range(2)]
                vaug2 = [sbt(sbk, f"vaug{i}", [64, 4, 257], BF16) for i in range(2)]
                vw2 = [sbt(sbk, f"vw{i}", [64, 4, 257], BF16) for i in range(2)]
                kk2 = [sbt(sbk, f"kk{i}", [64, D], BF16) for i in range(2)]
                gt2 = [sbt(sbk, f"gt{i}", [64, 8], F32) for i in range(2)]
                gs2 = [sbt(sbk, f"gs{i}", [128, 8, 4], F32) for i in range(2)]
                E1, NLF, EG, EB, TMP, KSC, WST, ENB = range(8)
                SpT = sbt(sbk, "SpT", [64, 64], BF16)
                ho = sbt(sbk, "ho", [64, 256], F32)
                hjunk = sbt(sbk, "hjunk", [64, 256], F32)
                hs = sbt(sbk, "hs", [64, 8], F32)
                hn = sbt(sbk, "hn", [64, D], BF16)
                CT = sbt(sbk, "CT", [128, 8, 257], F32)
                CTb = sbt(sbk, "CTb", [128, 8, 257], BF16)
                V(lambda e: e.memset(CT[:], 0.0), [], ["CT"])
                V(lambda e: e.memset(CTb[:], 0.0), [], ["CTb"])
                V(lambda e: e.memset(xmT[:], 0.0), [], ["xmT"])
                for i_ in range(2):
                    V(lambda e, i_=i_: e.memset(vaug2[i_][:], 1.0), [], [f"vaug{i_}"])
                    V(lambda e, i_=i_: e.memset(gs2[i_][:], 0.0), [], [f"gs{i_}"])
                LN16 = math.log(16.0)

                for blk in range(nblkB):
                    conv_batch(10)
                    N = 16 if blk == 0 else NB
                    tok0 = 0 if blk == 0 else 16 + NB * (blk - 1)
                    ntile = 1 if blk == 0 else 2
                    for i in range(ntile):
                        P = min(128, N)
                        r0 = tok0 + i * 128
                        DM(lambda e, i=i, P=P, r0=r0: e.dma_start(out=xt[:P, i, :], in_=xh[r0:r0 + P, :]), [], [f"xtB{i}"])
                        rms_rows(P, xt[:P, i, :], n1g_t[:P, :], [(xnb[:P, :], "xnbB")], f"xtB{i}", "nB", scrB)
                        transpose_rows(P, xnb[:P, :], "xnbB", xnT[:, :, i * 128:i * 128 + P], "xnTB")
                    DM(lambda e, tok0=tok0, N=N: e.dma_start(out=mixin[:, :, :N], in_=mixs.rearrange("(o p) t -> p o t", p=128)[:, :, tok0:tok0 + N]), ["mixs"], ["mixin"])
                    for c in range(16):
                        bank = ps[1 + c % 2]; kb = f"ps{1 + c % 2}"
                        col0 = c * 128 if c < 8 else 2048 + (c - 8) * 128
                        for k in range(8):
                            T(lambda e, k=k, col0=col0, bank=bank: e.matmul(bank[:, :N], lhsT=WinB[:, k, col0:col0 + 128], rhs=xnT[:, k, :N], start=(k == 0), stop=(k == 7)), ["WinB", "xnTB"], [kb])
                        if c < 8:
                            A(lambda e, c=c, bank=bank: e.activation(out=xmT[:, c, 3:3 + N], in_=bank[:, :N], func=AF.Copy), [kb], ["xmT"])
                        else:
                            A(lambda e, c=c, bank=bank: e.activation(out=sgm[:, c - 8, :N], in_=bank[:, :N], func=AF.Sigmoid), [kb], ["sgm"])
                    for c in range(8):
                        V(lambda e, c=c: e.tensor_scalar(out=acc[:, :N], in0=xmT[:, c, 0:N], scalar1=mlv_t[:, 0, c:c + 1], scalar2=None, op0=ALU.mult), ["xmT", "mlv"], ["accB"])
                        for j in range(1, 4):
                            V(lambda e, c=c, j=j: e.scalar_tensor_tensor(out=acc[:, :N], in0=xmT[:, c, j:j + N], scalar=mlv_t[:, j, c:c + 1], in1=acc[:, :N], op0=ALU.mult, op1=ALU.add), ["xmT", "mlv", "accB"], ["accB"])
                        A(lambda e, c=c: e.activation(out=xcT[:, c, :N], in_=acc[:, :N], func=AF.Silu, bias=mlv_t[:, 4, c:c + 1]), ["accB", "mlv"], ["xcT"])
                    for c in range(16):
                        bank = ps[1 + c % 2]; kb = f"ps{1 + c % 2}"
                        w = c // 8; cc = c % 8
                        T(lambda e, w=w, cc=cc, bank=bank: e.matmul(bank[:, :N], lhsT=bd[:, w, cc, :], rhs=xcT[:, cc, :N], start=True, stop=True), ["bd", "xcT"], [kb])
                        dst = qT if w == 0 else kT
                        A(lambda e, cc=cc, bank=bank, dst=dst: e.activation(out=dst[:, cc, :N], in_=bank[:, :N], func=AF.Copy), [kb], ["qT" if w == 0 else "kT"])
                    Lc = 16 if blk == 0 else 64

                    def chunk_pre(ch):
                        pp = ch % 2
                        sigzP, vaugP, vw_P, kkP, gtP, gsP = sigz2[pp], vaug2[pp], vw2[pp], kk2[pp], gt2[pp], gs2[pp]
                        o0 = ch * Lc
                        for half in range(2):
                            bank = ps[1 + half]; kb = f"ps{1 + half}"
                            for k in range(8):
                                T(lambda e, k=k, half=half, bank=bank: e.matmul(bank[:Lc, :], lhsT=xnT[:, k, o0:o0 + Lc], rhs=WinB[:, k, 1024 + half * 512:1024 + (half + 1) * 512], start=(k == 0), stop=(k == 7)), ["xnTB", "WinB"], [kb])
                            A(lambda e, half=half, bank=bank: e.activation(out=sigzP[:Lc, half * 512:(half + 1) * 512], in_=bank[:Lc, :], func=AF.Sigmoid), [kb], [f"sigz{pp}"])
                        for c in range(8):
                            T(lambda e, c=c: e.matmul(ps[0][:Lc, 0:8], lhsT=xcT[:, c, o0:o0 + Lc], rhs=G12[:, 0, c, :], start=(c == 0), stop=False), ["xcT", "G12"], ["ps0"])
                        for c in range(8):
                            T(lambda e, c=c: e.matmul(ps[0][:Lc, 0:8], lhsT=xmT[:, c, 3 + o0:3 + o0 + Lc], rhs=G12[:, 1, c, :], start=False, stop=(c == 7)), ["xmT", "G12"], ["ps0"])
                        V(lambda e: e.tensor_tensor(out=gtP[:Lc, :], in0=ps[0][:Lc, 0:8], in1=bif_t[:Lc, :], op=ALU.add), ["ps0", "bif"], [f"gt{pp}"])
                        A(lambda e: e.activation(out=gsP[:Lc, E1, :], in_=gtP[:Lc, 4:8], func=AF.Exp, scale=-1.0), [f"gt{pp}"], [f"gs_e1{pp}"])
                        A(lambda e: e.activation(out=gsP[:Lc, NLF, :], in_=gsP[:Lc, E1, :], func=AF.Ln, bias=1.0), [f"gs_e1{pp}"], [f"gs_nlf{pp}"])
                        T(lambda e: e.matmul(ps[0][:Lc, 8:12], lhsT=tri[:Lc, :Lc], rhs=gsP[:Lc, NLF, :], start=True, stop=True), ["cstf", f"gs_nlf{pp}"], ["ps0"])
                        T(lambda e: e.matmul(ps[0][:, 12:16], lhsT=ones_f[:Lc, :], rhs=gsP[:Lc, NLF, :], start=True, stop=True), ["cstf", f"gs_nlf{pp}"], ["ps0"])
                        A(lambda e: e.activation(out=gsP[:, EG, :], in_=ps[0][:, 12:16], func=AF.Exp, scale=-1.0), ["ps0"], [f"gs_eg{pp}"])
                        A(lambda e: e.activation(out=gsP[:Lc, ENB, :], in_=ps[0][:Lc, 8:12], func=AF.Exp), ["ps0"], [f"gs_enb{pp}"])
                        V(lambda e: e.tensor_tensor(out=gsP[:Lc, TMP, :], in0=ps[0][:Lc, 8:12], in1=gtP[:Lc, 0:4], op=ALU.add), ["ps0", f"gt{pp}"], [f"gs_tmp{pp}"])
                        A(lambda e: e.activation(out=gsP[:Lc, KSC, :], in_=gsP[:Lc, TMP, :], func=AF.Exp, bias=-LN16), [f"gs_tmp{pp}"], [f"gs_ksc{pp}"])
                        V(lambda e: e.tensor_tensor(out=gsP[:Lc, WST, :], in0=gsP[:Lc, KSC, :], in1=gsP[:Lc, EG, :], op=ALU.mult), [f"gs_ksc{pp}", f"gs_eg{pp}"], [f"gs_wst{pp}"])
                        for half in range(2):
                            bank = ps[1 + half]; kb = f"ps{1 + half}"
                            for c4 in range(4):
                                c = half * 4 + c4
                                T(lambda e, c=c, c4=c4, bank=bank: e.matmul(bank[:Lc, c4 * 128:(c4 + 1) * 128], lhsT=xmT[:, c, 3 + o0:3 + o0 + Lc], rhs=bd[:, 2, c, :], start=True, stop=True), ["xmT", "bd"], [kb])
                            A(lambda e, half=half, bank=bank: e.activation(out=vaugP[:Lc, half * 2:half * 2 + 2, 0:256], in_=bank[:Lc, :].rearrange("p (h d) -> p h d", h=2), func=AF.Copy), [kb], [f"vaug{pp}"])
                        for h in range(4):
                            V(lambda e, h=h: e.tensor_scalar(out=vw_P[:Lc, h, :], in0=vaugP[:Lc, h, :], scalar1=gsP[:Lc, WST, h:h + 1], scalar2=None, op0=ALU.mult), [f"vaug{pp}", f"gs_wst{pp}"], [f"vw{pp}"])
                        for half in range(2):
                            bank = ps[1 + half]; kb = f"ps{1 + half}"
                            for c4 in range(4):
                                c = half * 4 + c4
                                T(lambda e, c=c, c4=c4, bank=bank: e.matmul(bank[:Lc, c4 * 128:(c4 + 1) * 128], lhsT=xcT[:, c, o0:o0 + Lc], rhs=bd[:, 1, c, :], start=True, stop=True), ["xcT", "bd"], [kb])
                            A(lambda e, half=half, bank=bank: e.activation(out=kkP[:Lc, half * 512:(half + 1) * 512], in_=bank[:Lc, :], func=AF.Copy), [kb], [f"kk{pp}"])

                    def chunk_heads(ch):
                        pp = ch % 2
                        o0 = ch * Lc
                        sigzP, vaugP, vw_P, kkP, gtP, gsP = sigz2[pp], vaug2[pp], vw2[pp], kk2[pp], gt2[pp], gs2[pp]
                        for h in range(4):
                            c0 = 2 * h
                            T(lambda e, c0=c0: e.matmul(ps[3][:Lc, :Lc], lhsT=kT[:, c0, o0:o0 + Lc], rhs=qT[:, c0, o0:o0 + Lc], start=True, stop=False), ["kT", "qT"], ["ps3"])
                            T(lambda e, c0=c0: e.matmul(ps[3][:Lc, :Lc], lhsT=kT[:, c0 + 1, o0:o0 + Lc], rhs=qT[:, c0 + 1, o0:o0 + Lc], start=False, stop=True), ["kT", "qT"], ["ps3"])
                            V(lambda e, h=h: e.scalar_tensor_tensor(out=SpT[:Lc, :Lc], in0=ps[3][:Lc, :Lc], scalar=gsP[:Lc, KSC, h:h + 1], in1=maskT[:Lc, :Lc], op0=ALU.mult, op1=ALU.mult), ["ps3", f"gs_ksc{pp}", "cstf"], ["SpT"])
                            T(lambda e, h=h: e.matmul(ps[4][:Lc, 0:257], lhsT=SpT[:Lc, :Lc], rhs=vaugP[:Lc, h, :], start=True, stop=False), ["SpT", f"vaug{pp}"], ["ps4"])
                            T(lambda e, h=h, c0=c0: e.matmul(ps[4][:Lc, 0:257], lhsT=qT[:, c0, o0:o0 + Lc], rhs=CTb[:, c0, :], start=False, stop=False), ["qT", "CTb"], ["ps4"])
                            T(lambda e, h=h, c0=c0: e.matmul(ps[4][:Lc, 0:257], lhsT=qT[:, c0 + 1, o0:o0 + Lc], rhs=CTb[:, c0 + 1, :], start=False, stop=True), ["qT", "CTb"], ["ps4"])
                            V(lambda e, h=h: e.tensor_tensor(out=hs[:Lc, 0:1], in0=ps[4][:Lc, 256:257], in1=gsP[:Lc, ENB, h:h + 1], op=ALU.max), ["ps4", f"gs_enb{pp}"], ["hs0"])
                            V(lambda e, h=h: e.scalar_tensor_tensor(out=hs[:Lc, 1:2], in0=ps[4][:Lc, 256:257], scalar=-1.0, in1=hs[:Lc, 0:1], op0=ALU.mult, op1=ALU.max), ["ps4", "hs0"], ["hs1"])
                            V(lambda e: e.reciprocal(out=hs[:Lc, 3:4], in_=hs[:Lc, 1:2]), ["hs1"], ["hs3"])
                            V(lambda e, h=h: e.scalar_tensor_tensor(out=ho[:Lc, :], in0=ps[4][:Lc, 0:256], scalar=hs[:Lc, 3:4], in1=sigzP[:Lc, h * 256:(h + 1) * 256], op0=ALU.mult, op1=ALU.mult), ["ps4", "hs3", f"sigz{pp}"], ["ho"])
                            A(lambda e: e.activation(out=hjunk[:Lc, :], in_=ho[:Lc, :], func=AF.Square, accum_out=hs[:Lc, 4:5]), ["ho"], ["hjunk", "hs4"])
                            A(lambda e: e.activation(out=hs[:Lc, 6:7], in_=hs[:Lc, 4:5], func=AF.Ln, scale=1.0 / 256.0, bias=EPS), ["hs4"], ["hs6"])
                            A(lambda e: e.activation(out=hs[:Lc, 7:8], in_=hs[:Lc, 6:7], func=AF.Exp, scale=-0.5), ["hs6"], ["hs7"])
                            V(lambda e, h=h: e.scalar_tensor_tensor(out=hn[:Lc, h * 256:(h + 1) * 256], in0=ho[:Lc, :], scalar=hs[:Lc, 7:8], in1=mlng_t[:Lc, h * 256:(h + 1) * 256], op0=ALU.mult, op1=ALU.mult), ["ho", "hs7", "mlngB"], ["hn"])
                            for dc in range(2):
                                bank = ps[5 + dc]; kb = f"ps{5 + dc}"
                                T(lambda e, h=h, dc=dc, bank=bank: e.matmul(bank[:, 0:257], lhsT=kkP[:Lc, (2 * h + dc) * 128:(2 * h + dc + 1) * 128], rhs=vw_P[:Lc, h, :], start=True, stop=True), [f"kk{pp}", f"vw{pp}"], [kb])
                                V(lambda e, h=h, dc=dc, bank=bank: e.scalar_tensor_tensor(out=CT[:, 2 * h + dc, :], in0=CT[:, 2 * h + dc, :], scalar=gsP[:, EG, h:h + 1], in1=bank[:, 0:257], op0=ALU.mult, op1=ALU.add), ["CT", f"gs_eg{pp}", kb], ["CT"])
                                A(lambda e, h=h, dc=dc: e.activation(out=CTb[:, 2 * h + dc, :], in_=CT[:, 2 * h + dc, :], func=AF.Copy), ["CT"], ["CTb"])
                        for c in range(8):
                            T(lambda e, c=c: e.transpose(out=psb[:, c * 64:c * 64 + Lc], in_=hn[:Lc, c * 128:(c + 1) * 128], identity=identb[:Lc, :Lc]), ["hn", "identb"], ["psb"])
                        for c in range(8):
                            V(lambda e, c=c: e.scalar_tensor_tensor(out=hmT[:, c, o0:o0 + Lc], in0=xcT[:, c, o0:o0 + Lc], scalar=mlv_t[:, 5, c:c + 1], in1=psb[:, c * 64:c * 64 + Lc], op0=ALU.mult, op1=ALU.add), ["xcT", "mlv", "psb"], ["hmT"])

                    nch = N // Lc
                    chunk_pre(0)
                    for ch in range(nch):
                        if ch + 1 < nch:
                            chunk_pre(ch + 1)
                        chunk_heads(ch)
                    V(lambda e, N=N: e.tensor_copy(out=xmT[:, :, 0:3], in_=xmT[:, :, N:N + 3]), ["xmT"], ["xmT"])
                    if blk == 0:
                        continue
                    for o in range(8):
                        bank = ps[1 + o % 2]; kb = f"ps{1 + o % 2}"
                        for c in range(8):
                            T(lambda e, c=c, o=o, bank=bank: e.matmul(bank[:, :N], lhsT=Wmlo[:, c, o * 128:(o + 1) * 128], rhs=hmT[:, c, :N], start=(c == 0), stop=(c == 7)), ["Wmlo", "hmT"], [kb])
                        V(lambda e, o=o, bank=bank: e.tensor_tensor(out=tmpf[:, :N], in0=bank[:, :N], in1=sgm[:, o, :N], op=ALU.mult), [kb, "sgm"], ["tmpfB"])
                        V(lambda e, o=o: e.tensor_tensor(out=mixT[:, o, :N], in0=tmpf[:, :N], in1=mixin[:, o, :N], op=ALU.add), ["tmpfB", "mixin"], ["mixTB"])
                    for i in range(ntile):
                        for half in range(2):
                            bank = ps[1 + half]; kb = f"ps{1 + half}"
                            for c in range(8):
                                T(lambda e, c=c, i=i, half=half, bank=bank: e.matmul(bank[:, :], lhsT=mixT[:, c, i * 128:(i + 1) * 128], rhs=Wout[:, c, half * 512:(half + 1) * 512], start=(c == 0), stop=(c == 7)), ["mixTB", "Wout"], [kb])
                            V(lambda e, i=i, half=half, bank=bank: e.tensor_tensor(out=h1t[:, half * 512:(half + 1) * 512], in0=bank[:, :], in1=xt[:, i, half * 512:(half + 1) * 512], op=ALU.add), [kb, f"xtB{i}"], ["h1tB"])
                        r0 = tok0 - 16 + i * 128
                        DM(lambda e, r0=r0: e.dma_start(out=h1[r0:r0 + 128, :], in_=h1t[:, :]), ["h1tB"], ["h1"])
                S.emit(final=(passes[-1] == "B"))

        if "C" in passes:
            conv_batch(100000)
            with ExitStack() as sc_:
                Wpq = sbt(sc_, "Wpq", [128, 8, 2048], BF16)
                n2g_t = sbt(sc_, "n2gC", [128, D], F32)
                fng_t = sbt(sc_, "fngC", [128, D], F32)
                kyb = sbt(sc_, "kyb", [128, 16, 128], BF16)
                sc_outer = sc_
                sc_ = ExitStack()
                sc_.__enter__()
                stg = sbt(sc_, "stgC", [128, 2, 2048], F32)
                load_weight(sc_, "Wpq", w_pq, 1024, 0, 2048, Wpq, 0, stg, "c")
                kyf = sbt(sc_, "kyf", [128, 16, 128], F32)
                DM(lambda e: e.dma_start(out=n2g_t[:], in_=n2g), [], ["n2gC"])
                DM(lambda e: e.dma_start(out=fng_t[:], in_=fng), [], ["fngC"])
                DM(lambda e: e.dma_start(out=kyf[:], in_=keysT), [], ["kyf"])
                V(lambda e: e.tensor_copy(out=kyb[:], in_=kyf[:]), ["kyf"], ["kyb"])
                S.emit()
                sc_.__exit__(None, None, None)
                sc_ = sc_outer
                h1t = [sbt(sc_, f"h1tC{i}", [128, D], F32) for i in range(2)]
                xn2d = [sbt(sc_, f"xn2_{i}", [128, D], BF16) for i in range(2)]
                xn2b = sbt(sc_, "xn2b", [128, D], BF16)
                xn2T = sbt(sc_, "xn2T", [128, 8, 128], BF16)
                qTb = sbt(sc_, "qTb", [128, 16, 128], BF16)
                scq = sbt(sc_, "scq", [128, 16, 128], F32)
                wk = sbt(sc_, "wkC", [128, 256], F32)
                svt = sbt(sc_, "svt", [128, 16, 16], F32)
                sit = sbt(sc_, "sit", [128, 16, 16], U32)
                sif = sbt(sc_, "sif", [128, 16, 16], F32)
                si0s = sbt(sc_, "si0s", [128, 8, 16], F32)
                cand = sbt(sc_, "cand", [128, 8, 256], F32)
                cs = sbt(sc_, "cs", [128, 8, 16], F32)
                cpos = sbt(sc_, "cpos", [128, 8, 16], U32)
                cab = sbt(sc_, "cab", [128, 2, 8, 16], U32)
                cabf = sbt(sc_, "cabf", [128, 2, 8, 16], F32)
                oh = sbt(sc_, "oh", [128, 8, 16, 16], F32)
                e01 = sbt(sc_, "e01", [128, 2, 8, 16], F32)
                eidf = sbt(sc_, "eidf", [128, 128], F32)
                eidi2 = [sbt(sc_, f"eidi{i}", [128, 128], I32) for i in range(2)]
                gex = sbt(sc_, "gex", [128, 8, 16], F32)
                gz = sbt(sc_, "gz", [128, 2, 8], F32)
                gate2 = [sbt(sc_, f"gate{i}", [128, 128], F32) for i in range(2)]
                dots = sbt(sc_, "dots", [128, 128], F32)
                actv = sbt(sc_, "actv", [128, 128], F32)
                wgt2 = [sbt(sc_, f"wgt{i}", [128, 128], F32) for i in range(2)]
                NG = 16
                uvg = [sbt(sc_, f"uvg{i}", [128, 2 * D], BF16) for i in range(NG)]
                prd = [sbt(sc_, f"prd{i}", [128, D], BF16) for i in range(4)]
                dgw = [sbt(sc_, f"dgw{i}", [128, 128], BF16) for i in range(2)]
                hout = sbt(sc_, "hout", [128, D], F32)
                outt = sbt(sc_, "outt", [128, D], F32)
                scrC = dict(tag="nC", junk=sbt(sc_, "junkC", [128, D], F32), ssq=sbt(sc_, "ssqC", [128, 1], F32), rs=sbt(sc_, "rsC", [128, 3], F32))
                iota16 = cstf[:, 2, 0:16]

                def top16(src_ap, ksrc, n, v_ap, i_ap, kv, ki):
                    V(lambda e: e.max(out=v_ap[:, 0:8], in_=src_ap), [ksrc], [kv])
                    V(lambda e: e.max_index(out=i_ap[:, 0:8], in_max=v_ap[:, 0:8], in_values=src_ap), [ksrc, kv], [ki])
                    V(lambda e: e.match_replace(out=wk[:, :n], in_to_replace=v_ap[:, 0:8], in_values=src_ap, imm_value=-1e30), [ksrc, kv], ["wkC"])
                    V(lambda e: e.max(out=v_ap[:, 8:16], in_=wk[:, :n]), ["wkC"], [kv])
                    V(lambda e: e.max_index(out=i_ap[:, 8:16], in_max=v_ap[:, 8:16], in_values=wk[:, :n]), ["wkC", kv], [ki])

                def phase_P1(ti):
                    hb = h1t[ti % 2]; khb = f"h1tC{ti % 2}"
                    xn2 = xn2d[ti % 2]; kxn2 = f"xn2_{ti % 2}"
                    DM(lambda e, ti=ti, hb=hb: e.dma_start(out=hb[:, :], in_=h1[ti * 128:(ti + 1) * 128, :]), ["h1"], [khb])
                    rms_rows(128, hb[:, :], n2g_t[:, :], [(xn2[:, :], kxn2)], khb, "nC", scrC)
                    transpose_rows(128, xn2[:, :], kxn2, xn2T[:, :, :], "xn2T")
                    for g4 in range(4):
                        bank = ps[1 + g4 % 2]; kb = f"ps{1 + g4 % 2}"
                        for hi4 in range(4):
                            hi = g4 * 4 + hi4
                            for k in range(8):
                                T(lambda e, hi=hi, hi4=hi4, k=k, bank=bank: e.matmul(bank[:, hi4 * 128:(hi4 + 1) * 128], lhsT=Wpq[:, k, hi * 128:(hi + 1) * 128], rhs=xn2T[:, k, :], start=(k == 0), stop=(k == 7)), ["Wpq", "xn2T"], [kb])
                        A(lambda e, g4=g4, bank=bank: e.activation(out=qTb[:, g4 * 4:(g4 + 1) * 4, :], in_=bank[:, :].rearrange("p (a t) -> p a t", a=4), func=AF.Copy), [kb], ["qTb"])
                    for g4 in range(4):
                        bank = ps[1 + g4 % 2]; kb = f"ps{1 + g4 % 2}"
                        for hi4 in range(4):
                            hi = g4 * 4 + hi4
                            T(lambda e, hi=hi, hi4=hi4, bank=bank: e.matmul(bank[:, hi4 * 128:(hi4 + 1) * 128], lhsT=qTb[:, hi, :], rhs=kyb[:, hi, :], start=True, stop=True), ["qTb", "kyb"], [kb])
                        A(lambda e, g4=g4, bank=bank: e.activation(out=scq[:, g4 * 4:(g4 + 1) * 4, :], in_=bank[:, :].rearrange("p (a t) -> p a t", a=4), func=AF.Copy), [kb], ["scq"])

                def phase_P2_gen(ti):
                    eidi = eidi2[ti % 2]; keid = f"eidi{ti % 2}"
                    gate = gate2[ti % 2]; kgate = f"gate{ti % 2}"
                    for hi in range(16):
                        top16(scq[:, hi, :], "scq", 128, svt[:, hi, :], sit[:, hi, :], "svt", "sit")
                        yield
                    V(lambda e: e.tensor_copy(out=sif[:], in_=sit[:]), ["sit"], ["sif"])
                    sv4 = svt[:, :, :].rearrange("p (h i) k -> p h i k", i=2)
                    sf4 = sif[:, :, :].rearrange("p (h i) k -> p h i k", i=2)
                    V(lambda e, sf4=sf4: e.tensor_scalar(out=si0s[:], in0=sf4[:, :, 0, :], scalar1=128.0, scalar2=None, op0=ALU.mult), ["sif"], ["si0s"])
                    V(lambda e, sv4=sv4: e.tensor_tensor(out=cand[:, :, :].rearrange("p h (a b) -> p h a b", a=16),
                                                         in0=sv4[:, :, 0, :].unsqueeze(3).to_broadcast([128, 8, 16, 16]),
                                                         in1=sv4[:, :, 1, :].unsqueeze(2).to_broadcast([128, 8, 16, 16]), op=ALU.add), ["svt"], ["cand"])
                    yield
                    for h in range(8):
                        top16(cand[:, h, :], "cand", 256, cs[:, h, :], cpos[:, h, :], "cs", "cpos")
                        yield
                    V(lambda e: e.tensor_single_scalar(out=cab[:, 0, :, :], in_=cpos[:], scalar=4, op=ALU.logical_shift_right), ["cpos"], ["cab"])
                    V(lambda e: e.tensor_single_scalar(out=cab[:, 1, :, :], in_=cpos[:], scalar=15, op=ALU.bitwise_and), ["cpos"], ["cab"])
                    V(lambda e: e.tensor_copy(out=cabf[:], in_=cab[:]), ["cab"], ["cabf"])
                    yield
                    for ab in range(2):
                        srcv = si0s[:, :, :] if ab == 0 else sf4[:, :, 1, :]
                        V(lambda e, ab=ab: e.tensor_tensor(out=oh[:], in0=cabf[:, ab, :, :].unsqueeze(3).to_broadcast([128, 8, 16, 16]),
                                                           in1=iota16.unsqueeze(1).unsqueeze(1).to_broadcast([128, 8, 16, 16]), op=ALU.is_equal), ["cabf", "cstf"], ["oh"])
                        V(lambda e, srcv=srcv: e.tensor_tensor(out=oh[:], in0=oh[:], in1=srcv.unsqueeze(2).to_broadcast([128, 8, 16, 16]), op=ALU.mult), ["oh", "si0s", "sif"], ["oh"])
                        V(lambda e, ab=ab: e.tensor_reduce(out=e01[:, ab, :, :], in_=oh[:], axis=AX.X, op=ALU.add), ["oh"], ["e01"])
                        yield
                    V(lambda e: e.tensor_tensor(out=eidf[:, :].rearrange("p (h k) -> p h k", h=8), in0=e01[:, 0, :, :], in1=e01[:, 1, :, :], op=ALU.add), ["e01"], ["eidf"])
                    V(lambda e, eidi=eidi: e.tensor_copy(out=eidi[:], in_=eidf[:]), ["eidf"], [keid])
                    V(lambda e: e.tensor_tensor(out=gex[:], in0=cs[:], in1=cs[:, :, 0:1].to_broadcast([128, 8, 16]), op=ALU.subtract), ["cs"], ["gex"])
                    A(lambda e: e.activation(out=gex[:], in_=gex[:], func=AF.Exp), ["gex"], ["gex"])
                    V(lambda e: e.tensor_reduce(out=gz[:, 0, :], in_=gex[:], axis=AX.X, op=ALU.add), ["gex"], ["gz0"])
                    V(lambda e: e.reciprocal(out=gz[:, 1, :], in_=gz[:, 0, :]), ["gz0"], ["gz1"])
                    V(lambda e, gate=gate: e.tensor_tensor(out=gate[:, :].rearrange("p (h k) -> p h k", h=8), in0=gex[:], in1=gz[:, 1, :].unsqueeze(2).to_broadcast([128, 8, 16]), op=ALU.mult), ["gex", "gz1"], [kgate])

                djunk2 = [sbt(sc_, f"djunkb{i}", [128, D], BF16) for i in range(2)]
                def phase_P2(ti):
                    for _ in phase_P2_gen(ti):
                        pass

                first_gather = [True]
                gctr = [0]

                def epilogue(ti):
                    acc0 = 5 if ti % 2 == 0 else 3
                    hb = h1t[ti % 2]; khb = f"h1tC{ti % 2}"
                    for half in range(2):
                        V(lambda e, half=half, hb=hb: e.tensor_tensor(out=hout[:, half * 512:(half + 1) * 512], in0=ps[acc0 + half][:, :], in1=hb[:, half * 512:(half + 1) * 512], op=ALU.add), [f"ps{acc0 + half}", khb], ["hout"])
                    rms_rows(128, hout[:, :], fng_t[:, :], [(outt[:, :], "outt")], "hout", "nF", scrC)
                    DM(lambda e, ti=ti: e.dma_start(out=out[ti * 128:(ti + 1) * 128, :], in_=outt[:, :]), ["outt"], ["out"])

                def tile_body(ti):
                    acc0 = 5 if ti % 2 == 0 else 3
                    xn2 = xn2d[ti % 2]; kxn2 = f"xn2_{ti % 2}"
                    eidi = eidi2[ti % 2]; keid = f"eidi{ti % 2}"
                    gate = gate2[ti % 2]; kgate = f"gate{ti % 2}"
                    wgt = wgt2[ti % 2]; kw = f"wgt{ti % 2}"
                    hb = h1t[ti % 2]; khb = f"h1tC{ti % 2}"
                    GS = 2
                    pending = []

                    def vside(g, bufs):
                        for k_, sl in enumerate(range(g * GS, (g + 1) * GS)):
                            b = bufs[k_]
                            dg = dgw[sl % 2]; kdg = f"dgw{sl % 2}"
                            V(lambda e, sl=sl, dg=dg, gate=gate: e.tensor_scalar(out=dg[:, :], in0=identb[:, :], scalar1=actv[:, sl:sl + 1], scalar2=gate[:, sl:sl + 1], op0=ALU.mult, op1=ALU.mult), ["identb", f"actv{g % 4}", kgate], [kdg])
                            for half in range(2):
                                T(lambda e, sl=sl, half=half, dg=dg, b=b: e.matmul(ps[acc0 + half][:, :], lhsT=dg[:, :], rhs=uvg[b][:, D + half * 512:D + (half + 1) * 512], start=(sl == 0), stop=(sl == 127)), [kdg, f"uvg{b}"], [f"ps{acc0 + half}"])

                    for g in range(128 // GS):
                        bufs = []
                        for sl in range(g * GS, (g + 1) * GS):
                            b = gctr[0] % NG
                            gctr[0] += 1
                            bufs.append(b)
                            extra = conv_keys if first_gather[0] else []
                            first_gather[0] = False
                            S.op("pool", lambda e, sl=sl, b=b, eidi=eidi: e.indirect_dma_start(out=uvg[b][:, :], out_offset=None, in_=uvb, in_offset=bass.IndirectOffsetOnAxis(ap=eidi[:, sl:sl + 1], axis=0)),
                                 reads=[keid] + extra, writes=[f"uvg{b}"], dma=True, ring=GRING)
                            pr = prd[sl % 4]; kpr = f"prd{sl % 4}"
                            V(lambda e, sl=sl, b=b, pr=pr: e.tensor_tensor(out=pr[:, :], in0=uvg[b][:, 0:D], in1=xn2[:, :], op=ALU.mult), [f"uvg{b}", kxn2], [kpr])
                            dj = djunk2[sl % 2]
                            A(lambda e, sl=sl, pr=pr, dj=dj: e.activation(out=dj[:, :], in_=pr[:, :], func=AF.Copy, accum_out=dots[:, sl:sl + 1]), [kpr], [f"djunk{sl % 2}", f"dots{g % 4}_{sl % 2}"])
                        gsl = slice(g * GS, (g + 1) * GS)
                        A(lambda e, gsl=gsl: e.activation(out=actv[:, gsl], in_=dots[:, gsl], func=AF.Gelu), [f"dots{g % 4}_0", f"dots{g % 4}_1"], [f"actv{g % 4}"])
                        if pending:
                            vside(*pending.pop(0))
                        pending.append((g, bufs))
                        if g == 1 and ti > 0:
                            epilogue(ti - 1)
                        if g == 1 and ti + 1 < ntileC:
                            phase_P1(ti + 1)
                            p2gen = phase_P2_gen(ti + 1)
                        if g >= 34 and ti + 1 < ntileC:
                            next(p2gen, None)
                    while pending:
                        vside(*pending.pop(0))
                    if ti + 1 < ntileC:
                        for _ in p2gen:
                            pass

                phase_P1(0)
                phase_P2(0)
                for ti in range(ntileC):
                    tile_body(ti)
                epilogue(ntileC - 1)
                S.emit(final=True)
    return nc


def host_inputs(inp):
    f = lambda a: np.ascontiguousarray(np.asarray(a, dtype=np.float32))
    x = f(inp["x"]); meta = f(inp["meta_tokens"])
    rep = lambda v: np.ascontiguousarray(np.broadcast_to(f(v).reshape(1, -1), (128, f(v).size)))
    pm = lambda v, n: np.ascontiguousarray(f(v).reshape(n, 128).T)
    common = {}
    common["w_in"] = f(inp["w_in"][0]); common["w_glu"] = f(inp["ssm_w_glu"][0]); common["w_sso"] = f(inp["w_ssm_out"][0])
    common["w_mlo"] = f(inp["w_ml_out"][0]); common["w_out"] = f(inp["w_out"][0]); common["w_pq"] = f(inp["peer_w_q"][0])
    common["n1g"] = rep(inp["norm1_g"][0]); common["mlng"] = rep(inp["ml_norm_g"][0]); common["n2g"] = rep(inp["norm2_g"][0]); common["fng"] = rep(inp["final_norm_g"])
    def gp(a):
        return np.ascontiguousarray(f(a).reshape(16, 2, 64).transpose(1, 2, 0).reshape(128, 16))
    ldt = np.broadcast_to(f(inp["ssm_log_dt"][0])[:, None], (32, 64))
    common["s5a"] = np.ascontiguousarray(np.stack([gp(inp["ssm_a_re"][0]), gp(inp["ssm_a_im"][0]), gp(ldt)], axis=1))
    bwide = np.zeros((128, 2, 16, 128), np.float32)
    cwide = np.zeros((128, 2, 16, 128), np.float32)
    for ri, (bk, ck) in enumerate([("ssm_b_re", "ssm_c_re"), ("ssm_b_im", "ssm_c_im")]):
        B = f(inp[bk][0])
        C = f(inp[ck][0])
        for j in range(16):
            for g2 in range(2):
                g = 2 * j + g2
                gl = g % 8
                bwide[g2 * 64:(g2 + 1) * 64, ri, j, gl * 16:(gl + 1) * 16] = B[g]
                cwide[g2 * 64:(g2 + 1) * 64, ri, j, gl * 16:(gl + 1) * 16] = C[g].T
    common["s5bw"] = bwide; common["s5cw"] = cwide
    common["s5v"] = np.ascontiguousarray(np.stack([pm(inp["ssm_d"][0], 4), pm(inp["ssm_b_glu"][0], 4)], axis=1))
    cw_ = f(inp["ml_conv_w"][0])
    common["mlv"] = np.ascontiguousarray(np.stack([pm(cw_[0], 8), pm(cw_[1], 8), pm(cw_[2], 8), pm(cw_[3], 8), pm(inp["ml_conv_b"][0], 8), pm(inp["ml_skip"][0], 8)], axis=1))
    def blockdiag_chunks(w, transpose=False):
        w = f(w)
        o = np.zeros((8, 128, 128), np.float32)
        for n in range(256):
            c = n // 32; r = (n % 32) * 4
            o[c, r:r + 4, r:r + 4] = w[n].T if transpose else w[n]
        return o.transpose(1, 0, 2)
    common["mlbd"] = np.ascontiguousarray(np.stack([blockdiag_chunks(inp["ml_wq"][0]), blockdiag_chunks(inp["ml_wk"][0]), blockdiag_chunks(inp["ml_wv"][0]),
                                                    blockdiag_chunks(inp["ml_wq"][0], True), blockdiag_chunks(inp["ml_wk"][0], True)], axis=1))
    common["mlbdv"] = np.ascontiguousarray(blockdiag_chunks(inp["ml_wv"][0], True))
    common["mlwif"] = np.ascontiguousarray(f(inp["ml_w_if"][0]).reshape(24, 128, 8).transpose(1, 0, 2))
    common["mlbif"] = rep(inp["ml_b_if"][0])
    common["keysT"] = np.ascontiguousarray(f(inp["peer_sub_keys"][0]).reshape(16, 128, 128).transpose(2, 0, 1))
    common["u_tab"] = f(inp["peer_u"][0]); common["v_tab"] = f(inp["peer_v"][0])
    cst = np.zeros((128, 4, 128), np.float32)
    cst[:, 0, :] = np.eye(128, dtype=np.float32)
    m = np.triu(np.ones((64, 64), np.float32))
    cst[0:64, 1, 0:64] = m; cst[64:128, 1, 64:128] = m
    cst[:, 2, :] = np.arange(128, dtype=np.float32)[None, :]
    cst[:, 3, :] = 1.0
    common["cst"] = cst
    maps = []
    for b in range(8):
        d = dict(common)
        d["xh"] = np.ascontiguousarray(np.concatenate([meta, x[b]], axis=0))
        maps.append(d)
    return maps


def kernel(**inputs):
    nc = build()
    maps = host_inputs(inputs)
    res = run_bass_kernel_spmd(nc, maps, core_ids=list(range(8)))
    return np.stack([np.asarray(r["out"], dtype=np.float32) for r in res.results], axis=0)
```

```python
import math
import types
import numpy as np
from contextlib import ExitStack
import concourse.bass as bass
import concourse.mybir as mybir
from concourse.bass_utils import run_bass_kernel_spmd

F32 = mybir.dt.float32
BF16 = mybir.dt.bfloat16
I32 = mybir.dt.int32
U32 = mybir.dt.uint32
AF = mybir.ActivationFunctionType
ALU = mybir.AluOpType
AX = mybir.AxisListType

ENGINES = ("pe", "dve", "act", "pool", "sp")
EPOCH = 20000
N_DMA_SEMS = 24
RING_SIZES = {"cv": 6, "gq": 12}
L = 4112
NMETA = 16
D = 1024
EPS = 1e-6


class Sched:
    def __init__(self, nc, stack):
        self.nc = nc
        self.stack = stack
        self.ops = []
        self.buf = {}
        self.sem_cache = {}
        self.cnt = {e: 0 for e in ENGINES}
        self.dma_cnt = [0] * N_DMA_SEMS
        self.dma_rr = 0
        self.ring_cnt = {}
        self.ring_rr = {}
        self.emitted = 0
        self.fence_tickets = []

    def _sem(self, name):
        if name not in self.sem_cache:
            self.sem_cache[name] = self.stack.enter_context(self.nc.semaphore(name))
        return self.sem_cache[name]

    @staticmethod
    def _snap(fn):
        if fn.__closure__ is None:
            return fn
        cells = []
        for c in fn.__closure__:
            try:
                cells.append(types.CellType(c.cell_contents))
            except ValueError:
                cells.append(c)
        return types.FunctionType(fn.__code__, fn.__globals__, fn.__name__, fn.__defaults__, tuple(cells))

    def op(self, eng, fn, reads=(), writes=(), dma=False, detached=False, ring="dma"):
        fn = self._snap(fn)
        deps = {}
        for k in reads:
            st = self.buf.get(k)
            if st is not None and st["w"] is not None:
                deps[st["w"]] = True
        for k in writes:
            st = self.buf.get(k)
            if st is not None:
                if st["w"] is not None:
                    deps.setdefault(st["w"], False)
                for r_ in st["r"]:
                    deps.setdefault(r_, False)
        oid = len(self.ops)
        self.ops.append(dict(eng=eng, fn=fn, deps=deps, dma=dma, signal=dma, ticket=None, detached=detached, ring=ring))
        for k in reads:
            st = self.buf.setdefault(k, {"w": None, "r": []})
            st["r"].append(oid)
        for k in writes:
            self.buf[k] = {"w": oid, "r": []}
        return oid

    def emit(self, final=False):
        ops = self.ops
        start = self.emitted
        new = ops[start:]
        for o in new:
            live = set()
            for d, raw in o["deps"].items():
                if d < start and not (ops[d]["detached"] and ops[d]["dma"]):
                    continue
                od = ops[d]
                if od["eng"] == o["eng"] and not od["dma"] and not o["dma"] and (o["eng"] == "pe" or (RELAX and not raw)):
                    continue
                od["signal"] = True
                live.add(d)
            o["deps"] = live
        last = {}
        for o in new:
            if not o["dma"] and not o["detached"]:
                last[o["eng"]] = o
        for o in last.values():
            o["signal"] = True
        for o in new:
            if o["dma"] and o["ring"] != "dma":
                rn = o["ring"]
                cntl = self.ring_cnt.setdefault(rn, [0] * RING_SIZES[rn])
                j = self.ring_rr.get(rn, 0) % RING_SIZES[rn]
                self.ring_rr[rn] = self.ring_rr.get(rn, 0) + 1
                o["prev_dma"] = (f"{rn}{j}", cntl[j])
                cntl[j] += 16
                o["ticket"] = (f"{rn}{j}", cntl[j])
            elif o["dma"]:
                j = self.dma_rr % N_DMA_SEMS
                self.dma_rr += 1
                o["prev_dma"] = (f"dma{j}", self.dma_cnt[j])
                self.dma_cnt[j] += 16
                o["ticket"] = (f"dma{j}", self.dma_cnt[j])
            elif o["signal"]:
                e = o["eng"]
                self.cnt[e] += 1
                ep = (self.cnt[e] - 1) // EPOCH
                o["ticket"] = (f"c_{e}_{ep}", (self.cnt[e] - 1) % EPOCH + 1)
        for o in new:
            if o["ticket"] is not None:
                self._sem(o["ticket"][0])
        per_eng = {e: [] for e in ENGINES}
        for o in new:
            per_eng[o["eng"]].append(o)
        fence_in = list(self.fence_tickets)
        fence_out = [o["ticket"] for o in last.values()] + [(f"dma{j}", c) for j, c in enumerate(self.dma_cnt) if c > 0]
        fence_out += [(f"gq{j}", c) for j, c in enumerate(self.ring_cnt.get("gq", [])) if c > 0]

        def replay(engname):
            def body(eng):
                waited = {}
                for s, v in fence_in:
                    eng.wait_ge(self._sem(s), v)
                    waited[s] = max(waited.get(s, 0), v)
                for o in per_eng[engname]:
                    need = {}
                    for d in o["deps"]:
                        s, v = ops[d]["ticket"]
                        if need.get(s, 0) < v:
                            need[s] = v
                    if o["dma"]:
                        s, v = o["prev_dma"]
                        if v > 0 and need.get(s, 0) < v:
                            need[s] = v
                    for s, v in need.items():
                        if waited.get(s, 0) >= v:
                            continue
                        eng.wait_ge(self._sem(s), v)
                        waited[s] = v
                    ins = o["fn"](eng)
                    if o["signal"]:
                        s, v = o["ticket"]
                        ins.then_inc(self._sem(s), 16 if o["dma"] else 1)
                if final:
                    for s, v in fence_out:
                        if waited.get(s, 0) < v:
                            eng.wait_ge(self._sem(s), v)
            return body

        with self.nc.Block() as block:
            block.tensor(replay("pe"))
            block.vector(replay("dve"))
            block.scalar(replay("act"))
            block.gpsimd(replay("pool"))
            block.sync(replay("sp"))
        self.fence_tickets = fence_out
        self.emitted = len(ops)


GRING = "dma"
RELAX = False
TWO_PI = 2.0 * math.pi
CW1 = 6.28125
CW2 = TWO_PI - CW1


def build(passes=("A", "B", "C"), debug=False, nblkA=9, nblkB=17, ntileC=32):
    nc = bass.Bass("TRN2", target_bir_lowering=False)

    def din(name, shape, dt=F32):
        return nc.dram_tensor(name, list(shape), dt, kind="ExternalInput").ap()

    xh = din("xh", [L, D])
    w_in = din("w_in", [D, 4608])
    w_glu = din("w_glu", [512, 512])
    w_sso = din("w_sso", [512, D])
    w_mlo = din("w_mlo", [D, D])
    w_out = din("w_out", [D, D])
    w_pq = din("w_pq", [D, 2048])
    n1g = din("n1g", [128, D])
    mlng = din("mlng", [128, D])
    n2g = din("n2g", [128, D])
    fng = din("fng", [128, D])
    s5a = din("s5a", [128, 3, 16])
    s5bw = din("s5bw", [128, 2, 16, 128])
    s5cw = din("s5cw", [128, 2, 16, 128])
    s5v = din("s5v", [128, 2, 4])
    mlv = din("mlv", [128, 6, 8])
    mlbd = din("mlbd", [128, 5, 8, 128])
    mlbdv = din("mlbdv", [128, 8, 128])
    mlwif = din("mlwif", [128, 24, 8])
    mlbif = din("mlbif", [128, 8])
    keysT = din("keysT", [128, 16, 128])
    u_tab = din("u_tab", [16384, D])
    v_tab = din("v_tab", [16384, D])
    cst = din("cst", [128, 4, 128])
    kind_s = "ExternalOutput" if debug else "Internal"
    mixs = nc.dram_tensor("mixs", [D, L], BF16, kind=kind_s).ap()
    h1 = nc.dram_tensor("h1", [4096, D], F32, kind=kind_s).ap()
    out = nc.dram_tensor("out", [4096, D], F32, kind="ExternalOutput").ap()
    uvb = nc.dram_tensor("uvb", [16384, 2 * D], BF16, kind="Internal").ap()

    with ExitStack() as top:
        S = Sched(nc, top)
        V = lambda fn, r, w: S.op("dve", fn, reads=r, writes=w)
        A = lambda fn, r, w: S.op("act", fn, reads=r, writes=w)
        G = lambda fn, r, w: S.op("pool", fn, reads=r, writes=w)
        T = lambda fn, r, w: S.op("pe", fn, reads=r, writes=w)
        DM = lambda fn, r, w: S.op("sp", fn, reads=r, writes=w, dma=True)

        ps = [top.enter_context(nc.psum_tensor(f"ps{i}", [128, 512], F32)) for i in range(3)]
        psS = top.enter_context(nc.psum_tensor("psS", [128, 1024], F32))
        ps += [psS[:, 0:512], psS[:, 512:1024]]
        ps += [top.enter_context(nc.psum_tensor(f"ps{i}", [128, 512], F32)) for i in range(5, 7)]
        psb = top.enter_context(nc.psum_tensor("psb", [128, 1024], BF16))

        def sbt(stack, name, shape, dt):
            return stack.enter_context(nc.sbuf_tensor("sb_" + name, list(shape), dt))

        cstf = sbt(top, "cstf", [128, 4, 128], F32)
        identb = sbt(top, "identb", [128, 128], BF16)
        DM(lambda e: e.dma_start(out=cstf[:], in_=cst), [], ["cstf"])
        V(lambda e: e.tensor_copy(out=identb[:], in_=cstf[:, 0, :]), ["cstf"], ["identb"])
        identf = cstf[:, 0, :]
        maskT = cstf[:, 1, :]
        iota_f = cstf[:, 2, :]
        ones_f = cstf[:, 3, :]

        def load_weight(stack, name, src, rows, c0, c1, dst, dcol0, stg, tagn):
            nk = rows // 128
            w = c1 - c0
            i = 0
            for k in range(nk):
                SW = stg.shape[2]
                for s0 in range(0, w, SW):
                    s1 = min(w, s0 + SW)
                    b = load_weight.ctr % 2
                    load_weight.ctr += 1
                    DM(lambda e, b=b, k=k, s0=s0, s1=s1: e.dma_start(out=stg[:, b, 0:s1 - s0], in_=src[k * 128:(k + 1) * 128, c0 + s0:c0 + s1]),
                       [], [f"stg{b}"])
                    fn = lambda e, b=b, k=k, s0=s0, s1=s1: e.tensor_copy(out=dst[:, k, dcol0 + s0:dcol0 + s1], in_=stg[:, b, 0:s1 - s0])
                    if i % 2 == 0:
                        A(lambda e, b=b, k=k, s0=s0, s1=s1: e.activation(out=dst[:, k, dcol0 + s0:dcol0 + s1], in_=stg[:, b, 0:s1 - s0], func=AF.Copy), [f"stg{b}"], [name])
                    else:
                        V(fn, [f"stg{b}"], [name])
                    i += 1
        load_weight.ctr = 0

        def rms_rows(P, x_ap, g_ap, outs, kx, tag, scr):
            junk, ssq, rs = scr["junk"], scr["ssq"], scr["rs"]
            tag = scr["tag"]
            A(lambda e: e.activation(out=junk[:P, :], in_=x_ap, func=AF.Square, accum_out=ssq[:P, 0:1]), [kx], [tag + "junk", tag + "ssq"])
            A(lambda e: e.activation(out=rs[:P, 1:2], in_=ssq[:P, 0:1], func=AF.Ln, scale=1.0 / D, bias=EPS), [tag + "ssq"], [tag + "rs1"])
            A(lambda e: e.activation(out=rs[:P, 2:3], in_=rs[:P, 1:2], func=AF.Exp, scale=-0.5), [tag + "rs1"], [tag + "rs2"])
            for (o_ap, ko) in outs:
                V(lambda e, o_ap=o_ap: e.scalar_tensor_tensor(out=o_ap, in0=x_ap, scalar=rs[:P, 2:3], in1=g_ap, op0=ALU.mult, op1=ALU.mult),
                  [kx, tag + "rs2"], [ko])

        def transpose_rows(P, xn_bf_ap, kxn, dst_ap, kdst):
            for c in range(8):
                T(lambda e, c=c: e.transpose(out=psb[:, c * 128:c * 128 + P], in_=xn_bf_ap[:, c * 128:(c + 1) * 128], identity=identb[:P, :P]),
                  [kxn, "identb"], ["psb"])
            A(lambda e: e.activation(out=dst_ap, in_=psb[:, :].rearrange("p (c t) -> p c t", c=8)[:, :, :P], func=AF.Copy), ["psb"], [kdst])

        conv_keys = []
        if "C" in passes:
            CW = 1024
            cvf = [sbt(top, f"cvf{i}", [128, CW], F32) for i in range(2)]
            cvb = [sbt(top, f"cvb{i}", [128, CW], BF16) for i in range(2)]
            chunks = [(tsrc, half, c) for (tsrc, half) in ((u_tab, 0), (v_tab, 1)) for c in range(131072 // CW)]
            def cv_in(i):
                tsrc, half, c = chunks[i]
                S.op("pool", lambda e, i=i, tsrc=tsrc, c=c: e.dma_start(out=cvf[i % 2][:, :], in_=tsrc.rearrange("(p r) d -> p (r d)", p=128)[:, c * CW:(c + 1) * CW]),
                     reads=[], writes=[f"cvf{i % 2}"], dma=True, detached=True, ring="cv")
            cvs = {"i": 0}

            def conv_batch(n):
                for _ in range(n):
                    i = cvs["i"]
                    if i >= len(chunks):
                        return
                    if i == 0:
                        cv_in(0)
                    if i + 1 < len(chunks):
                        cv_in(i + 1)
                    tsrc, half, c = chunks[i]
                    S.op("pool", lambda e, i=i: e.tensor_copy(out=cvb[i % 2][:, :], in_=cvf[i % 2][:, :]), reads=[f"cvf{i % 2}"], writes=[f"cvb{i % 2}"], detached=True)
                    S.op("pool", lambda e, i=i, half=half, c=c: e.dma_start(out=uvb.rearrange("(p r) c -> p r c", p=128)[:, c, half * D:(half + 1) * D], in_=cvb[i % 2][:, :]),
                         reads=[f"cvb{i % 2}"], writes=[f"cvk{i}"], dma=True, detached=True, ring="cv")
                    conv_keys.append(f"cvk{i}")
                    cvs["i"] = i + 1
        else:
            def conv_batch(n):
                return

        if "A" in passes:
            with ExitStack() as sa:
                WinA = sbt(sa, "WinA", [128, 8, 1536], BF16)
                Wglu = sbt(sa, "Wglu", [128, 4, 512], BF16)
                Wsso = sbt(sa, "Wsso", [128, 4, 1024], BF16)
                n1g_t = sbt(sa, "n1gA", [128, D], F32)
                sv_ = sbt(sa, "s5v", [128, 2, 4], F32)
                sm = sbt(sa, "s5sm", [128, 24, 16], F32)
                cosT = sbt(sa, "cosT", [128, 16, 128], F32)
                sinT = sbt(sa, "sinT", [128, 16, 128], F32)
                BwT = sbt(sa, "BwT", [128, 2, 16, 128], BF16)
                CwT = sbt(sa, "CwT", [128, 2, 16, 128], BF16)
                DEC = sbt(sa, "DEC", [128, 16, 128], F32)
                sa_outer = sa
                sa = ExitStack()
                sa.__enter__()
                stg = sbt(sa, "stgA", [128, 2, 2048], F32)
                load_weight(sa, "WinA", w_in, 1024, 0, 512, WinA, 0, stg, "a")
                load_weight(sa, "WinA", w_in, 1024, 2560, 3584, WinA, 512, stg, "a")
                load_weight(sa, "Wglu", w_glu, 512, 0, 512, Wglu, 0, stg, "a")
                load_weight(sa, "Wsso", w_sso, 512, 0, 1024, Wsso, 0, stg, "a")
                DM(lambda e: e.dma_start(out=n1g_t[:], in_=n1g), [], ["n1g"])
                pa = sbt(sa, "s5a", [128, 3, 16], F32)
                bw = sbt(sa, "s5bw", [128, 2, 16, 128], F32)
                cw = sbt(sa, "s5cw", [128, 2, 16, 128], F32)
                DM(lambda e: e.dma_start(out=pa[:], in_=s5a), [], ["pa"])
                DM(lambda e: e.dma_start(out=bw[:], in_=s5bw), [], ["bw"])
                DM(lambda e: e.dma_start(out=cw[:], in_=s5cw), [], ["cw"])
                DM(lambda e: e.dma_start(out=sv_[:], in_=s5v), [], ["s5v"])
                DT_, LR, TH, RHO, STH, CTH, AR_, AI_, NR, DEN, FRE, FIM, T1, T2, CR128, SR128, CR16, SR16, TH128, TH16, FIMN = range(21)
                are = pa[:, 0, :]
                aim = pa[:, 1, :]
                A(lambda e: e.activation(out=sm[:, DT_, :], in_=pa[:, 2, :], func=AF.Exp), ["pa"], ["sm_dt"])
                V(lambda e: e.tensor_tensor(out=sm[:, LR, :], in0=are, in1=sm[:, DT_, :], op=ALU.mult), ["pa", "sm_dt"], ["sm_lr"])
                V(lambda e: e.tensor_tensor(out=sm[:, TH, :], in0=aim, in1=sm[:, DT_, :], op=ALU.mult), ["pa", "sm_dt"], ["sm_th"])
                A(lambda e: e.activation(out=sm[:, RHO, :], in_=sm[:, LR, :], func=AF.Exp), ["sm_lr"], ["sm_rho"])
                V(lambda e: e.tensor_scalar(out=sm[:, TH128, :], in0=sm[:, TH, :], scalar1=128.0, scalar2=None, op0=ALU.mult), ["sm_th"], ["sm_th128"])
                V(lambda e: e.tensor_scalar(out=sm[:, TH16, :], in0=sm[:, TH, :], scalar1=16.0, scalar2=None, op0=ALU.mult), ["sm_th"], ["sm_th16"])

                thT = sbt(sa, "thT", [128, 16, 128], F32)
                scs = [sbt(sa, f"scs{i}", [128, 2048], F32) for i in range(4)]
                sci = sbt(sa, "sci", [128, 2048], I32)

                def sincos(x_ap, kx, Fn, sin_ap, ksin, cos_ap, kcos, tag):
                    k_ = scs[0][:, :Fn]; y_ = scs[1][:, :Fn]; s1 = scs[2][:, :Fn]; s2 = scs[3][:, :Fn]; ki = sci[:, :Fn]
                    shp = list(x_ap.shape)
                    def vw(ap):
                        if len(shp) == 3:
                            return ap.rearrange("p (a b) -> p a b", a=shp[1])
                        return ap
                    V(lambda e: e.tensor_scalar(out=vw(k_), in0=x_ap, scalar1=1.0 / TWO_PI, scalar2=None, op0=ALU.mult), [kx], ["scs0"])
                    V(lambda e: e.tensor_copy(out=ki, in_=k_), ["scs0"], ["sci"])
                    V(lambda e: e.tensor_copy(out=k_, in_=ki), ["sci"], ["scs0"])
                    V(lambda e: e.scalar_tensor_tensor(out=vw(y_), in0=vw(k_), scalar=-CW1, in1=x_ap, op0=ALU.mult, op1=ALU.add), ["scs0", kx], ["scs1"])
                    V(lambda e: e.scalar_tensor_tensor(out=y_, in0=k_, scalar=-CW2, in1=y_, op0=ALU.mult, op1=ALU.add), ["scs0", "scs1"], ["scs1"])
                    A(lambda e: e.activation(out=s1, in_=y_, func=AF.Sin, scale=0.5), ["scs1"], ["scs2"])
                    A(lambda e: e.activation(out=s2, in_=y_, func=AF.Sin, scale=0.25), ["scs1"], ["scs3"])
                    V(lambda e: e.tensor_tensor(out=k_, in0=s2, in1=s2, op=ALU.mult), ["scs3"], ["scs0"])
                    V(lambda e: e.tensor_scalar(out=k_, in0=k_, scalar1=-2.0, scalar2=1.0, op0=ALU.mult, op1=ALU.add), ["scs0"], ["scs0"])
                    V(lambda e: e.scalar_tensor_tensor(out=sin_ap, in0=vw(s1), scalar=2.0, in1=vw(k_), op0=ALU.mult, op1=ALU.mult), ["scs2", "scs0"], [ksin])
                    V(lambda e: e.tensor_tensor(out=y_, in0=s1, in1=s1, op=ALU.mult), ["scs2"], ["scs1"])
                    V(lambda e: e.tensor_scalar(out=cos_ap, in0=vw(y_), scalar1=-2.0, scalar2=1.0, op0=ALU.mult, op1=ALU.add), ["scs1"], [kcos])

                sincos(sm[:, TH, :], "sm_th", 16, sm[:, STH, :], "sm_sth", sm[:, CTH, :], "sm_cth", "a")
                sincos(sm[:, TH128, :], "sm_th128", 16, sm[:, SR128, :], "sm_sr128", sm[:, CR128, :], "sm_cr128", "b")
                sincos(sm[:, TH16, :], "sm_th16", 16, sm[:, SR16, :], "sm_sr16", sm[:, CR16, :], "sm_cr16", "c")
                for j in range(16):
                    V(lambda e, j=j: e.tensor_scalar(out=thT[:, j, :], in0=iota_f, scalar1=sm[:, TH, j:j + 1], scalar2=None, op0=ALU.mult), ["cstf", "sm_th"], ["thT"])
                sincos(thT[:, :, :], "thT", 2048, sinT[:, :, :], "sinT", cosT[:, :, :], "cosT", "d")
                V(lambda e: e.tensor_tensor(out=sm[:, AR_, :], in0=sm[:, RHO, :], in1=sm[:, CTH, :], op=ALU.mult), ["sm_rho", "sm_cth"], ["sm_ar"])
                V(lambda e: e.tensor_tensor(out=sm[:, AI_, :], in0=sm[:, RHO, :], in1=sm[:, STH, :], op=ALU.mult), ["sm_rho", "sm_sth"], ["sm_ai"])
                V(lambda e: e.tensor_scalar(out=sm[:, NR, :], in0=sm[:, AR_, :], scalar1=-1.0, scalar2=None, op0=ALU.add), ["sm_ar"], ["sm_nr"])
                V(lambda e: e.tensor_tensor(out=sm[:, T1, :], in0=are, in1=are, op=ALU.mult), ["pa"], ["sm_t1"])
                V(lambda e: e.tensor_tensor(out=sm[:, T2, :], in0=aim, in1=aim, op=ALU.mult), ["pa"], ["sm_t2"])
                V(lambda e: e.tensor_tensor(out=sm[:, DEN, :], in0=sm[:, T1, :], in1=sm[:, T2, :], op=ALU.add), ["sm_t1", "sm_t2"], ["sm_den"])
                V(lambda e: e.reciprocal(out=sm[:, DEN, :], in_=sm[:, DEN, :]), ["sm_den"], ["sm_den"])
                V(lambda e: e.tensor_tensor(out=sm[:, T1, :], in0=sm[:, NR, :], in1=are, op=ALU.mult), ["sm_nr", "pa", "sm_den"], ["sm_t1"])
                V(lambda e: e.tensor_tensor(out=sm[:, T2, :], in0=sm[:, AI_, :], in1=aim, op=ALU.mult), ["sm_ai", "pa", "sm_den"], ["sm_t2"])
                V(lambda e: e.tensor_tensor(out=sm[:, FRE, :], in0=sm[:, T1, :], in1=sm[:, T2, :], op=ALU.add), ["sm_t1", "sm_t2"], ["sm_fre"])
                V(lambda e: e.tensor_tensor(out=sm[:, FRE, :], in0=sm[:, FRE, :], in1=sm[:, DEN, :], op=ALU.mult), ["sm_fre", "sm_den"], ["sm_fre"])
                V(lambda e: e.tensor_tensor(out=sm[:, T1, :], in0=sm[:, AI_, :], in1=are, op=ALU.mult), ["sm_ai", "pa", "sm_fre"], ["sm_t1"])
                V(lambda e: e.tensor_tensor(out=sm[:, T2, :], in0=sm[:, NR, :], in1=aim, op=ALU.mult), ["sm_nr", "pa", "sm_fre"], ["sm_t2"])
                V(lambda e: e.tensor_tensor(out=sm[:, FIM, :], in0=sm[:, T1, :], in1=sm[:, T2, :], op=ALU.subtract), ["sm_t1", "sm_t2"], ["sm_fim"])
                V(lambda e: e.tensor_tensor(out=sm[:, FIM, :], in0=sm[:, FIM, :], in1=sm[:, DEN, :], op=ALU.mult), ["sm_fim", "sm_den"], ["sm_fim"])
                bb = sbt(sa, "bb", [128, 3, 128], F32)
                for j in range(16):
                    V(lambda e, j=j: e.tensor_scalar(out=bb[:, 2, :], in0=bw[:, 1, j, :], scalar1=sm[:, FIM, j:j + 1], scalar2=None, op0=ALU.mult), ["bw", "sm_fim"], ["bb2"])
                    V(lambda e, j=j: e.scalar_tensor_tensor(out=bb[:, 0, :], in0=bw[:, 0, j, :], scalar=sm[:, FRE, j:j + 1], in1=bb[:, 2, :], op0=ALU.mult, op1=ALU.subtract), ["bw", "sm_fre", "bb2"], ["bb0"])
                    V(lambda e, j=j: e.tensor_scalar(out=bb[:, 2, :], in0=bw[:, 1, j, :], scalar1=sm[:, FRE, j:j + 1], scalar2=None, op0=ALU.mult), ["bw", "sm_fre", "bb0"], ["bb2"])
                    V(lambda e, j=j: e.scalar_tensor_tensor(out=bb[:, 1, :], in0=bw[:, 0, j, :], scalar=sm[:, FIM, j:j + 1], in1=bb[:, 2, :], op0=ALU.mult, op1=ALU.add), ["bw", "sm_fim", "bb2"], ["bb1"])
                    for ri in range(2):
                        T(lambda e, ri=ri: e.transpose(out=ps[0][:, ri * 128:(ri + 1) * 128], in_=bb[:, ri, :], identity=identf), [f"bb{ri}", "cstf"], ["ps0"])
                    A(lambda e, j=j: e.activation(out=BwT[:, :, j, :], in_=ps[0][:, 0:256].rearrange("p (r m) -> p r m", r=2), func=AF.Copy), ["ps0"], ["BwT"])
                V(lambda e: e.tensor_copy(out=CwT[:, 0, :, :], in_=cw[:, 0, :, :]), ["cw"], ["CwT"])
                V(lambda e: e.tensor_scalar(out=CwT[:, 1, :, :], in0=cw[:, 1, :, :], scalar1=-1.0, scalar2=None, op0=ALU.mult), ["cw"], ["CwT"])
                for j in range(16):
                    V(lambda e, j=j: e.tensor_scalar(out=DEC[:, j, :], in0=ones_f, scalar1=sm[:, RHO, j:j + 1], scalar2=None, op0=ALU.mult), ["cstf", "sm_rho"], ["DEC"])
                V(lambda e: e.memset(DEC[:, :, 0:1], 0.0), ["DEC"], ["DEC"])

                S.emit()
                sa.__exit__(None, None, None)
                sa = sa_outer
                xt = sbt(sa, "xtA", [128, 4, D], F32)
                xnb = sbt(sa, "xnbA", [128, D], BF16)
                xnT = sbt(sa, "xnTA", [128, 8, 512], BF16)
                uT = sbt(sa, "uT", [128, 4, 512], BF16)
                sgs = sbt(sa, "sgs", [128, 8, 512], BF16)
                yg = sbt(sa, "yg", [128, 4, 512], BF16)
                y2 = sbt(sa, "y2", [128, 4, 512], BF16)
                mixT = sbt(sa, "mixTA", [128, 8, 512], BF16)
                scrA = dict(tag="nA", junk=sbt(sa, "junkA", [128, D], F32), ssq=sbt(sa, "ssqA", [128, 1], F32), rs=sbt(sa, "rsA", [128, 3], F32))
                P1f = sbt(sa, "P1f", [128, 1024], F32)
                P2f = sbt(sa, "P2f", [128, 1024], F32)
                Xf = sbt(sa, "Xf", [128, 2, 4, 128], F32)
                RI = sbt(sa, "RI", [128, 2, 16], F32)
                SbB = [sbt(sa, f"SbB{i}", [128, 2, 4, 128], BF16) for i in range(2)]
                P1 = [P1f[:, i * 256:(i + 1) * 256].rearrange("p (r t) -> p r t", r=2) for i in range(2)]
                P2 = [P2f[:, i * 256:(i + 1) * 256].rearrange("p (r t) -> p r t", r=2) for i in range(2)]
                X_ = [Xf[:, :, i, :] for i in range(2)]
                R_ = sbt(sa, "R_", [128, 2, 16, 128], F32)
                Rinit = sbt(sa, "Rinit", [128, 2, 16], F32)
                rt = sbt(sa, "rt", [128, 4, 16], F32)
                Sb = [sbt(sa, f"Sb{i}", [128, 2, 128], BF16) for i in range(2)]
                yv = sbt(sa, "yv", [128, 128], F32)
                sgt = sbt(sa, "sgt", [128, 512], BF16)
                V(lambda e: e.memset(Rinit[:], 0.0), [], ["Rinit"])
                V(lambda e: e.memset(RI[:], 0.0), [], ["RI"])

                for blk in range(nblkA):
                    conv_batch(10)
                    N = 16 if blk == 0 else 512
                    tok0 = 0 if blk == 0 else 16 + 512 * (blk - 1)
                    ntile = 1 if blk == 0 else 4
                    for i in range(ntile):
                        P = min(128, N)
                        r0 = tok0 + i * 128
                        DM(lambda e, i=i, P=P, r0=r0: e.dma_start(out=xt[:P, i, :], in_=xh[r0:r0 + P, :]), [], [f"xtA{i}"])
                        rms_rows(P, xt[:P, i, :], n1g_t[:P, :], [(xnb[:P, :], "xnbA")], f"xtA{i}", "nA", scrA)
                        transpose_rows(P, xnb[:P, :], "xnbA", xnT[:, :, i * 128:i * 128 + P], "xnTA")
                    for q in range(12):
                        bank = ps[1 + q % 2]; kb = f"ps{1 + q % 2}"
                        for k in range(8):
                            T(lambda e, q=q, k=k, bank=bank: e.matmul(bank[:, :N], lhsT=WinA[:, k, q * 128:(q + 1) * 128], rhs=xnT[:, k, :N], start=(k == 0), stop=(k == 7)),
                              ["WinA", "xnTA"], [kb])
                        if q < 4:
                            A(lambda e, q=q, bank=bank: e.activation(out=uT[:, q, :N], in_=bank[:, :N], func=AF.Copy), [kb], ["uT"])
                        else:
                            A(lambda e, q=q, bank=bank: e.activation(out=sgs[:, q - 4, :N], in_=bank[:, :N], func=AF.Sigmoid), [kb], ["sgs"])
                    nsc = 1 if blk == 0 else 4
                    Ts = 16 if blk == 0 else 128
                    allR = [f"R_{j}" for j in range(16)]
                    for sc in range(nsc):
                        t0 = sc * Ts
                        if Ts == 128:
                            for q in range(4):
                                buv = psS[:, :].rearrange("p (j r t) -> p j r t", j=4, r=2)
                                for jj in range(4):
                                    j = 4 * q + jj
                                    for ri in range(2):
                                        T(lambda e, j=j, jj=jj, ri=ri, q=q: e.matmul(psS[:, jj * 256 + ri * 128:jj * 256 + ri * 128 + 128], lhsT=BwT[:, ri, j, :], rhs=uT[:, q, t0:t0 + 128], start=True, stop=True),
                                          ["BwT", "uT"], ["ps3", "ps4"])
                                cosb = cosT[:, 4 * q:4 * q + 4, :].unsqueeze(2).to_broadcast([128, 4, 2, 128])
                                sinb = sinT[:, 4 * q:4 * q + 4, :].unsqueeze(2).to_broadcast([128, 4, 2, 128])
                                P1v = P1f[:, :].rearrange("p (j r t) -> p j r t", j=4, r=2)
                                P2v = P2f[:, :].rearrange("p (j r t) -> p j r t", j=4, r=2)
                                V(lambda e, buv=buv, cosb=cosb, P1v=P1v: e.tensor_tensor(out=P1v, in0=buv, in1=cosb, op=ALU.mult), ["ps3", "ps4", "cosT"], ["P1"])
                                V(lambda e, buv=buv, sinb=sinb, P2v=P2v: e.tensor_tensor(out=P2v, in0=buv, in1=sinb, op=ALU.mult), ["ps3", "ps4", "sinT"], ["P2"])
                                V(lambda e, P1v=P1v, P2v=P2v: e.tensor_tensor(out=Xf[:, 0, :, :], in0=P1v[:, :, 0, :], in1=P2v[:, :, 1, :], op=ALU.add), ["P1", "P2"], ["X"])
                                V(lambda e, P1v=P1v, P2v=P2v: e.tensor_tensor(out=Xf[:, 1, :, :], in0=P1v[:, :, 1, :], in1=P2v[:, :, 0, :], op=ALU.subtract), ["P1", "P2"], ["X"])
                                V(lambda e, q=q: e.tensor_tensor(out=Xf[:, :, :, 0], in0=Xf[:, :, :, 0], in1=RI[:, :, 4 * q:4 * q + 4], op=ALU.add), ["X", "RI"], ["X"])
                                for ri in range(2):
                                    V(lambda e, ri=ri, q=q: e.tensor_tensor_scan(out=R_[:, ri, 4 * q:4 * q + 4, :].rearrange("p j t -> p (j t)"), data0=DEC[:, 4 * q:4 * q + 4, :].rearrange("p j t -> p (j t)"),
                                                                                data1=Xf[:, ri, :, :].rearrange("p j t -> p (j t)"), initial=0.0, op0=ALU.mult, op1=ALU.add),
                                      ["DEC", "X"], [f"R_{4 * q + jj}" for jj in range(4)])
                                Rv = R_[:, :, 4 * q:4 * q + 4, :]
                                cosb2 = cosT[:, 4 * q:4 * q + 4, :].unsqueeze(1).to_broadcast([128, 2, 4, 128])
                                sinb2 = sinT[:, 4 * q:4 * q + 4, :].unsqueeze(1).to_broadcast([128, 2, 4, 128])
                                Q1v = P1f[:, :].rearrange("p (r j t) -> p r j t", r=2, j=4)
                                Q2v = P2f[:, :].rearrange("p (r j t) -> p r j t", r=2, j=4)
                                kR = [f"R_{4 * q + jj}" for jj in range(4)]
                                V(lambda e, Rv=Rv, cosb2=cosb2, Q1v=Q1v: e.tensor_tensor(out=Q1v, in0=Rv, in1=cosb2, op=ALU.mult), kR + ["cosT"], ["P1"])
                                V(lambda e, Rv=Rv, sinb2=sinb2, Q2v=Q2v: e.tensor_tensor(out=Q2v, in0=Rv, in1=sinb2, op=ALU.mult), kR + ["sinT"], ["P2"])
                                sbq = SbB[q % 2]; ksb = f"SbB{q % 2}"
                                V(lambda e, Q1v=Q1v, Q2v=Q2v, sbq=sbq: e.tensor_tensor(out=sbq[:, 0, :, :], in0=Q1v[:, 0, :, :], in1=Q2v[:, 1, :, :], op=ALU.subtract), ["P1", "P2"], [ksb])
                                V(lambda e, Q1v=Q1v, Q2v=Q2v, sbq=sbq: e.tensor_tensor(out=sbq[:, 1, :, :], in0=Q2v[:, 0, :, :], in1=Q1v[:, 1, :, :], op=ALU.add), ["P1", "P2"], [ksb])
                                for jj in range(4):
                                    j = 4 * q + jj
                                    for ri in range(2):
                                        T(lambda e, j=j, jj=jj, ri=ri, q=q, sbq=sbq: e.matmul(ps[5][:, q * 128:(q + 1) * 128], lhsT=CwT[:, ri, j, :], rhs=sbq[:, ri, jj, :], start=(jj == 0 and ri == 0), stop=(jj == 3 and ri == 1)),
                                          ["CwT", ksb], ["ps5"])
                                V(lambda e, q=q: e.scalar_tensor_tensor(out=yv[:, :], in0=uT[:, q, t0:t0 + 128], scalar=sv_[:, 0, q:q + 1], in1=ps[5][:, q * 128:(q + 1) * 128], op0=ALU.mult, op1=ALU.add),
                                  ["uT", "s5v", "ps5"], ["yv"])
                                A(lambda e, q=q: e.activation(out=yg[:, q, t0:t0 + 128], in_=yv[:, :], func=AF.Gelu), ["yv"], ["yg"])
                        else:
                            for j in range(16):
                                q = j // 4
                                pb = j % 2
                                bank = ps[3 + pb]; kb = f"ps{3 + pb}"
                                buv = bank[:, 0:256].rearrange("p (r t) -> p r t", r=2)
                                T(lambda e, j=j, q=q, bank=bank: e.matmul(bank[:, 0:Ts], lhsT=BwT[:, 0, j, :], rhs=uT[:, q, t0:t0 + Ts], start=True, stop=True), ["BwT", "uT"], [kb])
                                T(lambda e, j=j, q=q, bank=bank: e.matmul(bank[:, 128:128 + Ts], lhsT=BwT[:, 1, j, :], rhs=uT[:, q, t0:t0 + Ts], start=True, stop=True), ["BwT", "uT"], [kb])
                                cosb = cosT[:, j, :Ts].unsqueeze(1).to_broadcast([128, 2, Ts])
                                sinb = sinT[:, j, :Ts].unsqueeze(1).to_broadcast([128, 2, Ts])
                                V(lambda e, buv=buv, pb=pb, cosb=cosb: e.tensor_tensor(out=P1[pb][:, :, :Ts], in0=buv[:, :, :Ts], in1=cosb, op=ALU.mult), [kb, "cosT"], ["P1"])
                                V(lambda e, buv=buv, pb=pb, sinb=sinb: e.tensor_tensor(out=P2[pb][:, :, :Ts], in0=buv[:, :, :Ts], in1=sinb, op=ALU.mult), [kb, "sinT"], ["P2"])
                                V(lambda e, pb=pb: e.tensor_tensor(out=X_[pb][:, 0, :Ts], in0=P1[pb][:, 0, :Ts], in1=P2[pb][:, 1, :Ts], op=ALU.add), ["P1", "P2"], ["X"])
                                V(lambda e, pb=pb: e.tensor_tensor(out=X_[pb][:, 1, :Ts], in0=P1[pb][:, 1, :Ts], in1=P2[pb][:, 0, :Ts], op=ALU.subtract), ["P1", "P2"], ["X"])
                                for ri in range(2):
                                    V(lambda e, pb=pb, ri=ri, j=j: e.tensor_tensor_scan(out=R_[:, ri, j, :Ts], data0=sm[:, RHO, j:j + 1].to_broadcast([128, Ts]), data1=X_[pb][:, ri, :Ts],
                                                                                       initial=Rinit[:, ri, j:j + 1], op0=ALU.mult, op1=ALU.add),
                                      ["sm_rho", "X", "Rinit"], [f"R_{j}"])
                                V(lambda e, pb=pb, j=j, cosb=cosb: e.tensor_tensor(out=P1[pb][:, :, :Ts], in0=R_[:, :, j, :Ts], in1=cosb, op=ALU.mult), [f"R_{j}", "cosT"], ["P1"])
                                V(lambda e, pb=pb, j=j, sinb=sinb: e.tensor_tensor(out=P2[pb][:, :, :Ts], in0=R_[:, :, j, :Ts], in1=sinb, op=ALU.mult), [f"R_{j}", "sinT"], ["P2"])
                                V(lambda e, pb=pb: e.tensor_tensor(out=Sb[pb][:, 0, :Ts], in0=P1[pb][:, 0, :Ts], in1=P2[pb][:, 1, :Ts], op=ALU.subtract), ["P1", "P2"], [f"Sb{pb}"])
                                V(lambda e, pb=pb: e.tensor_tensor(out=Sb[pb][:, 1, :Ts], in0=P2[pb][:, 0, :Ts], in1=P1[pb][:, 1, :Ts], op=ALU.add), ["P1", "P2"], [f"Sb{pb}"])
                                T(lambda e, j=j, q=q, pb=pb: e.matmul(ps[5][:, q * 128:q * 128 + Ts], lhsT=CwT[:, 0, j, :], rhs=Sb[pb][:, 0, :Ts], start=(j % 4 == 0), stop=False), ["CwT", f"Sb{pb}"], ["ps5"])
                                T(lambda e, j=j, q=q, pb=pb: e.matmul(ps[5][:, q * 128:q * 128 + Ts], lhsT=CwT[:, 1, j, :], rhs=Sb[pb][:, 1, :Ts], start=False, stop=(j % 4 == 3)), ["CwT", f"Sb{pb}"], ["ps5"])
                                if j % 4 == 3:
                                    V(lambda e, q=q: e.scalar_tensor_tensor(out=yv[:, :Ts], in0=uT[:, q, t0:t0 + Ts], scalar=sv_[:, 0, q:q + 1], in1=ps[5][:, q * 128:q * 128 + Ts], op0=ALU.mult, op1=ALU.add),
                                      ["uT", "s5v", "ps5"], ["yv"])
                                    A(lambda e, q=q: e.activation(out=yg[:, q, t0:t0 + Ts], in_=yv[:, :Ts], func=AF.Gelu), ["yv"], ["yg"])
                        cR = sm[:, CR16 if Ts == 16 else CR128, :]
                        sR = sm[:, SR16 if Ts == 16 else SR128, :]
                        kc = ["sm_cr16", "sm_sr16"] if Ts == 16 else ["sm_cr128", "sm_sr128"]
                        lre = R_[:, 0, :, Ts - 1]
                        lim = R_[:, 1, :, Ts - 1]
                        V(lambda e, cR=cR, lre=lre: e.tensor_tensor(out=rt[:, 0, :], in0=cR, in1=lre, op=ALU.mult), allR + kc, ["rt0"])
                        V(lambda e, sR=sR, lim=lim: e.tensor_tensor(out=rt[:, 1, :], in0=sR, in1=lim, op=ALU.mult), allR + kc, ["rt1"])
                        V(lambda e, sR=sR, lre=lre: e.tensor_tensor(out=rt[:, 2, :], in0=sR, in1=lre, op=ALU.mult), allR + kc, ["rt2"])
                        V(lambda e, cR=cR, lim=lim: e.tensor_tensor(out=rt[:, 3, :], in0=cR, in1=lim, op=ALU.mult), allR + kc, ["rt3"])
                        V(lambda e: e.tensor_tensor(out=Rinit[:, 0, :], in0=rt[:, 0, :], in1=rt[:, 1, :], op=ALU.subtract), ["rt0", "rt1"], ["Rinit"])
                        V(lambda e: e.tensor_tensor(out=Rinit[:, 1, :], in0=rt[:, 2, :], in1=rt[:, 3, :], op=ALU.add), ["rt2", "rt3"], ["Rinit"])
                        V(lambda e: e.tensor_tensor(out=RI[:, :, :], in0=Rinit[:, :, :], in1=sm[:, RHO, :].unsqueeze(1).to_broadcast([128, 2, 16]), op=ALU.mult), ["Rinit", "sm_rho"], ["RI"])
                    for qo in range(4):
                        bank = ps[1 + qo % 2]; kb = f"ps{1 + qo % 2}"
                        for q in range(4):
                            T(lambda e, q=q, qo=qo, bank=bank: e.matmul(bank[:, :N], lhsT=Wglu[:, q, qo * 128:(qo + 1) * 128], rhs=yg[:, q, :N], start=(q == 0), stop=(q == 3)), ["Wglu", "yg"], [kb])
                        A(lambda e, qo=qo, bank=bank: e.activation(out=sgt[:, :N], in_=bank[:, :N], func=AF.Sigmoid, bias=sv_[:, 1, qo:qo + 1]), [kb, "s5v"], ["sgt"])
                        V(lambda e, qo=qo: e.tensor_tensor(out=y2[:, qo, :N], in0=yg[:, qo, :N], in1=sgt[:, :N], op=ALU.mult), ["yg", "sgt"], ["y2"])
                    for o in range(8):
                        bank = ps[1 + o % 2]; kb = f"ps{1 + o % 2}"
                        for q in range(4):
                            T(lambda e, q=q, o=o, bank=bank: e.matmul(bank[:, :N], lhsT=Wsso[:, q, o * 128:(o + 1) * 128], rhs=y2[:, q, :N], start=(q == 0), stop=(q == 3)), ["Wsso", "y2"], [kb])
                        V(lambda e, o=o, bank=bank: e.tensor_tensor(out=mixT[:, o, :N], in0=bank[:, :N], in1=sgs[:, o, :N], op=ALU.mult), [kb, "sgs"], ["mixTA"])
                    DM(lambda e, tok0=tok0, N=N: e.dma_start(out=mixs.rearrange("(o p) t -> p o t", p=128)[:, :, tok0:tok0 + N], in_=mixT[:, :, :N]), ["mixTA"], ["mixs"])
                S.emit(final=(passes[-1] == "A"))

        if "B" in passes:
            with ExitStack() as sbk:
                WinB = sbt(sbk, "WinB", [128, 8, 3072], BF16)
                Wmlo = sbt(sbk, "Wmlo", [128, 8, 1024], BF16)
                Wout = sbt(sbk, "Wout", [128, 8, 1024], BF16)
                n1g_t = sbt(sbk, "n1gB", [128, D], F32)
                mlng_t = sbt(sbk, "mlngB", [128, D], F32)
                mlv_t = sbt(sbk, "mlv", [128, 6, 8], F32)
                bif_t = sbt(sbk, "bif", [128, 8], F32)
                bd = sbt(sbk, "bd", [128, 3, 8, 128], BF16)
                G12 = sbt(sbk, "G12", [128, 2, 8, 8], BF16)
                sbk_outer = sbk
                sbk = ExitStack()
                sbk.__enter__()
                stg = sbt(sbk, "stgB", [128, 2, 1024], F32)
                load_weight(sbk, "WinB", w_in, 1024, 512, 2560, WinB, 0, stg, "b")
                load_weight(sbk, "WinB", w_in, 1024, 3584, 4608, WinB, 2048, stg, "b")
                load_weight(sbk, "Wmlo", w_mlo, 1024, 0, 1024, Wmlo, 0, stg, "b")
                load_weight(sbk, "Wout", w_out, 1024, 0, 1024, Wout, 0, stg, "b")
                DM(lambda e: e.dma_start(out=n1g_t[:], in_=n1g), [], ["n1gB"])
                DM(lambda e: e.dma_start(out=mlng_t[:], in_=mlng), [], ["mlngB"])
                bdf = sbt(sbk, "bdf", [128, 5, 8, 128], F32)
                bdvf = sbt(sbk, "bdvf", [128, 8, 128], F32)
                wiff = sbt(sbk, "wiff", [128, 24, 8], F32)
                DM(lambda e: e.dma_start(out=mlv_t[:], in_=mlv), [], ["mlv"])
                DM(lambda e: e.dma_start(out=bdf[:], in_=mlbd), [], ["bdf"])
                DM(lambda e: e.dma_start(out=bdvf[:], in_=mlbdv), [], ["bdvf"])
                DM(lambda e: e.dma_start(out=wiff[:], in_=mlwif), [], ["wiff"])
                DM(lambda e: e.dma_start(out=bif_t[:], in_=mlbif), [], ["bif"])
                V(lambda e: e.tensor_copy(out=bd[:], in_=bdf[:, 0:3, :, :]), ["bdf"], ["bd"])
                for c in range(8):
                    T(lambda e, c=c: e.matmul(ps[0][:, c * 8:(c + 1) * 8], lhsT=bdf[:, 3, c, :], rhs=wiff[:, c, :], start=True, stop=False), ["bdf", "wiff"], ["ps0"])
                    T(lambda e, c=c: e.matmul(ps[0][:, c * 8:(c + 1) * 8], lhsT=bdf[:, 4, c, :], rhs=wiff[:, 8 + c, :], start=False, stop=True), ["bdf", "wiff"], ["ps0"])
                    T(lambda e, c=c: e.matmul(ps[0][:, 64 + c * 8:64 + (c + 1) * 8], lhsT=bdvf[:, c, :], rhs=wiff[:, 16 + c, :], start=True, stop=True), ["bdvf", "wiff"], ["ps0"])
                V(lambda e: e.tensor_copy(out=G12[:], in_=ps[0][:, 0:128].rearrange("p (a c n) -> p a c n", a=2, c=8)), ["ps0"], ["G12"])
                tri = cstf[:, 1, :]

                S.emit()
                sbk.__exit__(None, None, None)
                sbk = sbk_outer
                NB = 256
                xt = sbt(sbk, "xtB", [128, 2, D], F32)
                xnb = sbt(sbk, "xnbB", [128, D], BF16)
                xnT = sbt(sbk, "xnTB", [128, 8, NB], BF16)
                xmT = sbt(sbk, "xmT", [128, 8, 3 + NB], BF16)
                xcT = sbt(sbk, "xcT", [128, 8, NB], BF16)
                acc = sbt(sbk, "accB", [128, NB], F32)
                sgm = sbt(sbk, "sgm", [128, 8, NB], BF16)
                qT = sbt(sbk, "qT", [128, 8, NB], BF16)
                kT = sbt(sbk, "kT", [128, 8, NB], BF16)
                mixin = sbt(sbk, "mixin", [128, 8, NB], BF16)
                hmT = sbt(sbk, "hmT", [128, 8, NB], BF16)
                mixT = sbt(sbk, "mixTB", [128, 8, NB], BF16)
                tmpf = sbt(sbk, "tmpfB", [128, NB], F32)
                h1t = sbt(sbk, "h1tB", [128, D], F32)
                scrB = dict(tag="nB", junk=sbt(sbk, "junkB", [128, D], F32), ssq=sbt(sbk, "ssqB", [128, 1], F32), rs=sbt(sbk, "rsB", [128, 3], F32))
                sigz2 = [sbt(sbk, f"sigz{i}", [64, D], BF16) for i in range(2)]
                vaug2 = [sbt(sbk, f"vaug{i}", [64, 4, 257], BF16) for i in range(2)]
                vw2 = [sbt(sbk, f"vw{i}", [64, 4, 257], BF16) for i in range(2)]
                kk2 = [sbt(sbk, f"kk{i}", [64, D], BF16) for i in range(2)]
                gt2 = [sbt(sbk, f"gt{i}", [64, 8], F32) for i in range(2)]
                gs2 = [sbt(sbk, f"gs{i}", [128, 8, 4], F32) for i in range(2)]
                E1, NLF, EG, EB, TMP, KSC, WST, ENB = range(8)
                SpT = sbt(sbk, "SpT", [64, 64], BF16)
                ho = sbt(sbk, "ho", [64, 256], F32)
                hjunk = sbt(sbk, "hjunk", [64, 256], F32)
                hs = sbt(sbk, "hs", [64, 8], F32)
                hn = sbt(sbk, "hn", [64, D], BF16)
                CT = sbt(sbk, "CT", [128, 8, 257], F32)
                CTb = sbt(sbk, "CTb", [128, 8, 257], BF16)
                V(lambda e: e.memset(CT[:], 0.0), [], ["CT"])
                V(lambda e: e.memset(CTb[:], 0.0), [], ["CTb"])
                V(lambda e: e.memset(xmT[:], 0.0), [], ["xmT"])
                for i_ in range(2):
                    V(lambda e, i_=i_: e.memset(vaug2[i_][:], 1.0), [], [f"vaug{i_}"])
                    V(lambda e, i_=i_: e.memset(gs2[i_][:], 0.0), [], [f"gs{i_}"])
                LN16 = math.log(16.0)

                for blk in range(nblkB):
                    conv_batch(10)
                    N = 16 if blk == 0 else NB
                    tok0 = 0 if blk == 0 else 16 + NB * (blk - 1)
                    ntile = 1 if blk == 0 else 2
                    for i in range(ntile):
                        P = min(128, N)
                        r0 = tok0 + i * 128
                        DM(lambda e, i=i, P=P, r0=r0: e.dma_start(out=xt[:P, i, :], in_=xh[r0:r0 + P, :]), [], [f"xtB{i}"])
                        rms_rows(P, xt[:P, i, :], n1g_t[:P, :], [(xnb[:P, :], "xnbB")], f"xtB{i}", "nB", scrB)
                        transpose_rows(P, xnb[:P, :], "xnbB", xnT[:, :, i * 128:i * 128 + P], "xnTB")
                    DM(lambda e, tok0=tok0, N=N: e.dma_start(out=mixin[:, :, :N], in_=mixs.rearrange("(o p) t -> p o t", p=128)[:, :, tok0:tok0 + N]), ["mixs"], ["mixin"])
                    for c in range(16):
                        bank = ps[1 + c % 2]; kb = f"ps{1 + c % 2}"
                        col0 = c * 128 if c < 8 else 2048 + (c - 8) * 128
                        for k in range(8):
                            T(lambda e, k=k, col0=col0, bank=bank: e.matmul(bank[:, :N], lhsT=WinB[:, k, col0:col0 + 128], rhs=xnT[:, k, :N], start=(k == 0), stop=(k == 7)), ["WinB", "xnTB"], [kb])
                        if c < 8:
                            A(lambda e, c=c, bank=bank: e.activation(out=xmT[:, c, 3:3 + N], in_=bank[:, :N], func=AF.Copy), [kb], ["xmT"])
                        else:
                            A(lambda e, c=c, bank=bank: e.activation(out=sgm[:, c - 8, :N], in_=bank[:, :N], func=AF.Sigmoid), [kb], ["sgm"])
                    for c in range(8):
                        V(lambda e, c=c: e.tensor_scalar(out=acc[:, :N], in0=xmT[:, c, 0:N], scalar1=mlv_t[:, 0, c:c + 1], scalar2=None, op0=ALU.mult), ["xmT", "mlv"], ["accB"])
                        for j in range(1, 4):
                            V(lambda e, c=c, j=j: e.scalar_tensor_tensor(out=acc[:, :N], in0=xmT[:, c, j:j + N], scalar=mlv_t[:, j, c:c + 1], in1=acc[:, :N], op0=ALU.mult, op1=ALU.add), ["xmT", "mlv", "accB"], ["accB"])
                        A(lambda e, c=c: e.activation(out=xcT[:, c, :N], in_=acc[:, :N], func=AF.Silu, bias=mlv_t[:, 4, c:c + 1]), ["accB", "mlv"], ["xcT"])
                    for c in range(16):
                        bank = ps[1 + c % 2]; kb = f"ps{1 + c % 2}"
                        w = c // 8; cc = c % 8
                        T(lambda e, w=w, cc=cc, bank=bank: e.matmul(bank[:, :N], lhsT=bd[:, w, cc, :], rhs=xcT[:, cc, :N], start=True, stop=True), ["bd", "xcT"], [kb])
                        dst = qT if w == 0 else kT
                        A(lambda e, cc=cc, bank=bank, dst=dst: e.activation(out=dst[:, cc, :N], in_=bank[:, :N], func=AF.Copy), [kb], ["qT" if w == 0 else "kT"])
                    Lc = 16 if blk == 0 else 64

                    def chunk_pre(ch):
                        pp = ch % 2
                        sigzP, vaugP, vw_P, kkP, gtP, gsP = sigz2[pp], vaug2[pp], vw2[pp], kk2[pp], gt2[pp], gs2[pp]
                        o0 = ch * Lc
                        for half in range(2):
                            bank = ps[1 + half]; kb = f"ps{1 + half}"
                            for k in range(8):
                                T(lambda e, k=k, half=half, bank=bank: e.matmul(bank[:Lc, :], lhsT=xnT[:, k, o0:o0 + Lc], rhs=WinB[:, k, 1024 + half * 512:1024 + (half + 1) * 512], start=(k == 0), stop=(k == 7)), ["xnTB", "WinB"], [kb])
                            A(lambda e, half=half, bank=bank: e.activation(out=sigzP[:Lc, half * 512:(half + 1) * 512], in_=bank[:Lc, :], func=AF.Sigmoid), [kb], [f"sigz{pp}"])
                        for c in range(8):
                            T(lambda e, c=c: e.matmul(ps[0][:Lc, 0:8], lhsT=xcT[:, c, o0:o0 + Lc], rhs=G12[:, 0, c, :], start=(c == 0), stop=False), ["xcT", "G12"], ["ps0"])
                        for c in range(8):
                            T(lambda e, c=c: e.matmul(ps[0][:Lc, 0:8], lhsT=xmT[:, c, 3 + o0:3 + o0 + Lc], rhs=G12[:, 1, c, :], start=False, stop=(c == 7)), ["xmT", "G12"], ["ps0"])
                        V(lambda e: e.tensor_tensor(out=gtP[:Lc, :], in0=ps[0][:Lc, 0:8], in1=bif_t[:Lc, :], op=ALU.add), ["ps0", "bif"], [f"gt{pp}"])
                        A(lambda e: e.activation(out=gsP[:Lc, E1, :], in_=gtP[:Lc, 4:8], func=AF.Exp, scale=-1.0), [f"gt{pp}"], [f"gs_e1{pp}"])
                        A(lambda e: e.activation(out=gsP[:Lc, NLF, :], in_=gsP[:Lc, E1, :], func=AF.Ln, bias=1.0), [f"gs_e1{pp}"], [f"gs_nlf{pp}"])
                        T(lambda e: e.matmul(ps[0][:Lc, 8:12], lhsT=tri[:Lc, :Lc], rhs=gsP[:Lc, NLF, :], start=True, stop=True), ["cstf", f"gs_nlf{pp}"], ["ps0"])
                        T(lambda e: e.matmul(ps[0][:, 12:16], lhsT=ones_f[:Lc, :], rhs=gsP[:Lc, NLF, :], start=True, stop=True), ["cstf", f"gs_nlf{pp}"], ["ps0"])
                        A(lambda e: e.activation(out=gsP[:, EG, :], in_=ps[0][:, 12:16], func=AF.Exp, scale=-1.0), ["ps0"], [f"gs_eg{pp}"])
                        A(lambda e: e.activation(out=gsP[:Lc, ENB, :], in_=ps[0][:Lc, 8:12], func=AF.Exp), ["ps0"], [f"gs_enb{pp}"])
                        V(lambda e: e.tensor_tensor(out=gsP[:Lc, TMP, :], in0=ps[0][:Lc, 8:12], in1=gtP[:Lc, 0:4], op=ALU.add), ["ps0", f"gt{pp}"], [f"gs_tmp{pp}"])
                        A(lambda e: e.activation(out=gsP[:Lc, KSC, :], in_=gsP[:Lc, TMP, :], func=AF.Exp, bias=-LN16), [f"gs_tmp{pp}"], [f"gs_ksc{pp}"])
                        V(lambda e: e.tensor_tensor(out=gsP[:Lc, WST, :], in0=gsP[:Lc, KSC, :], in1=gsP[:Lc, EG, :], op=ALU.mult), [f"gs_ksc{pp}", f"gs_eg{pp}"], [f"gs_wst{pp}"])
                        for half in range(2):
                            bank = ps[1 + half]; kb = f"ps{1 + half}"
                            for c4 in range(4):
                                c = half * 4 + c4
                                T(lambda e, c=c, c4=c4, bank=bank: e.matmul(bank[:Lc, c4 * 128:(c4 + 1) * 128], lhsT=xmT[:, c, 3 + o0:3 + o0 + Lc], rhs=bd[:, 2, c, :], start=True, stop=True), ["xmT", "bd"], [kb])
                            A(lambda e, half=half, bank=bank: e.activation(out=vaugP[:Lc, half * 2:half * 2 + 2, 0:256], in_=bank[:Lc, :].rearrange("p (h d) -> p h d", h=2), func=AF.Copy), [kb], [f"vaug{pp}"])
                        for h in range(4):
                            V(lambda e, h=h: e.tensor_scalar(out=vw_P[:Lc, h, :], in0=vaugP[:Lc, h, :], scalar1=gsP[:Lc, WST, h:h + 1], scalar2=None, op0=ALU.mult), [f"vaug{pp}", f"gs_wst{pp}"], [f"vw{pp}"])
                        for half in range(2):
                            bank = ps[1 + half]; kb = f"ps{1 + half}"
                            for c4 in range(4):
                                c = half * 4 + c4
                                T(lambda e, c=c, c4=c4, bank=bank: e.matmul(bank[:Lc, c4 * 128:(c4 + 1) * 128], lhsT=xcT[:, c, o0:o0 + Lc], rhs=bd[:, 1, c, :], start=True, stop=True), ["xcT", "bd"], [kb])
                            A(lambda e, half=half, bank=bank: e.activation(out=kkP[:Lc, half * 512:(half + 1) * 512], in_=bank[:Lc, :], func=AF.Copy), [kb], [f"kk{pp}"])

                    def chunk_heads(ch):
                        pp = ch % 2
                        o0 = ch * Lc
                        sigzP, vaugP, vw_P, kkP, gtP, gsP = sigz2[pp], vaug2[pp], vw2[pp], kk2[pp], gt2[pp], gs2[pp]
                        for h in range(4):
                            c0 = 2 * h
                            T(lambda e, c0=c0: e.matmul(ps[3][:Lc, :Lc], lhsT=kT[:, c0, o0:o0 + Lc], rhs=qT[:, c0, o0:o0 + Lc], start=True, stop=False), ["kT", "qT"], ["ps3"])
                            T(lambda e, c0=c0: e.matmul(ps[3][:Lc, :Lc], lhsT=kT[:, c0 + 1, o0:o0 + Lc], rhs=qT[:, c0 + 1, o0:o0 + Lc], start=False, stop=True), ["kT", "qT"], ["ps3"])
                            V(lambda e, h=h: e.scalar_tensor_tensor(out=SpT[:Lc, :Lc], in0=ps[3][:Lc, :Lc], scalar=gsP[:Lc, KSC, h:h + 1], in1=maskT[:Lc, :Lc], op0=ALU.mult, op1=ALU.mult), ["ps3", f"gs_ksc{pp}", "cstf"], ["SpT"])
                            T(lambda e, h=h: e.matmul(ps[4][:Lc, 0:257], lhsT=SpT[:Lc, :Lc], rhs=vaugP[:Lc, h, :], start=True, stop=False), ["SpT", f"vaug{pp}"], ["ps4"])
                            T(lambda e, h=h, c0=c0: e.matmul(ps[4][:Lc, 0:257], lhsT=qT[:, c0, o0:o0 + Lc], rhs=CTb[:, c0, :], start=False, stop=False), ["qT", "CTb"], ["ps4"])
                            T(lambda e, h=h, c0=c0: e.matmul(ps[4][:Lc, 0:257], lhsT=qT[:, c0 + 1, o0:o0 + Lc], rhs=CTb[:, c0 + 1, :], start=False, stop=True), ["qT", "CTb"], ["ps4"])
                            V(lambda e, h=h: e.tensor_tensor(out=hs[:Lc, 0:1], in0=ps[4][:Lc, 256:257], in1=gsP[:Lc, ENB, h:h + 1], op=ALU.max), ["ps4", f"gs_enb{pp}"], ["hs0"])
                            V(lambda e, h=h: e.scalar_tensor_tensor(out=hs[:Lc, 1:2], in0=ps[4][:Lc, 256:257], scalar=-1.0, in1=hs[:Lc, 0:1], op0=ALU.mult, op1=ALU.max), ["ps4", "hs0"], ["hs1"])
                            V(lambda e: e.reciprocal(out=hs[:Lc, 3:4], in_=hs[:Lc, 1:2]), ["hs1"], ["hs3"])
                            V(lambda e, h=h: e.scalar_tensor_tensor(out=ho[:Lc, :], in0=ps[4][:Lc, 0:256], scalar=hs[:Lc, 3:4], in1=sigzP[:Lc, h * 256:(h + 1) * 256], op0=ALU.mult, op1=ALU.mult), ["ps4", "hs3", f"sigz{pp}"], ["ho"])
                            A(lambda e: e.activation(out=hjunk[:Lc, :], in_=ho[:Lc, :], func=AF.Square, accum_out=hs[:Lc, 4:5]), ["ho"], ["hjunk", "hs4"])
                            A(lambda e: e.activation(out=hs[:Lc, 6:7], in_=hs[:Lc, 4:5], func=AF.Ln, scale=1.0 / 256.0, bias=EPS), ["hs4"], ["hs6"])
                            A(lambda e: e.activation(out=hs[:Lc, 7:8], in_=hs[:Lc, 6:7], func=AF.Exp, scale=-0.5), ["hs6"], ["hs7"])
                            V(lambda e, h=h: e.scalar_tensor_tensor(out=hn[:Lc, h * 256:(h + 1) * 256], in0=ho[:Lc, :], scalar=hs[:Lc, 7:8], in1=mlng_t[:Lc, h * 256:(h + 1) * 256], op0=ALU.mult, op1=ALU.mult), ["ho", "hs7", "mlngB"], ["hn"])
                            for dc in range(2):
                                bank = ps[5 + dc]; kb = f"ps{5 + dc}"
                                T(lambda e, h=h, dc=dc, bank=bank: e.matmul(bank[:, 0:257], lhsT=kkP[:Lc, (2 * h + dc) * 128:(2 * h + dc + 1) * 128], rhs=vw_P[:Lc, h, :], start=True, stop=True), [f"kk{pp}", f"vw{pp}"], [kb])
                                V(lambda e, h=h, dc=dc, bank=bank: e.scalar_tensor_tensor(out=CT[:, 2 * h + dc, :], in0=CT[:, 2 * h + dc, :], scalar=gsP[:, EG, h:h + 1], in1=bank[:, 0:257], op0=ALU.mult, op1=ALU.add), ["CT", f"gs_eg{pp}", kb], ["CT"])
                                A(lambda e, h=h, dc=dc: e.activation(out=CTb[:, 2 * h + dc, :], in_=CT[:, 2 * h + dc, :], func=AF.Copy), ["CT"], ["CTb"])
                        for c in range(8):
                            T(lambda e, c=c: e.transpose(out=psb[:, c * 64:c * 64 + Lc], in_=hn[:Lc, c * 128:(c + 1) * 128], identity=identb[:Lc, :Lc]), ["hn", "identb"], ["psb"])
                        for c in range(8):
                            V(lambda e, c=c: e.scalar_tensor_tensor(out=hmT[:, c, o0:o0 + Lc], in0=xcT[:, c, o0:o0 + Lc], scalar=mlv_t[:, 5, c:c + 1], in1=psb[:, c * 64:c * 64 + Lc], op0=ALU.mult, op1=ALU.add), ["xcT", "mlv", "psb"], ["hmT"])

                    nch = N // Lc
                    chunk_pre(0)
                    for ch in range(nch):
                        if ch + 1 < nch:
                            chunk_pre(ch + 1)
                        chunk_heads(ch)
                    V(lambda e, N=N: e.tensor_copy(out=xmT[:, :, 0:3], in_=xmT[:, :, N:N + 3]), ["xmT"], ["xmT"])
                    if blk == 0:
                        continue
                    for o in range(8):
                        bank = ps[1 + o % 2]; kb = f"ps{1 + o % 2}"
                        for c in range(8):
                            T(lambda e, c=c, o=o, bank=bank: e.matmul(bank[:, :N], lhsT=Wmlo[:, c, o * 128:(o + 1) * 128], rhs=hmT[:, c, :N], start=(c == 0), stop=(c == 7)), ["Wmlo", "hmT"], [kb])
                        V(lambda e, o=o, bank=bank: e.tensor_tensor(out=tmpf[:, :N], in0=bank[:, :N], in1=sgm[:, o, :N], op=ALU.mult), [kb, "sgm"], ["tmpfB"])
                        V(lambda e, o=o: e.tensor_tensor(out=mixT[:, o, :N], in0=tmpf[:, :N], in1=mixin[:, o, :N], op=ALU.add), ["tmpfB", "mixin"], ["mixTB"])
                    for i in range(ntile):
                        for half in range(2):
                            bank = ps[1 + half]; kb = f"ps{1 + half}"
                            for c in range(8):
                                T(lambda e, c=c, i=i, half=half, bank=bank: e.matmul(bank[:, :], lhsT=mixT[:, c, i * 128:(i + 1) * 128], rhs=Wout[:, c, half * 512:(half + 1) * 512], start=(c == 0), stop=(c == 7)), ["mixTB", "Wout"], [kb])
                            V(lambda e, i=i, half=half, bank=bank: e.tensor_tensor(out=h1t[:, half * 512:(half + 1) * 512], in0=bank[:, :], in1=xt[:, i, half * 512:(half + 1) * 512], op=ALU.add), [kb, f"xtB{i}"], ["h1tB"])
                        r0 = tok0 - 16 + i * 128
                        DM(lambda e, r0=r0: e.dma_start(out=h1[r0:r0 + 128, :], in_=h1t[:, :]), ["h1tB"], ["h1"])
                S.emit(final=(passes[-1] == "B"))

        if "C" in passes:
            conv_batch(100000)
            with ExitStack() as sc_:
                Wpq = sbt(sc_, "Wpq", [128, 8, 2048], BF16)
                n2g_t = sbt(sc_, "n2gC", [128, D], F32)
                fng_t = sbt(sc_, "fngC", [128, D], F32)
                kyb = sbt(sc_, "kyb", [128, 16, 128], BF16)
                sc_outer = sc_
                sc_ = ExitStack()
                sc_.__enter__()
                stg = sbt(sc_, "stgC", [128, 2, 2048], F32)
                load_weight(sc_, "Wpq", w_pq, 1024, 0, 2048, Wpq, 0, stg, "c")
                kyf = sbt(sc_, "kyf", [128, 16, 128], F32)
                DM(lambda e: e.dma_start(out=n2g_t[:], in_=n2g), [], ["n2gC"])
                DM(lambda e: e.dma_start(out=fng_t[:], in_=fng), [], ["fngC"])
                DM(lambda e: e.dma_start(out=kyf[:], in_=keysT), [], ["kyf"])
                V(lambda e: e.tensor_copy(out=kyb[:], in_=kyf[:]), ["kyf"], ["kyb"])
                S.emit()
                sc_.__exit__(None, None, None)
                sc_ = sc_outer
                h1t = [sbt(sc_, f"h1tC{i}", [128, D], F32) for i in range(2)]
                xn2d = [sbt(sc_, f"xn2_{i}", [128, D], BF16) for i in range(2)]
                xn2b = sbt(sc_, "xn2b", [128, D], BF16)
                xn2T = sbt(sc_, "xn2T", [128, 8, 128], BF16)
                qTb = sbt(sc_, "qTb", [128, 16, 128], BF16)
                scq = sbt(sc_, "scq", [128, 16, 128], F32)
                wk = sbt(sc_, "wkC", [128, 256], F32)
                svt = sbt(sc_, "svt", [128, 16, 16], F32)
                sit = sbt(sc_, "sit", [128, 16, 16], U32)
                sif = sbt(sc_, "sif", [128, 16, 16], F32)
                si0s = sbt(sc_, "si0s", [128, 8, 16], F32)
                cand = sbt(sc_, "cand", [128, 8, 256], F32)
                cs = sbt(sc_, "cs", [128, 8, 16], F32)
                cpos = sbt(sc_, "cpos", [128, 8, 16], U32)
                cab = sbt(sc_, "cab", [128, 2, 8, 16], U32)
                cabf = sbt(sc_, "cabf", [128, 2, 8, 16], F32)
                oh = sbt(sc_, "oh", [128, 8, 16, 16], F32)
                e01 = sbt(sc_, "e01", [128, 2, 8, 16], F32)
                eidf = sbt(sc_, "eidf", [128, 128], F32)
                eidi2 = [sbt(sc_, f"eidi{i}", [128, 128], I32) for i in range(2)]
                gex = sbt(sc_, "gex", [128, 8, 16], F32)
                gz = sbt(sc_, "gz", [128, 2, 8], F32)
                gate2 = [sbt(sc_, f"gate{i}", [128, 128], F32) for i in range(2)]
                dots = sbt(sc_, "dots", [128, 128], F32)
                actv = sbt(sc_, "actv", [128, 128], F32)
                wgt2 = [sbt(sc_, f"wgt{i}", [128, 128], F32) for i in range(2)]
                NG = 16
                uvg = [sbt(sc_, f"uvg{i}", [128, 2 * D], BF16) for i in range(NG)]
                prd = [sbt(sc_, f"prd{i}", [128, D], BF16) for i in range(4)]
                dgw = [sbt(sc_, f"dgw{i}", [128, 128], BF16) for i in range(2)]
                hout = sbt(sc_, "hout", [128, D], F32)
                outt = sbt(sc_, "outt", [128, D], F32)
                scrC = dict(tag="nC", junk=sbt(sc_, "junkC", [128, D], F32), ssq=sbt(sc_, "ssqC", [128, 1], F32), rs=sbt(sc_, "rsC", [128, 3], F32))
                iota16 = cstf[:, 2, 0:16]

                def top16(src_ap, ksrc, n, v_ap, i_ap, kv, ki):
                    V(lambda e: e.max(out=v_ap[:, 0:8], in_=src_ap), [ksrc], [kv])
                    V(lambda e: e.max_index(out=i_ap[:, 0:8], in_max=v_ap[:, 0:8], in_values=src_ap), [ksrc, kv], [ki])
                    V(lambda e: e.match_replace(out=wk[:, :n], in_to_replace=v_ap[:, 0:8], in_values=src_ap, imm_value=-1e30), [ksrc, kv], ["wkC"])
                    V(lambda e: e.max(out=v_ap[:, 8:16], in_=wk[:, :n]), ["wkC"], [kv])
                    V(lambda e: e.max_index(out=i_ap[:, 8:16], in_max=v_ap[:, 8:16], in_values=wk[:, :n]), ["wkC", kv], [ki])

                def phase_P1(ti):
                    hb = h1t[ti % 2]; khb = f"h1tC{ti % 2}"
                    xn2 = xn2d[ti % 2]; kxn2 = f"xn2_{ti % 2}"
                    DM(lambda e, ti=ti, hb=hb: e.dma_start(out=hb[:, :], in_=h1[ti * 128:(ti + 1) * 128, :]), ["h1"], [khb])
                    rms_rows(128, hb[:, :], n2g_t[:, :], [(xn2[:, :], kxn2)], khb, "nC", scrC)
                    transpose_rows(128, xn2[:, :], kxn2, xn2T[:, :, :], "xn2T")
                    for g4 in range(4):
                        bank = ps[1 + g4 % 2]; kb = f"ps{1 + g4 % 2}"
                        for hi4 in range(4):
                            hi = g4 * 4 + hi4
                            for k in range(8):
                                T(lambda e, hi=hi, hi4=hi4, k=k, bank=bank: e.matmul(bank[:, hi4 * 128:(hi4 + 1) * 128], lhsT=Wpq[:, k, hi * 128:(hi + 1) * 128], rhs=xn2T[:, k, :], start=(k == 0), stop=(k == 7)), ["Wpq", "xn2T"], [kb])
                        A(lambda e, g4=g4, bank=bank: e.activation(out=qTb[:, g4 * 4:(g4 + 1) * 4, :], in_=bank[:, :].rearrange("p (a t) -> p a t", a=4), func=AF.Copy), [kb], ["qTb"])
                    for g4 in range(4):
                        bank = ps[1 + g4 % 2]; kb = f"ps{1 + g4 % 2}"
                        for hi4 in range(4):
                            hi = g4 * 4 + hi4
                            T(lambda e, hi=hi, hi4=hi4, bank=bank: e.matmul(bank[:, hi4 * 128:(hi4 + 1) * 128], lhsT=qTb[:, hi, :], rhs=kyb[:, hi, :], start=True, stop=True), ["qTb", "kyb"], [kb])
                        A(lambda e, g4=g4, bank=bank: e.activation(out=scq[:, g4 * 4:(g4 + 1) * 4, :], in_=bank[:, :].rearrange("p (a t) -> p a t", a=4), func=AF.Copy), [kb], ["scq"])

                def phase_P2_gen(ti):
                    eidi = eidi2[ti % 2]; keid = f"eidi{ti % 2}"
                    gate = gate2[ti % 2]; kgate = f"gate{ti % 2}"
                    for hi in range(16):
                        top16(scq[:, hi, :], "scq", 128, svt[:, hi, :], sit[:, hi, :], "svt", "sit")
                        yield
                    V(lambda e: e.tensor_copy(out=sif[:], in_=sit[:]), ["sit"], ["sif"])
                    sv4 = svt[:, :, :].rearrange("p (h i) k -> p h i k", i=2)
                    sf4 = sif[:, :, :].rearrange("p (h i) k -> p h i k", i=2)
                    V(lambda e, sf4=sf4: e.tensor_scalar(out=si0s[:], in0=sf4[:, :, 0, :], scalar1=128.0, scalar2=None, op0=ALU.mult), ["sif"], ["si0s"])
                    V(lambda e, sv4=sv4: e.tensor_tensor(out=cand[:, :, :].rearrange("p h (a b) -> p h a b", a=16),
                                                         in0=sv4[:, :, 0, :].unsqueeze(3).to_broadcast([128, 8, 16, 16]),
                                                         in1=sv4[:, :, 1, :].unsqueeze(2).to_broadcast([128, 8, 16, 16]), op=ALU.add), ["svt"], ["cand"])
                    yield
                    for h in range(8):
                        top16(cand[:, h, :], "cand", 256, cs[:, h, :], cpos[:, h, :], "cs", "cpos")
                        yield
                    V(lambda e: e.tensor_single_scalar(out=cab[:, 0, :, :], in_=cpos[:], scalar=4, op=ALU.logical_shift_right), ["cpos"], ["cab"])
                    V(lambda e: e.tensor_single_scalar(out=cab[:, 1, :, :], in_=cpos[:], scalar=15, op=ALU.bitwise_and), ["cpos"], ["cab"])
                    V(lambda e: e.tensor_copy(out=cabf[:], in_=cab[:]), ["cab"], ["cabf"])
                    yield
                    for ab in range(2):
                        srcv = si0s[:, :, :] if ab == 0 else sf4[:, :, 1, :]
                        V(lambda e, ab=ab: e.tensor_tensor(out=oh[:], in0=cabf[:, ab, :, :].unsqueeze(3).to_broadcast([128, 8, 16, 16]),
                                                           in1=iota16.unsqueeze(1).unsqueeze(1).to_broadcast([128, 8, 16, 16]), op=ALU.is_equal), ["cabf", "cstf"], ["oh"])
                        V(lambda e, srcv=srcv: e.tensor_tensor(out=oh[:], in0=oh[:], in1=srcv.unsqueeze(2).to_broadcast([128, 8, 16, 16]), op=ALU.mult), ["oh", "si0s", "sif"], ["oh"])
                        V(lambda e, ab=ab: e.tensor_reduce(out=e01[:, ab, :, :], in_=oh[:], axis=AX.X, op=ALU.add), ["oh"], ["e01"])
                        yield
                    V(lambda e: e.tensor_tensor(out=eidf[:, :].rearrange("p (h k) -> p h k", h=8), in0=e01[:, 0, :, :], in1=e01[:, 1, :, :], op=ALU.add), ["e01"], ["eidf"])
                    V(lambda e, eidi=eidi: e.tensor_copy(out=eidi[:], in_=eidf[:]), ["eidf"], [keid])
                    V(lambda e: e.tensor_tensor(out=gex[:], in0=cs[:], in1=cs[:, :, 0:1].to_broadcast([128, 8, 16]), op=ALU.subtract), ["cs"], ["gex"])
                    A(lambda e: e.activation(out=gex[:], in_=gex[:], func=AF.Exp), ["gex"], ["gex"])
                    V(lambda e: e.tensor_reduce(out=gz[:, 0, :], in_=gex[:], axis=AX.X, op=ALU.add), ["gex"], ["gz0"])
                    V(lambda e: e.reciprocal(out=gz[:, 1, :], in_=gz[:, 0, :]), ["gz0"], ["gz1"])
                    V(lambda e, gate=gate: e.tensor_tensor(out=gate[:, :].rearrange("p (h k) -> p h k", h=8), in0=gex[:], in1=gz[:, 1, :].unsqueeze(2).to_broadcast([128, 8, 16]), op=ALU.mult), ["gex", "gz1"], [kgate])

                djunk2 = [sbt(sc_, f"djunkb{i}", [128, D], BF16) for i in range(2)]
                def phase_P2(ti):
                    for _ in phase_P2_gen(ti):
                        pass

                first_gather = [True]
                gctr = [0]

                def epilogue(ti):
                    acc0 = 5 if ti % 2 == 0 else 3
                    hb = h1t[ti % 2]; khb = f"h1tC{ti % 2}"
                    for half in range(2):
                        V(lambda e, half=half, hb=hb: e.tensor_tensor(out=hout[:, half * 512:(half + 1) * 512], in0=ps[acc0 + half][:, :], in1=hb[:, half * 512:(half + 1) * 512], op=ALU.add), [f"ps{acc0 + half}", khb], ["hout"])
                    rms_rows(128, hout[:, :], fng_t[:, :], [(outt[:, :], "outt")], "hout", "nF", scrC)
                    DM(lambda e, ti=ti: e.dma_start(out=out[ti * 128:(ti + 1) * 128, :], in_=outt[:, :]), ["outt"], ["out"])

                def tile_body(ti):
                    acc0 = 5 if ti % 2 == 0 else 3
                    xn2 = xn2d[ti % 2]; kxn2 = f"xn2_{ti % 2}"
                    eidi = eidi2[ti % 2]; keid = f"eidi{ti % 2}"
                    gate = gate2[ti % 2]; kgate = f"gate{ti % 2}"
                    wgt = wgt2[ti % 2]; kw = f"wgt{ti % 2}"
                    hb = h1t[ti % 2]; khb = f"h1tC{ti % 2}"
                    GS = 2
                    pending = []

                    def vside(g, bufs):
                        for k_, sl in enumerate(range(g * GS, (g + 1) * GS)):
                            b = bufs[k_]
                            dg = dgw[sl % 2]; kdg = f"dgw{sl % 2}"
                            V(lambda e, sl=sl, dg=dg, gate=gate: e.tensor_scalar(out=dg[:, :], in0=identb[:, :], scalar1=actv[:, sl:sl + 1], scalar2=gate[:, sl:sl + 1], op0=ALU.mult, op1=ALU.mult), ["identb", f"actv{g % 4}", kgate], [kdg])
                            for half in range(2):
                                T(lambda e, sl=sl, half=half, dg=dg, b=b: e.matmul(ps[acc0 + half][:, :], lhsT=dg[:, :], rhs=uvg[b][:, D + half * 512:D + (half + 1) * 512], start=(sl == 0), stop=(sl == 127)), [kdg, f"uvg{b}"], [f"ps{acc0 + half}"])

                    for g in range(128 // GS):
                        bufs = []
                        for sl in range(g * GS, (g + 1) * GS):
                            b = gctr[0] % NG
                            gctr[0] += 1
                            bufs.append(b)
                            extra = conv_keys if first_gather[0] else []
                            first_gather[0] = False
                            S.op("pool", lambda e, sl=sl, b=b, eidi=eidi: e.indirect_dma_start(out=uvg[b][:, :], out_offset=None, in_=uvb, in_offset=bass.IndirectOffsetOnAxis(ap=eidi[:, sl:sl + 1], axis=0)),
                                 reads=[keid] + extra, writes=[f"uvg{b}"], dma=True, ring=GRING)
                            pr = prd[sl % 4]; kpr = f"prd{sl % 4}"
                            V(lambda e, sl=sl, b=b, pr=pr: e.tensor_tensor(out=pr[:, :], in0=uvg[b][:, 0:D], in1=xn2[:, :], op=ALU.mult), [f"uvg{b}", kxn2], [kpr])
                            dj = djunk2[sl % 2]
                            A(lambda e, sl=sl, pr=pr, dj=dj: e.activation(out=dj[:, :], in_=pr[:, :], func=AF.Copy, accum_out=dots[:, sl:sl + 1]), [kpr], [f"djunk{sl % 2}", f"dots{g % 4}_{sl % 2}"])
                        gsl = slice(g * GS, (g + 1) * GS)
                        A(lambda e, gsl=gsl: e.activation(out=actv[:, gsl], in_=dots[:, gsl], func=AF.Gelu), [f"dots{g % 4}_0", f"dots{g % 4}_1"], [f"actv{g % 4}"])
                        if pending:
                            vside(*pending.pop(0))
                        pending.append((g, bufs))
                        if g == 1 and ti > 0:
                            epilogue(ti - 1)
                        if g == 1 and ti + 1 < ntileC:
                            phase_P1(ti + 1)
                            p2gen = phase_P2_gen(ti + 1)
                        if g >= 18 and ti + 1 < ntileC and ((g - 17) * 30) // 46 > ((g - 18) * 30) // 46:
                            next(p2gen, None)
                    while pending:
                        vside(*pending.pop(0))
                    if ti + 1 < ntileC:
                        for _ in p2gen:
                            pass

                phase_P1(0)
                phase_P2(0)
                for ti in range(ntileC):
                    tile_body(ti)
                epilogue(ntileC - 1)
                S.emit(final=True)
    return nc


def host_inputs(inp):
    f = lambda a: np.ascontiguousarray(np.asarray(a, dtype=np.float32))
    x = f(inp["x"]); meta = f(inp["meta_tokens"])
    rep = lambda v: np.ascontiguousarray(np.broadcast_to(f(v).reshape(1, -1), (128, f(v).size)))
    pm = lambda v, n: np.ascontiguousarray(f(v).reshape(n, 128).T)
    common = {}
    common["w_in"] = f(inp["w_in"][0]); common["w_glu"] = f(inp["ssm_w_glu"][0]); common["w_sso"] = f(inp["w_ssm_out"][0])
    common["w_mlo"] = f(inp["w_ml_out"][0]); common["w_out"] = f(inp["w_out"][0]); common["w_pq"] = f(inp["peer_w_q"][0])
    common["n1g"] = rep(inp["norm1_g"][0]); common["mlng"] = rep(inp["ml_norm_g"][0]); common["n2g"] = rep(inp["norm2_g"][0]); common["fng"] = rep(inp["final_norm_g"])
    def gp(a):
        return np.ascontiguousarray(f(a).reshape(16, 2, 64).transpose(1, 2, 0).reshape(128, 16))
    ldt = np.broadcast_to(f(inp["ssm_log_dt"][0])[:, None], (32, 64))
    common["s5a"] = np.ascontiguousarray(np.stack([gp(inp["ssm_a_re"][0]), gp(inp["ssm_a_im"][0]), gp(ldt)], axis=1))
    bwide = np.zeros((128, 2, 16, 128), np.float32)
    cwide = np.zeros((128, 2, 16, 128), np.float32)
    for ri, (bk, ck) in enumerate([("ssm_b_re", "ssm_c_re"), ("ssm_b_im", "ssm_c_im")]):
        B = f(inp[bk][0])
        C = f(inp[ck][0])
        for j in range(16):
            for g2 in range(2):
                g = 2 * j + g2
                gl = g % 8
                bwide[g2 * 64:(g2 + 1) * 64, ri, j, gl * 16:(gl + 1) * 16] = B[g]
                cwide[g2 * 64:(g2 + 1) * 64, ri, j, gl * 16:(gl + 1) * 16] = C[g].T
    common["s5bw"] = bwide; common["s5cw"] = cwide
    common["s5v"] = np.ascontiguousarray(np.stack([pm(inp["ssm_d"][0], 4), pm(inp["ssm_b_glu"][0], 4)], axis=1))
    cw_ = f(inp["ml_conv_w"][0])
    common["mlv"] = np.ascontiguousarray(np.stack([pm(cw_[0], 8), pm(cw_[1], 8), pm(cw_[2], 8), pm(cw_[3], 8), pm(inp["ml_conv_b"][0], 8), pm(inp["ml_skip"][0], 8)], axis=1))
    def blockdiag_chunks(w, transpose=False):
        w = f(w)
        o = np.zeros((8, 128, 128), np.float32)
        for n in range(256):
            c = n // 32; r = (n % 32) * 4
            o[c, r:r + 4, r:r + 4] = w[n].T if transpose else w[n]
        return o.transpose(1, 0, 2)
    common["mlbd"] = np.ascontiguousarray(np.stack([blockdiag_chunks(inp["ml_wq"][0]), blockdiag_chunks(inp["ml_wk"][0]), blockdiag_chunks(inp["ml_wv"][0]),
                                                    blockdiag_chunks(inp["ml_wq"][0], True), blockdiag_chunks(inp["ml_wk"][0], True)], axis=1))
    common["mlbdv"] = np.ascontiguousarray(blockdiag_chunks(inp["ml_wv"][0], True))
    common["mlwif"] = np.ascontiguousarray(f(inp["ml_w_if"][0]).reshape(24, 128, 8).transpose(1, 0, 2))
    common["mlbif"] = rep(inp["ml_b_if"][0])
    common["keysT"] = np.ascontiguousarray(f(inp["peer_sub_keys"][0]).reshape(16, 128, 128).transpose(2, 0, 1))
    common["u_tab"] = f(inp["peer_u"][0]); common["v_tab"] = f(inp["peer_v"][0])
    cst = np.zeros((128, 4, 128), np.float32)
    cst[:, 0, :] = np.eye(128, dtype=np.float32)
    m = np.triu(np.ones((64, 64), np.float32))
    cst[0:64, 1, 0:64] = m; cst[64:128, 1, 64:128] = m
    cst[:, 2, :] = np.arange(128, dtype=np.float32)[None, :]
    cst[:, 3, :] = 1.0
    common["cst"] = cst
    maps = []
    for b in range(8):
        d = dict(common)
        d["xh"] = np.ascontiguousarray(np.concatenate([meta, x[b]], axis=0))
        maps.append(d)
    return maps


def kernel(**inputs):
    nc = build()
    maps = host_inputs(inputs)
    res = run_bass_kernel_spmd(nc, maps, core_ids=list(range(8)))
    return np.stack([np.asarray(r["out"], dtype=np.float32) for r in res.results], axis=0)
```

```python
import math
import types
import numpy as np
from contextlib import ExitStack
import concourse.bass as bass
import concourse.mybir as mybir
from concourse.bass_utils import run_bass_kernel_spmd

F32 = mybir.dt.float32
BF16 = mybir.dt.bfloat16
I32 = mybir.dt.int32
U32 = mybir.dt.uint32
AF = mybir.ActivationFunctionType
ALU = mybir.AluOpType
AX = mybir.AxisListType

ENGINES = ("pe", "dve", "act", "pool", "sp")
EPOCH = 20000
N_DMA_SEMS = 24
RING_SIZES = {"cv": 6, "gq": 12}
L = 4112
NMETA = 16
D = 1024
EPS = 1e-6


class Sched:
    def __init__(self, nc, stack):
        self.nc = nc
        self.stack = stack
        self.ops = []
        self.buf = {}
        self.sem_cache = {}
        self.cnt = {e: 0 for e in ENGINES}
        self.dma_cnt = [0] * N_DMA_SEMS
        self.dma_rr = 0
        self.ring_cnt = {}
        self.ring_rr = {}
        self.emitted = 0
        self.fence_tickets = []

    def _sem(self, name):
        if name not in self.sem_cache:
            self.sem_cache[name] = self.stack.enter_context(self.nc.semaphore(name))
        return self.sem_cache[name]

    @staticmethod
    def _snap(fn):
        if fn.__closure__ is None:
            return fn
        cells = []
        for c in fn.__closure__:
            try:
                cells.append(types.CellType(c.cell_contents))
            except ValueError:
                cells.append(c)
        return types.FunctionType(fn.__code__, fn.__globals__, fn.__name__, fn.__defaults__, tuple(cells))

    def op(self, eng, fn, reads=(), writes=(), dma=False, detached=False, ring="dma"):
        fn = self._snap(fn)
        deps = {}
        for k in reads:
            st = self.buf.get(k)
            if st is not None and st["w"] is not None:
                deps[st["w"]] = True
        for k in writes:
            st = self.buf.get(k)
            if st is not None:
                if st["w"] is not None:
                    deps.setdefault(st["w"], False)
                for r_ in st["r"]:
                    deps.setdefault(r_, False)
        oid = len(self.ops)
        self.ops.append(dict(eng=eng, fn=fn, deps=deps, dma=dma, signal=dma, ticket=None, detached=detached, ring=ring))
        for k in reads:
            st = self.buf.setdefault(k, {"w": None, "r": []})
            st["r"].append(oid)
        for k in writes:
            self.buf[k] = {"w": oid, "r": []}
        return oid

    def emit(self, final=False):
        ops = self.ops
        start = self.emitted
        new = ops[start:]
        for o in new:
            live = set()
            for d, raw in o["deps"].items():
                if d < start and not (ops[d]["detached"] and ops[d]["dma"]):
                    continue
                od = ops[d]
                if od["eng"] == o["eng"] and not od["dma"] and not o["dma"] and (o["eng"] == "pe" or (RELAX and not raw)):
                    continue
                od["signal"] = True
                live.add(d)
            o["deps"] = live
        last = {}
        for o in new:
            if not o["dma"] and not o["detached"]:
                last[o["eng"]] = o
        for o in last.values():
            o["signal"] = True
        for o in new:
            if o["dma"] and o["ring"] != "dma":
                rn = o["ring"]
                cntl = self.ring_cnt.setdefault(rn, [0] * RING_SIZES[rn])
                j = self.ring_rr.get(rn, 0) % RING_SIZES[rn]
                self.ring_rr[rn] = self.ring_rr.get(rn, 0) + 1
                o["prev_dma"] = (f"{rn}{j}", cntl[j])
                cntl[j] += 16
                o["ticket"] = (f"{rn}{j}", cntl[j])
            elif o["dma"]:
                j = self.dma_rr % N_DMA_SEMS
                self.dma_rr += 1
                o["prev_dma"] = (f"dma{j}", self.dma_cnt[j])
                self.dma_cnt[j] += 16
                o["ticket"] = (f"dma{j}", self.dma_cnt[j])
            elif o["signal"]:
                e = o["eng"]
                self.cnt[e] += 1
                ep = (self.cnt[e] - 1) // EPOCH
                o["ticket"] = (f"c_{e}_{ep}", (self.cnt[e] - 1) % EPOCH + 1)
        for o in new:
            if o["ticket"] is not None:
                self._sem(o["ticket"][0])
        per_eng = {e: [] for e in ENGINES}
        for o in new:
            per_eng[o["eng"]].append(o)
        fence_in = list(self.fence_tickets)
        fence_out = [o["ticket"] for o in last.values()] + [(f"dma{j}", c) for j, c in enumerate(self.dma_cnt) if c > 0]
        fence_out += [(f"gq{j}", c) for j, c in enumerate(self.ring_cnt.get("gq", [])) if c > 0]

        def replay(engname):
            def body(eng):
                waited = {}
                for s, v in fence_in:
                    eng.wait_ge(self._sem(s), v)
                    waited[s] = max(waited.get(s, 0), v)
                for o in per_eng[engname]:
                    need = {}
                    for d in o["deps"]:
                        s, v = ops[d]["ticket"]
                        if need.get(s, 0) < v:
                            need[s] = v
                    if o["dma"]:
                        s, v = o["prev_dma"]
                        if v > 0 and need.get(s, 0) < v:
                            need[s] = v
                    for s, v in need.items():
                        if waited.get(s, 0) >= v:
                            continue
                        eng.wait_ge(self._sem(s), v)
                        waited[s] = v
                    ins = o["fn"](eng)
                    if o["signal"]:
                        s, v = o["ticket"]
                        ins.then_inc(self._sem(s), 16 if o["dma"] else 1)
                if final:
                    for s, v in fence_out:
                        if waited.get(s, 0) < v:
                            eng.wait_ge(self._sem(s), v)
            return body

        with self.nc.Block() as block:
            block.tensor(replay("pe"))
            block.vector(replay("dve"))
            block.scalar(replay("act"))
            block.gpsimd(replay("pool"))
            block.sync(replay("sp"))
        self.fence_tickets = fence_out
        self.emitted = len(ops)


GRING = "dma"
RELAX = False
TWO_PI = 2.0 * math.pi
CW1 = 6.28125
CW2 = TWO_PI - CW1


def build(passes=("A", "B", "C"), debug=False, nblkA=9, nblkB=17, ntileC=32):
    nc = bass.Bass("TRN2", target_bir_lowering=False)

    def din(name, shape, dt=F32):
        return nc.dram_tensor(name, list(shape), dt, kind="ExternalInput").ap()

    xh = din("xh", [L, D])
    w_in = din("w_in", [D, 4608])
    w_glu = din("w_glu", [512, 512])
    w_sso = din("w_sso", [512, D])
    w_mlo = din("w_mlo", [D, D])
    w_out = din("w_out", [D, D])
    w_pq = din("w_pq", [D, 2048])
    n1g = din("n1g", [128, D])
    mlng = din("mlng", [128, D])
    n2g = din("n2g", [128, D])
    fng = din("fng", [128, D])
    s5a = din("s5a", [128, 3, 16])
    s5bw = din("s5bw", [128, 2, 16, 128])
    s5cw = din("s5cw", [128, 2, 16, 128])
    s5v = din("s5v", [128, 2, 4])
    mlv = din("mlv", [128, 6, 8])
    mlbd = din("mlbd", [128, 5, 8, 128])
    mlbdv = din("mlbdv", [128, 8, 128])
    mlwif = din("mlwif", [128, 24, 8])
    mlbif = din("mlbif", [128, 8])
    keysT = din("keysT", [128, 16, 128])
    u_tab = din("u_tab", [16384, D])
    v_tab = din("v_tab", [16384, D])
    cst = din("cst", [128, 4, 128])
    kind_s = "ExternalOutput" if debug else "Internal"
    mixs = nc.dram_tensor("mixs", [D, L], BF16, kind=kind_s).ap()
    h1 = nc.dram_tensor("h1", [4096, D], F32, kind=kind_s).ap()
    out = nc.dram_tensor("out", [4096, D], F32, kind="ExternalOutput").ap()
    uvb = nc.dram_tensor("uvb", [16384, 2 * D], BF16, kind="Internal").ap()

    with ExitStack() as top:
        S = Sched(nc, top)
        V = lambda fn, r, w: S.op("dve", fn, reads=r, writes=w)
        A = lambda fn, r, w: S.op("act", fn, reads=r, writes=w)
        G = lambda fn, r, w: S.op("pool", fn, reads=r, writes=w)
        T = lambda fn, r, w: S.op("pe", fn, reads=r, writes=w)
        DM = lambda fn, r, w: S.op("sp", fn, reads=r, writes=w, dma=True)

        ps = [top.enter_context(nc.psum_tensor(f"ps{i}", [128, 512], F32)) for i in range(3)]
        psS = top.enter_context(nc.psum_tensor("psS", [128, 1024], F32))
        ps += [psS[:, 0:512], psS[:, 512:1024]]
        ps += [top.enter_context(nc.psum_tensor(f"ps{i}", [128, 512], F32)) for i in range(5, 7)]
        psb = top.enter_context(nc.psum_tensor("psb", [128, 1024], BF16))

        def sbt(stack, name, shape, dt):
            return stack.enter_context(nc.sbuf_tensor("sb_" + name, list(shape), dt))

        cstf = sbt(top, "cstf", [128, 4, 128], F32)
        identb = sbt(top, "identb", [128, 128], BF16)
        DM(lambda e: e.dma_start(out=cstf[:], in_=cst), [], ["cstf"])
        V(lambda e: e.tensor_copy(out=identb[:], in_=cstf[:, 0, :]), ["cstf"], ["identb"])
        identf = cstf[:, 0, :]
        maskT = cstf[:, 1, :]
        iota_f = cstf[:, 2, :]
        ones_f = cstf[:, 3, :]

        def load_weight(stack, name, src, rows, c0, c1, dst, dcol0, stg, tagn):
            nk = rows // 128
            w = c1 - c0
            i = 0
            for k in range(nk):
                SW = stg.shape[2]
                for s0 in range(0, w, SW):
                    s1 = min(w, s0 + SW)
                    b = load_weight.ctr % 2
                    load_weight.ctr += 1
                    DM(lambda e, b=b, k=k, s0=s0, s1=s1: e.dma_start(out=stg[:, b, 0:s1 - s0], in_=src[k * 128:(k + 1) * 128, c0 + s0:c0 + s1]),
                       [], [f"stg{b}"])
                    fn = lambda e, b=b, k=k, s0=s0, s1=s1: e.tensor_copy(out=dst[:, k, dcol0 + s0:dcol0 + s1], in_=stg[:, b, 0:s1 - s0])
                    if i % 2 == 0:
                        A(lambda e, b=b, k=k, s0=s0, s1=s1: e.activation(out=dst[:, k, dcol0 + s0:dcol0 + s1], in_=stg[:, b, 0:s1 - s0], func=AF.Copy), [f"stg{b}"], [name])
                    else:
                        V(fn, [f"stg{b}"], [name])
                    i += 1
        load_weight.ctr = 0

        def rms_rows(P, x_ap, g_ap, outs, kx, tag, scr):
            junk, ssq, rs = scr["junk"], scr["ssq"], scr["rs"]
            tag = scr["tag"]
            A(lambda e: e.activation(out=junk[:P, :], in_=x_ap, func=AF.Square, accum_out=ssq[:P, 0:1]), [kx], [tag + "junk", tag + "ssq"])
            A(lambda e: e.activation(out=rs[:P, 1:2], in_=ssq[:P, 0:1], func=AF.Ln, scale=1.0 / D, bias=EPS), [tag + "ssq"], [tag + "rs1"])
            A(lambda e: e.activation(out=rs[:P, 2:3], in_=rs[:P, 1:2], func=AF.Exp, scale=-0.5), [tag + "rs1"], [tag + "rs2"])
            for (o_ap, ko) in outs:
                V(lambda e, o_ap=o_ap: e.scalar_tensor_tensor(out=o_ap, in0=x_ap, scalar=rs[:P, 2:3], in1=g_ap, op0=ALU.mult, op1=ALU.mult),
                  [kx, tag + "rs2"], [ko])

        def transpose_rows(P, xn_bf_ap, kxn, dst_ap, kdst):
            for c in range(8):
                T(lambda e, c=c: e.transpose(out=psb[:, c * 128:c * 128 + P], in_=xn_bf_ap[:, c * 128:(c + 1) * 128], identity=identb[:P, :P]),
                  [kxn, "identb"], ["psb"])
            A(lambda e: e.activation(out=dst_ap, in_=psb[:, :].rearrange("p (c t) -> p c t", c=8)[:, :, :P], func=AF.Copy), ["psb"], [kdst])

        conv_keys = []
        if "C" in passes:
            CW = 1024
            cvf = [sbt(top, f"cvf{i}", [128, CW], F32) for i in range(2)]
            cvb = [sbt(top, f"cvb{i}", [128, CW], BF16) for i in range(2)]
            chunks = [(tsrc, half, c) for (tsrc, half) in ((u_tab, 0), (v_tab, 1)) for c in range(131072 // CW)]
            def cv_in(i):
                tsrc, half, c = chunks[i]
                S.op("pool", lambda e, i=i, tsrc=tsrc, c=c: e.dma_start(out=cvf[i % 2][:, :], in_=tsrc.rearrange("(p r) d -> p (r d)", p=128)[:, c * CW:(c + 1) * CW]),
                     reads=[], writes=[f"cvf{i % 2}"], dma=True, detached=True, ring="cv")
            cvs = {"i": 0}

            def conv_batch(n):
                for _ in range(n):
                    i = cvs["i"]
                    if i >= len(chunks):
                        return
                    if i == 0:
                        cv_in(0)
                    if i + 1 < len(chunks):
                        cv_in(i + 1)
                    tsrc, half, c = chunks[i]
                    S.op("pool", lambda e, i=i: e.tensor_copy(out=cvb[i % 2][:, :], in_=cvf[i % 2][:, :]), reads=[f"cvf{i % 2}"], writes=[f"cvb{i % 2}"], detached=True)
                    S.op("pool", lambda e, i=i, half=half, c=c: e.dma_start(out=uvb.rearrange("(p r) c -> p r c", p=128)[:, c, half * D:(half + 1) * D], in_=cvb[i % 2][:, :]),
                         reads=[f"cvb{i % 2}"], writes=[f"cvk{i}"], dma=True, detached=True, ring="cv")
                    conv_keys.append(f"cvk{i}")
                    cvs["i"] = i + 1
        else:
            def conv_batch(n):
                return

        if "A" in passes:
            with ExitStack() as sa:
                WinA = sbt(sa, "WinA", [128, 8, 1536], BF16)
                Wglu = sbt(sa, "Wglu", [128, 4, 512], BF16)
                Wsso = sbt(sa, "Wsso", [128, 4, 1024], BF16)
                n1g_t = sbt(sa, "n1gA", [128, D], F32)
                sv_ = sbt(sa, "s5v", [128, 2, 4], F32)
                sm = sbt(sa, "s5sm", [128, 24, 16], F32)
                cosT = sbt(sa, "cosT", [128, 16, 128], F32)
                sinT = sbt(sa, "sinT", [128, 16, 128], F32)
                BwT = sbt(sa, "BwT", [128, 2, 16, 128], BF16)
                CwT = sbt(sa, "CwT", [128, 2, 16, 128], BF16)
                DEC = sbt(sa, "DEC", [128, 16, 128], F32)
                sa_outer = sa
                sa = ExitStack()
                sa.__enter__()
                stg = sbt(sa, "stgA", [128, 2, 2048], F32)
                load_weight(sa, "WinA", w_in, 1024, 0, 512, WinA, 0, stg, "a")
                load_weight(sa, "WinA", w_in, 1024, 2560, 3584, WinA, 512, stg, "a")
                load_weight(sa, "Wglu", w_glu, 512, 0, 512, Wglu, 0, stg, "a")
                load_weight(sa, "Wsso", w_sso, 512, 0, 1024, Wsso, 0, stg, "a")
                DM(lambda e: e.dma_start(out=n1g_t[:], in_=n1g), [], ["n1g"])
                pa = sbt(sa, "s5a", [128, 3, 16], F32)
                bw = sbt(sa, "s5bw", [128, 2, 16, 128], F32)
                cw = sbt(sa, "s5cw", [128, 2, 16, 128], F32)
                DM(lambda e: e.dma_start(out=pa[:], in_=s5a), [], ["pa"])
                DM(lambda e: e.dma_start(out=bw[:], in_=s5bw), [], ["bw"])
                DM(lambda e: e.dma_start(out=cw[:], in_=s5cw), [], ["cw"])
                DM(lambda e: e.dma_start(out=sv_[:], in_=s5v), [], ["s5v"])
                DT_, LR, TH, RHO, STH, CTH, AR_, AI_, NR, DEN, FRE, FIM, T1, T2, CR128, SR128, CR16, SR16, TH128, TH16, FIMN = range(21)
                are = pa[:, 0, :]
                aim = pa[:, 1, :]
                A(lambda e: e.activation(out=sm[:, DT_, :], in_=pa[:, 2, :], func=AF.Exp), ["pa"], ["sm_dt"])
                V(lambda e: e.tensor_tensor(out=sm[:, LR, :], in0=are, in1=sm[:, DT_, :], op=ALU.mult), ["pa", "sm_dt"], ["sm_lr"])
                V(lambda e: e.tensor_tensor(out=sm[:, TH, :], in0=aim, in1=sm[:, DT_, :], op=ALU.mult), ["pa", "sm_dt"], ["sm_th"])
                A(lambda e: e.activation(out=sm[:, RHO, :], in_=sm[:, LR, :], func=AF.Exp), ["sm_lr"], ["sm_rho"])
                V(lambda e: e.tensor_scalar(out=sm[:, TH128, :], in0=sm[:, TH, :], scalar1=128.0, scalar2=None, op0=ALU.mult), ["sm_th"], ["sm_th128"])
                V(lambda e: e.tensor_scalar(out=sm[:, TH16, :], in0=sm[:, TH, :], scalar1=16.0, scalar2=None, op0=ALU.mult), ["sm_th"], ["sm_th16"])

                thT = sbt(sa, "thT", [128, 16, 128], F32)
                scs = [sbt(sa, f"scs{i}", [128, 2048], F32) for i in range(4)]
                sci = sbt(sa, "sci", [128, 2048], I32)

                def sincos(x_ap, kx, Fn, sin_ap, ksin, cos_ap, kcos, tag):
                    k_ = scs[0][:, :Fn]; y_ = scs[1][:, :Fn]; s1 = scs[2][:, :Fn]; s2 = scs[3][:, :Fn]; ki = sci[:, :Fn]
                    shp = list(x_ap.shape)
                    def vw(ap):
                        if len(shp) == 3:
                            return ap.rearrange("p (a b) -> p a b", a=shp[1])
                        return ap
                    V(lambda e: e.tensor_scalar(out=vw(k_), in0=x_ap, scalar1=1.0 / TWO_PI, scalar2=None, op0=ALU.mult), [kx], ["scs0"])
                    V(lambda e: e.tensor_copy(out=ki, in_=k_), ["scs0"], ["sci"])
                    V(lambda e: e.tensor_copy(out=k_, in_=ki), ["sci"], ["scs0"])
                    V(lambda e: e.scalar_tensor_tensor(out=vw(y_), in0=vw(k_), scalar=-CW1, in1=x_ap, op0=ALU.mult, op1=ALU.add), ["scs0", kx], ["scs1"])
                    V(lambda e: e.scalar_tensor_tensor(out=y_, in0=k_, scalar=-CW2, in1=y_, op0=ALU.mult, op1=ALU.add), ["scs0", "scs1"], ["scs1"])
                    A(lambda e: e.activation(out=s1, in_=y_, func=AF.Sin, scale=0.5), ["scs1"], ["scs2"])
                    A(lambda e: e.activation(out=s2, in_=y_, func=AF.Sin, scale=0.25), ["scs1"], ["scs3"])
                    V(lambda e: e.tensor_tensor(out=k_, in0=s2, in1=s2, op=ALU.mult), ["scs3"], ["scs0"])
                    V(lambda e: e.tensor_scalar(out=k_, in0=k_, scalar1=-2.0, scalar2=1.0, op0=ALU.mult, op1=ALU.add), ["scs0"], ["scs0"])
                    V(lambda e: e.scalar_tensor_tensor(out=sin_ap, in0=vw(s1), scalar=2.0, in1=vw(k_), op0=ALU.mult, op1=ALU.mult), ["scs2", "scs0"], [ksin])
                    V(lambda e: e.tensor_tensor(out=y_, in0=s1, in1=s1, op=ALU.mult), ["scs2"], ["scs1"])
                    V(lambda e: e.tensor_scalar(out=cos_ap, in0=vw(y_), scalar1=-2.0, scalar2=1.0, op0=ALU.mult, op1=ALU.add), ["scs1"], [kcos])

                sincos(sm[:, TH, :], "sm_th", 16, sm[:, STH, :], "sm_sth", sm[:, CTH, :], "sm_cth", "a")
                sincos(sm[:, TH128, :], "sm_th128", 16, sm[:, SR128, :], "sm_sr128", sm[:, CR128, :], "sm_cr128", "b")
                sincos(sm[:, TH16, :], "sm_th16", 16, sm[:, SR16, :], "sm_sr16", sm[:, CR16, :], "sm_cr16", "c")
                for j in range(16):
                    V(lambda e, j=j: e.tensor_scalar(out=thT[:, j, :], in0=iota_f, scalar1=sm[:, TH, j:j + 1], scalar2=None, op0=ALU.mult), ["cstf", "sm_th"], ["thT"])
                sincos(thT[:, :, :], "thT", 2048, sinT[:, :, :], "sinT", cosT[:, :, :], "cosT", "d")
                V(lambda e: e.tensor_tensor(out=sm[:, AR_, :], in0=sm[:, RHO, :], in1=sm[:, CTH, :], op=ALU.mult), ["sm_rho", "sm_cth"], ["sm_ar"])
                V(lambda e: e.tensor_tensor(out=sm[:, AI_, :], in0=sm[:, RHO, :], in1=sm[:, STH, :], op=ALU.mult), ["sm_rho", "sm_sth"], ["sm_ai"])
                V(lambda e: e.tensor_scalar(out=sm[:, NR, :], in0=sm[:, AR_, :], scalar1=-1.0, scalar2=None, op0=ALU.add), ["sm_ar"], ["sm_nr"])
                V(lambda e: e.tensor_tensor(out=sm[:, T1, :], in0=are, in1=are, op=ALU.mult), ["pa"], ["sm_t1"])
                V(lambda e: e.tensor_tensor(out=sm[:, T2, :], in0=aim, in1=aim, op=ALU.mult), ["pa"], ["sm_t2"])
                V(lambda e: e.tensor_tensor(out=sm[:, DEN, :], in0=sm[:, T1, :], in1=sm[:, T2, :], op=ALU.add), ["sm_t1", "sm_t2"], ["sm_den"])
                V(lambda e: e.reciprocal(out=sm[:, DEN, :], in_=sm[:, DEN, :]), ["sm_den"], ["sm_den"])
                V(lambda e: e.tensor_tensor(out=sm[:, T1, :], in0=sm[:, NR, :], in1=are, op=ALU.mult), ["sm_nr", "pa", "sm_den"], ["sm_t1"])
                V(lambda e: e.tensor_tensor(out=sm[:, T2, :], in0=sm[:, AI_, :], in1=aim, op=ALU.mult), ["sm_ai", "pa", "sm_den"], ["sm_t2"])
                V(lambda e: e.tensor_tensor(out=sm[:, FRE, :], in0=sm[:, T1, :], in1=sm[:, T2, :], op=ALU.add), ["sm_t1", "sm_t2"], ["sm_fre"])
                V(lambda e: e.tensor_tensor(out=sm[:, FRE, :], in0=sm[:, FRE, :], in1=sm[:, DEN, :], op=ALU.mult), ["sm_fre", "sm_den"], ["sm_fre"])
                V(lambda e: e.tensor_tensor(out=sm[:, T1, :], in0=sm[:, AI_, :], in1=are, op=ALU.mult), ["sm_ai", "pa", "sm_fre"], ["sm_t1"])
                V(lambda e: e.tensor_tensor(out=sm[:, T2, :], in0=sm[:, NR, :], in1=aim, op=ALU.mult), ["sm_nr", "pa", "sm_fre"], ["sm_t2"])
                V(lambda e: e.tensor_tensor(out=sm[:, FIM, :], in0=sm[:, T1, :], in1=sm[:, T2, :], op=ALU.subtract), ["sm_t1", "sm_t2"], ["sm_fim"])
                V(lambda e: e.tensor_tensor(out=sm[:, FIM, :], in0=sm[:, FIM, :], in1=sm[:, DEN, :], op=ALU.mult), ["sm_fim", "sm_den"], ["sm_fim"])
                bb = sbt(sa, "bb", [128, 3, 128], F32)
                for j in range(16):
                    V(lambda e, j=j: e.tensor_scalar(out=bb[:, 2, :], in0=bw[:, 1, j, :], scalar1=sm[:, FIM, j:j + 1], scalar2=None, op0=ALU.mult), ["bw", "sm_fim"], ["bb2"])
                    V(lambda e, j=j: e.scalar_tensor_tensor(out=bb[:, 0, :], in0=bw[:, 0, j, :], scalar=sm[:, FRE, j:j + 1], in1=bb[:, 2, :], op0=ALU.mult, op1=ALU.subtract), ["bw", "sm_fre", "bb2"], ["bb0"])
                    V(lambda e, j=j: e.tensor_scalar(out=bb[:, 2, :], in0=bw[:, 1, j, :], scalar1=sm[:, FRE, j:j + 1], scalar2=None, op0=ALU.mult), ["bw", "sm_fre", "bb0"], ["bb2"])
                    V(lambda e, j=j: e.scalar_tensor_tensor(out=bb[:, 1, :], in0=bw[:, 0, j, :], scalar=sm[:, FIM, j:j + 1], in1=bb[:, 2, :], op0=ALU.mult, op1=ALU.add), ["bw", "sm_fim", "bb2"], ["bb1"])
                    for ri in range(2):
                        T(lambda e, ri=ri: e.transpose(out=ps[0][:, ri * 128:(ri + 1) * 128], in_=bb[:, ri, :], identity=identf), [f"bb{ri}", "cstf"], ["ps0"])
                    A(lambda e, j=j: e.activation(out=BwT[:, :, j, :], in_=ps[0][:, 0:256].rearrange("p (r m) -> p r m", r=2), func=AF.Copy), ["ps0"], ["BwT"])
                V(lambda e: e.tensor_copy(out=CwT[:, 0, :, :], in_=cw[:, 0, :, :]), ["cw"], ["CwT"])
                V(lambda e: e.tensor_scalar(out=CwT[:, 1, :, :], in0=cw[:, 1, :, :], scalar1=-1.0, scalar2=None, op0=ALU.mult), ["cw"], ["CwT"])
                for j in range(16):
                    V(lambda e, j=j: e.tensor_scalar(out=DEC[:, j, :], in0=ones_f, scalar1=sm[:, RHO, j:j + 1], scalar2=None, op0=ALU.mult), ["cstf", "sm_rho"], ["DEC"])
                V(lambda e: e.memset(DEC[:, :, 0:1], 0.0), ["DEC"], ["DEC"])

                S.emit()
                sa.__exit__(None, None, None)
                sa = sa_outer
                xt = sbt(sa, "xtA", [128, 4, D], F32)
                xnb = sbt(sa, "xnbA", [128, D], BF16)
                xnT = sbt(sa, "xnTA", [128, 8, 512], BF16)
                uT = sbt(sa, "uT", [128, 4, 512], BF16)
                sgs = sbt(sa, "sgs", [128, 8, 512], BF16)
                yg = sbt(sa, "yg", [128, 4, 512], BF16)
                y2 = sbt(sa, "y2", [128, 4, 512], BF16)
                mixT = sbt(sa, "mixTA", [128, 8, 512], BF16)
                scrA = dict(tag="nA", junk=sbt(sa, "junkA", [128, D], F32), ssq=sbt(sa, "ssqA", [128, 1], F32), rs=sbt(sa, "rsA", [128, 3], F32))
                P1f = sbt(sa, "P1f", [128, 1024], F32)
                P2f = sbt(sa, "P2f", [128, 1024], F32)
                Xf = sbt(sa, "Xf", [128, 2, 4, 128], F32)
                RI = sbt(sa, "RI", [128, 2, 16], F32)
                SbB = [sbt(sa, f"SbB{i}", [128, 2, 4, 128], BF16) for i in range(2)]
                P1 = [P1f[:, i * 256:(i + 1) * 256].rearrange("p (r t) -> p r t", r=2) for i in range(2)]
                P2 = [P2f[:, i * 256:(i + 1) * 256].rearrange("p (r t) -> p r t", r=2) for i in range(2)]
                X_ = [Xf[:, :, i, :] for i in range(2)]
                R_ = sbt(sa, "R_", [128, 2, 16, 128], F32)
                Rinit = sbt(sa, "Rinit", [128, 2, 16], F32)
                rt = sbt(sa, "rt", [128, 4, 16], F32)
                Sb = [sbt(sa, f"Sb{i}", [128, 2, 128], BF16) for i in range(2)]
                yv = sbt(sa, "yv", [128, 128], F32)
                sgt = sbt(sa, "sgt", [128, 512], BF16)
                V(lambda e: e.memset(Rinit[:], 0.0), [], ["Rinit"])
                V(lambda e: e.memset(RI[:], 0.0), [], ["RI"])

                for blk in range(nblkA):
                    conv_batch(10)
                    N = 16 if blk == 0 else 512
                    tok0 = 0 if blk == 0 else 16 + 512 * (blk - 1)
                    ntile = 1 if blk == 0 else 4
                    for i in range(ntile):
                        P = min(128, N)
                        r0 = tok0 + i * 128
                        DM(lambda e, i=i, P=P, r0=r0: e.dma_start(out=xt[:P, i, :], in_=xh[r0:r0 + P, :]), [], [f"xtA{i}"])
                        rms_rows(P, xt[:P, i, :], n1g_t[:P, :], [(xnb[:P, :], "xnbA")], f"xtA{i}", "nA", scrA)
                        transpose_rows(P, xnb[:P, :], "xnbA", xnT[:, :, i * 128:i * 128 + P], "xnTA")
                    for q in range(12):
                        bank = ps[1 + q % 2]; kb = f"ps{1 + q % 2}"
                        for k in range(8):
                            T(lambda e, q=q, k=k, bank=bank: e.matmul(bank[:, :N], lhsT=WinA[:, k, q * 128:(q + 1) * 128], rhs=xnT[:, k, :N], start=(k == 0), stop=(k == 7)),
                              ["WinA", "xnTA"], [kb])
                        if q < 4:
                            A(lambda e, q=q, bank=bank: e.activation(out=uT[:, q, :N], in_=bank[:, :N], func=AF.Copy), [kb], ["uT"])
                        else:
                            A(lambda e, q=q, bank=bank: e.activation(out=sgs[:, q - 4, :N], in_=bank[:, :N], func=AF.Sigmoid), [kb], ["sgs"])
                    nsc = 1 if blk == 0 else 4
                    Ts = 16 if blk == 0 else 128
                    allR = [f"R_{j}" for j in range(16)]
                    for sc in range(nsc):
                        t0 = sc * Ts
                        if Ts == 128:
                            for q in range(4):
                                buv = psS[:, :].rearrange("p (j r t) -> p j r t", j=4, r=2)
                                for jj in range(4):
                                    j = 4 * q + jj
                                    for ri in range(2):
                                        T(lambda e, j=j, jj=jj, ri=ri, q=q: e.matmul(psS[:, jj * 256 + ri * 128:jj * 256 + ri * 128 + 128], lhsT=BwT[:, ri, j, :], rhs=uT[:, q, t0:t0 + 128], start=True, stop=True),
                                          ["BwT", "uT"], ["ps3", "ps4"])
                                cosb = cosT[:, 4 * q:4 * q + 4, :].unsqueeze(2).to_broadcast([128, 4, 2, 128])
                                sinb = sinT[:, 4 * q:4 * q + 4, :].unsqueeze(2).to_broadcast([128, 4, 2, 128])
                                P1v = P1f[:, :].rearrange("p (j r t) -> p j r t", j=4, r=2)
                                P2v = P2f[:, :].rearrange("p (j r t) -> p j r t", j=4, r=2)
                                V(lambda e, buv=buv, cosb=cosb, P1v=P1v: e.tensor_tensor(out=P1v, in0=buv, in1=cosb, op=ALU.mult), ["ps3", "ps4", "cosT"], ["P1"])
                                V(lambda e, buv=buv, sinb=sinb, P2v=P2v: e.tensor_tensor(out=P2v, in0=buv, in1=sinb, op=ALU.mult), ["ps3", "ps4", "sinT"], ["P2"])
                                V(lambda e, P1v=P1v, P2v=P2v: e.tensor_tensor(out=Xf[:, 0, :, :], in0=P1v[:, :, 0, :], in1=P2v[:, :, 1, :], op=ALU.add), ["P1", "P2"], ["X"])
                                V(lambda e, P1v=P1v, P2v=P2v: e.tensor_tensor(out=Xf[:, 1, :, :], in0=P1v[:, :, 1, :], in1=P2v[:, :, 0, :], op=ALU.subtract), ["P1", "P2"], ["X"])
                                V(lambda e, q=q: e.tensor_tensor(out=Xf[:, :, :, 0], in0=Xf[:, :, :, 0], in1=RI[:, :, 4 * q:4 * q + 4], op=ALU.add), ["X", "RI"], ["X"])
                                for ri in range(2):
                                    V(lambda e, ri=ri, q=q: e.tensor_tensor_scan(out=R_[:, ri, 4 * q:4 * q + 4, :].rearrange("p j t -> p (j t)"), data0=DEC[:, 4 * q:4 * q + 4, :].rearrange("p j t -> p (j t)"),
                                                                                data1=Xf[:, ri, :, :].rearrange("p j t -> p (j t)"), initial=0.0, op0=ALU.mult, op1=ALU.add),
                                      ["DEC", "X"], [f"R_{4 * q + jj}" for jj in range(4)])
                                Rv = R_[:, :, 4 * q:4 * q + 4, :]
                                cosb2 = cosT[:, 4 * q:4 * q + 4, :].unsqueeze(1).to_broadcast([128, 2, 4, 128])
                                sinb2 = sinT[:, 4 * q:4 * q + 4, :].unsqueeze(1).to_broadcast([128, 2, 4, 128])
                                Q1v = P1f[:, :].rearrange("p (r j t) -> p r j t", r=2, j=4)
                                Q2v = P2f[:, :].rearrange("p (r j t) -> p r j t", r=2, j=4)
                                kR = [f"R_{4 * q + jj}" for jj in range(4)]
                                V(lambda e, Rv=Rv, cosb2=cosb2, Q1v=Q1v: e.tensor_tensor(out=Q1v, in0=Rv, in1=cosb2, op=ALU.mult), kR + ["cosT"], ["P1"])
                                V(lambda e, Rv=Rv, sinb2=sinb2, Q2v=Q2v: e.tensor_tensor(out=Q2v, in0=Rv, in1=sinb2, op=ALU.mult), kR + ["sinT"], ["P2"])
                                sbq = SbB[q % 2]; ksb = f"SbB{q % 2}"
                                V(lambda e, Q1v=Q1v, Q2v=Q2v, sbq=sbq: e.tensor_tensor(out=sbq[:, 0, :, :], in0=Q1v[:, 0, :, :], in1=Q2v[:, 1, :, :], op=ALU.subtract), ["P1", "P2"], [ksb])
                                V(lambda e, Q1v=Q1v, Q2v=Q2v, sbq=sbq: e.tensor_tensor(out=sbq[:, 1, :, :], in0=Q2v[:, 0, :, :], in1=Q1v[:, 1, :, :], op=ALU.add), ["P1", "P2"], [ksb])
                                for jj in range(4):
                                    j = 4 * q + jj
                                    for ri in range(2):
                                        T(lambda e, j=j, jj=jj, ri=ri, q=q, sbq=sbq: e.matmul(ps[5][:, q * 128:(q + 1) * 128], lhsT=CwT[:, ri, j, :], rhs=sbq[:, ri, jj, :], start=(jj == 0 and ri == 0), stop=(jj == 3 and ri == 1)),
                                          ["CwT", ksb], ["ps5"])
                                V(lambda e, q=q: e.scalar_tensor_tensor(out=yv[:, :], in0=uT[:, q, t0:t0 + 128], scalar=sv_[:, 0, q:q + 1], in1=ps[5][:, q * 128:(q + 1) * 128], op0=ALU.mult, op1=ALU.add),
                                  ["uT", "s5v", "ps5"], ["yv"])
                                A(lambda e, q=q: e.activation(out=yg[:, q, t0:t0 + 128], in_=yv[:, :], func=AF.Gelu), ["yv"], ["yg"])
                        else:
                            for j in range(16):
                                q = j // 4
                                pb = j % 2
                                bank = ps[3 + pb]; kb = f"ps{3 + pb}"
                                buv = bank[:, 0:256].rearrange("p (r t) -> p r t", r=2)
                                T(lambda e, j=j, q=q, bank=bank: e.matmul(bank[:, 0:Ts], lhsT=BwT[:, 0, j, :], rhs=uT[:, q, t0:t0 + Ts], start=True, stop=True), ["BwT", "uT"], [kb])
                                T(lambda e, j=j, q=q, bank=bank: e.matmul(bank[:, 128:128 + Ts], lhsT=BwT[:, 1, j, :], rhs=uT[:, q, t0:t0 + Ts], start=True, stop=True), ["BwT", "uT"], [kb])
                                cosb = cosT[:, j, :Ts].unsqueeze(1).to_broadcast([128, 2, Ts])
                                sinb = sinT[:, j, :Ts].unsqueeze(1).to_broadcast([128, 2, Ts])
                                V(lambda e, buv=buv, pb=pb, cosb=cosb: e.tensor_tensor(out=P1[pb][:, :, :Ts], in0=buv[:, :, :Ts], in1=cosb, op=ALU.mult), [kb, "cosT"], ["P1"])
                                V(lambda e, buv=buv, pb=pb, sinb=sinb: e.tensor_tensor(out=P2[pb][:, :, :Ts], in0=buv[:, :, :Ts], in1=sinb, op=ALU.mult), [kb, "sinT"], ["P2"])
                                V(lambda e, pb=pb: e.tensor_tensor(out=X_[pb][:, 0, :Ts], in0=P1[pb][:, 0, :Ts], in1=P2[pb][:, 1, :Ts], op=ALU.add), ["P1", "P2"], ["X"])
                                V(lambda e, pb=pb: e.tensor_tensor(out=X_[pb][:, 1, :Ts], in0=P1[pb][:, 1, :Ts], in1=P2[pb][:, 0, :Ts], op=ALU.subtract), ["P1", "P2"], ["X"])
                                for ri in range(2):
                                    V(lambda e, pb=pb, ri=ri, j=j: e.tensor_tensor_scan(out=R_[:, ri, j, :Ts], data0=sm[:, RHO, j:j + 1].to_broadcast([128, Ts]), data1=X_[pb][:, ri, :Ts],
                                                                                       initial=Rinit[:, ri, j:j + 1], op0=ALU.mult, op1=ALU.add),
                                      ["sm_rho", "X", "Rinit"], [f"R_{j}"])
                                V(lambda e, pb=pb, j=j, cosb=cosb: e.tensor_tensor(out=P1[pb][:, :, :Ts], in0=R_[:, :, j, :Ts], in1=cosb, op=ALU.mult), [f"R_{j}", "cosT"], ["P1"])
                                V(lambda e, pb=pb, j=j, sinb=sinb: e.tensor_tensor(out=P2[pb][:, :, :Ts], in0=R_[:, :, j, :Ts], in1=sinb, op=ALU.mult), [f"R_{j}", "sinT"], ["P2"])
                                V(lambda e, pb=pb: e.tensor_tensor(out=Sb[pb][:, 0, :Ts], in0=P1[pb][:, 0, :Ts], in1=P2[pb][:, 1, :Ts], op=ALU.subtract), ["P1", "P2"], [f"Sb{pb}"])
                                V(lambda e, pb=pb: e.tensor_tensor(out=Sb[pb][:, 1, :Ts], in0=P2[pb][:, 0, :Ts], in1=P1[pb][:, 1, :Ts], op=ALU.add), ["P1", "P2"], [f"Sb{pb}"])
                                T(lambda e, j=j, q=q, pb=pb: e.matmul(ps[5][:, q * 128:q * 128 + Ts], lhsT=CwT[:, 0, j, :], rhs=Sb[pb][:, 0, :Ts], start=(j % 4 == 0), stop=False), ["CwT", f"Sb{pb}"], ["ps5"])
                                T(lambda e, j=j, q=q, pb=pb: e.matmul(ps[5][:, q * 128:q * 128 + Ts], lhsT=CwT[:, 1, j, :], rhs=Sb[pb][:, 1, :Ts], start=False, stop=(j % 4 == 3)), ["CwT", f"Sb{pb}"], ["ps5"])
                                if j % 4 == 3:
                                    V(lambda e, q=q: e.scalar_tensor_tensor(out=yv[:, :Ts], in0=uT[:, q, t0:t0 + Ts], scalar=sv_[:, 0, q:q + 1], in1=ps[5][:, q * 128:q * 128 + Ts], op0=ALU.mult, op1=ALU.add),
                                      ["uT", "s5v", "ps5"], ["yv"])
                                    A(lambda e, q=q: e.activation(out=yg[:, q, t0:t0 + Ts], in_=yv[:, :Ts], func=AF.Gelu), ["yv"], ["yg"])
                        cR = sm[:, CR16 if Ts == 16 else CR128, :]
                        sR = sm[:, SR16 if Ts == 16 else SR128, :]
                        kc = ["sm_cr16", "sm_sr16"] if Ts == 16 else ["sm_cr128", "sm_sr128"]
                        lre = R_[:, 0, :, Ts - 1]
                        lim = R_[:, 1, :, Ts - 1]
                        V(lambda e, cR=cR, lre=lre: e.tensor_tensor(out=rt[:, 0, :], in0=cR, in1=lre, op=ALU.mult), allR + kc, ["rt0"])
                        V(lambda e, sR=sR, lim=lim: e.tensor_tensor(out=rt[:, 1, :], in0=sR, in1=lim, op=ALU.mult), allR + kc, ["rt1"])
                        V(lambda e, sR=sR, lre=lre: e.tensor_tensor(out=rt[:, 2, :], in0=sR, in1=lre, op=ALU.mult), allR + kc, ["rt2"])
                        V(lambda e, cR=cR, lim=lim: e.tensor_tensor(out=rt[:, 3, :], in0=cR, in1=lim, op=ALU.mult), allR + kc, ["rt3"])
                        V(lambda e: e.tensor_tensor(out=Rinit[:, 0, :], in0=rt[:, 0, :], in1=rt[:, 1, :], op=ALU.subtract), ["rt0", "rt1"], ["Rinit"])
                        V(lambda e: e.tensor_tensor(out=Rinit[:, 1, :], in0=rt[:, 2, :], in1=rt[:, 3, :], op=ALU.add), ["rt2", "rt3"], ["Rinit"])
                        V(lambda e: e.tensor_tensor(out=RI[:, :, :], in0=Rinit[:, :, :], in1=sm[:, RHO, :].unsqueeze(1).to_broadcast([128, 2, 16]), op=ALU.mult), ["Rinit", "sm_rho"], ["RI"])
                    for qo in range(4):
                        bank = ps[1 + qo % 2]; kb = f"ps{1 + qo % 2}"
                        for q in range(4):
                            T(lambda e, q=q, qo=qo, bank=bank: e.matmul(bank[:, :N], lhsT=Wglu[:, q, qo * 128:(qo + 1) * 128], rhs=yg[:, q, :N], start=(q == 0), stop=(q == 3)), ["Wglu", "yg"], [kb])
                        A(lambda e, qo=qo, bank=bank: e.activation(out=sgt[:, :N], in_=bank[:, :N], func=AF.Sigmoid, bias=sv_[:, 1, qo:qo + 1]), [kb, "s5v"], ["sgt"])
                        V(lambda e, qo=qo: e.tensor_tensor(out=y2[:, qo, :N], in0=yg[:, qo, :N], in1=sgt[:, :N], op=ALU.mult), ["yg", "sgt"], ["y2"])
                    for o in range(8):
                        bank = ps[1 + o % 2]; kb = f"ps{1 + o % 2}"
                        for q in range(4):
                            T(lambda e, q=q, o=o, bank=bank: e.matmul(bank[:, :N], lhsT=Wsso[:, q, o * 128:(o + 1) * 128], rhs=y2[:, q, :N], start=(q == 0), stop=(q == 3)), ["Wsso", "y2"], [kb])
                        V(lambda e, o=o, bank=bank: e.tensor_tensor(out=mixT[:, o, :N], in0=bank[:, :N], in1=sgs[:, o, :N], op=ALU.mult), [kb, "sgs"], ["mixTA"])
                    DM(lambda e, tok0=tok0, N=N: e.dma_start(out=mixs.rearrange("(o p) t -> p o t", p=128)[:, :, tok0:tok0 + N], in_=mixT[:, :, :N]), ["mixTA"], ["mixs"])
                S.emit(final=(passes[-1] == "A"))

        if "B" in passes:
            with ExitStack() as sbk:
                WinB = sbt(sbk, "WinB", [128, 8, 3072], BF16)
                Wmlo = sbt(sbk, "Wmlo", [128, 8, 1024], BF16)
                Wout = sbt(sbk, "Wout", [128, 8, 1024], BF16)
                n1g_t = sbt(sbk, "n1gB", [128, D], F32)
                mlng_t = sbt(sbk, "mlngB", [128, D], F32)
                mlv_t = sbt(sbk, "mlv", [128, 6, 8], F32)
                bif_t = sbt(sbk, "bif", [128, 8], F32)
                bd = sbt(sbk, "bd", [128, 3, 8, 128], BF16)
                G12 = sbt(sbk, "G12", [128, 2, 8, 8], BF16)
                sbk_outer = sbk
                sbk = ExitStack()
                sbk.__enter__()
                stg = sbt(sbk, "stgB", [128, 2, 1024], F32)
                load_weight(sbk, "WinB", w_in, 1024, 512, 2560, WinB, 0, stg, "b")
                load_weight(sbk, "WinB", w_in, 1024, 3584, 4608, WinB, 2048, stg, "b")
                load_weight(sbk, "Wmlo", w_mlo, 1024, 0, 1024, Wmlo, 0, stg, "b")
                load_weight(sbk, "Wout", w_out, 1024, 0, 1024, Wout, 0, stg, "b")
                DM(lambda e: e.dma_start(out=n1g_t[:], in_=n1g), [], ["n1gB"])
                DM(lambda e: e.dma_start(out=mlng_t[:], in_=mlng), [], ["mlngB"])
                bdf = sbt(sbk, "bdf", [128, 5, 8, 128], F32)
                bdvf = sbt(sbk, "bdvf", [128, 8, 128], F32)
                wiff = sbt(sbk, "wiff", [128, 24, 8], F32)
                DM(lambda e: e.dma_start(out=mlv_t[:], in_=mlv), [], ["mlv"])
                DM(lambda e: e.dma_start(out=bdf[:], in_=mlbd), [], ["bdf"])
                DM(lambda e: e.dma_start(out=bdvf[:], in_=mlbdv), [], ["bdvf"])
                DM(lambda e: e.dma_start(out=wiff[:], in_=mlwif), [], ["wiff"])
                DM(lambda e: e.dma_start(out=bif_t[:], in_=mlbif), [], ["bif"])
                V(lambda e: e.tensor_copy(out=bd[:], in_=bdf[:, 0:3, :, :]), ["bdf"], ["bd"])
                for c in range(8):
                    T(lambda e, c=c: e.matmul(ps[0][:, c * 8:(c + 1) * 8], lhsT=bdf[:, 3, c, :], rhs=wiff[:, c, :], start=True, stop=False), ["bdf", "wiff"], ["ps0"])
                    T(lambda e, c=c: e.matmul(ps[0][:, c * 8:(c + 1) * 8], lhsT=bdf[:, 4, c, :], rhs=wiff[:, 8 + c, :], start=False, stop=True), ["bdf", "wiff"], ["ps0"])
                    T(lambda e, c=c: e.matmul(ps[0][:, 64 + c * 8:64 + (c + 1) * 8], lhsT=bdvf[:, c, :], rhs=wiff[:, 16 + c, :], start=True, stop=True), ["bdvf", "wiff"], ["ps0"])
                V(lambda e: e.tensor_copy(out=G12[:], in_=ps[0][:, 0:128].rearrange("p (a c n) -> p a c n", a=2, c=8)), ["ps0"], ["G12"])
                tri = cstf[:, 1, :]

                S.emit()
                sbk.__exit__(None, None, None)
                sbk = sbk_outer
                NB = 256
                xt = sbt(sbk, "xtB", [128, 2, D], F32)
                xnb = sbt(sbk, "xnbB", [128, D], BF16)
                xnT = sbt(sbk, "xnTB", [128, 8, NB], BF16)
                xmT = sbt(sbk, "xmT", [128, 8, 3 + NB], BF16)
                xcT = sbt(sbk, "xcT", [128, 8, NB], BF16)
                acc = sbt(sbk, "accB", [128, NB], F32)
                sgm = sbt(sbk, "sgm", [128, 8, NB], BF16)
                qT = sbt(sbk, "qT", [128, 8, NB], BF16)
                kT = sbt(sbk, "kT", [128, 8, NB], BF16)
                mixin = sbt(sbk, "mixin", [128, 8, NB], BF16)
                hmT = sbt(sbk, "hmT", [128, 8, NB], BF16)
                mixT = sbt(sbk, "mixTB", [128, 8, NB], BF16)
                tmpf = sbt(sbk, "tmpfB", [128, NB], F32)
                h1t = sbt(sbk, "h1tB", [128, D], F32)
                scrB = dict(tag="nB", junk=sbt(sbk, "junkB", [128, D], F32), ssq=sbt(sbk, "ssqB", [128, 1], F32), rs=sbt(sbk, "rsB", [128, 3], F32))
                sigz2 = [sbt(sbk, f"sigz{i}", [64, D], BF16) for i in range(2)]
                vaug2 = [sbt(sbk, f"vaug{i}", [64, 4, 257], BF16) for i in range(2)]
                vw2 = [sbt(sbk, f"vw{i}", [64, 4, 257], BF16) for i in range(2)]
                kk2 = [sbt(sbk, f"kk{i}", [64, D], BF16) for i in range(2)]
                gt2 = [sbt(sbk, f"gt{i}", [64, 8], F32) for i in range(2)]
                gs2 = [sbt(sbk, f"gs{i}", [128, 8, 4], F32) for i in range(2)]
                E1, NLF, EG, EB, TMP, KSC, WST, ENB = range(8)
                SpT2 = [sbt(sbk, f"SpT{i}", [64, 64], BF16) for i in range(2)]
                ho2 = [sbt(sbk, f"ho{i}", [64, 256], F32) for i in range(2)]
                hjunk = sbt(sbk, "hjunk", [64, 256], F32)
                hs2 = [sbt(sbk, f"hs{i}", [64, 8], F32) for i in range(2)]
                hn = sbt(sbk, "hn", [64, D], BF16)
                CT = sbt(sbk, "CT", [128, 8, 257], F32)
                CTb = sbt(sbk, "CTb", [128, 8, 257], BF16)
                V(lambda e: e.memset(CT[:], 0.0), [], [f"CT{h_}" for h_ in range(4)])
                V(lambda e: e.memset(CTb[:], 0.0), [], [f"CTb{h_}" for h_ in range(4)])
                V(lambda e: e.memset(xmT[:], 0.0), [], ["xmT"])
                for i_ in range(2):
                    V(lambda e, i_=i_: e.memset(vaug2[i_][:], 1.0), [], [f"vaug{i_}"])
                    V(lambda e, i_=i_: e.memset(gs2[i_][:], 0.0), [], [f"gs{i_}"])
                LN16 = math.log(16.0)

                for blk in range(nblkB):
                    conv_batch(10)
                    N = 16 if blk == 0 else NB
                    tok0 = 0 if blk == 0 else 16 + NB * (blk - 1)
                    ntile = 1 if blk == 0 else 2
                    for i in range(ntile):
                        P = min(128, N)
                        r0 = tok0 + i * 128
                        DM(lambda e, i=i, P=P, r0=r0: e.dma_start(out=xt[:P, i, :], in_=xh[r0:r0 + P, :]), [], [f"xtB{i}"])
                        rms_rows(P, xt[:P, i, :], n1g_t[:P, :], [(xnb[:P, :], "xnbB")], f"xtB{i}", "nB", scrB)
                        transpose_rows(P, xnb[:P, :], "xnbB", xnT[:, :, i * 128:i * 128 + P], "xnTB")
                    DM(lambda e, tok0=tok0, N=N: e.dma_start(out=mixin[:, :, :N], in_=mixs.rearrange("(o p) t -> p o t", p=128)[:, :, tok0:tok0 + N]), ["mixs"], ["mixin"])
                    for c in range(16):
                        bank = ps[1 + c % 2]; kb = f"ps{1 + c % 2}"
                        col0 = c * 128 if c < 8 else 2048 + (c - 8) * 128
                        for k in range(8):
                            T(lambda e, k=k, col0=col0, bank=bank: e.matmul(bank[:, :N], lhsT=WinB[:, k, col0:col0 + 128], rhs=xnT[:, k, :N], start=(k == 0), stop=(k == 7)), ["WinB", "xnTB"], [kb])
                        if c < 8:
                            A(lambda e, c=c, bank=bank: e.activation(out=xmT[:, c, 3:3 + N], in_=bank[:, :N], func=AF.Copy), [kb], ["xmT"])
                        else:
                            A(lambda e, c=c, bank=bank: e.activation(out=sgm[:, c - 8, :N], in_=bank[:, :N], func=AF.Sigmoid), [kb], ["sgm"])
                    for c in range(8):
                        V(lambda e, c=c: e.tensor_scalar(out=acc[:, :N], in0=xmT[:, c, 0:N], scalar1=mlv_t[:, 0, c:c + 1], scalar2=None, op0=ALU.mult), ["xmT", "mlv"], ["accB"])
                        for j in range(1, 4):
                            V(lambda e, c=c, j=j: e.scalar_tensor_tensor(out=acc[:, :N], in0=xmT[:, c, j:j + N], scalar=mlv_t[:, j, c:c + 1], in1=acc[:, :N], op0=ALU.mult, op1=ALU.add), ["xmT", "mlv", "accB"], ["accB"])
                        A(lambda e, c=c: e.activation(out=xcT[:, c, :N], in_=acc[:, :N], func=AF.Silu, bias=mlv_t[:, 4, c:c + 1]), ["accB", "mlv"], ["xcT"])
                    for c in range(16):
                        bank = ps[1 + c % 2]; kb = f"ps{1 + c % 2}"
                        w = c // 8; cc = c % 8
                        T(lambda e, w=w, cc=cc, bank=bank: e.matmul(bank[:, :N], lhsT=bd[:, w, cc, :], rhs=xcT[:, cc, :N], start=True, stop=True), ["bd", "xcT"], [kb])
                        dst = qT if w == 0 else kT
                        A(lambda e, cc=cc, bank=bank, dst=dst: e.activation(out=dst[:, cc, :N], in_=bank[:, :N], func=AF.Copy), [kb], ["qT" if w == 0 else "kT"])
                    Lc = 16 if blk == 0 else 64

                    def chunk_pre(ch):
                        pp = ch % 2
                        sigzP, vaugP, vw_P, kkP, gtP, gsP = sigz2[pp], vaug2[pp], vw2[pp], kk2[pp], gt2[pp], gs2[pp]
                        o0 = ch * Lc
                        for half in range(2):
                            bank = ps[1 + half]; kb = f"ps{1 + half}"
                            for k in range(8):
                                T(lambda e, k=k, half=half, bank=bank: e.matmul(bank[:Lc, :], lhsT=xnT[:, k, o0:o0 + Lc], rhs=WinB[:, k, 1024 + half * 512:1024 + (half + 1) * 512], start=(k == 0), stop=(k == 7)), ["xnTB", "WinB"], [kb])
                            A(lambda e, half=half, bank=bank: e.activation(out=sigzP[:Lc, half * 512:(half + 1) * 512], in_=bank[:Lc, :], func=AF.Sigmoid), [kb], [f"sigz{pp}"])
                        for c in range(8):
                            T(lambda e, c=c: e.matmul(ps[0][:Lc, 0:8], lhsT=xcT[:, c, o0:o0 + Lc], rhs=G12[:, 0, c, :], start=(c == 0), stop=False), ["xcT", "G12"], ["ps0"])
                        for c in range(8):
                            T(lambda e, c=c: e.matmul(ps[0][:Lc, 0:8], lhsT=xmT[:, c, 3 + o0:3 + o0 + Lc], rhs=G12[:, 1, c, :], start=False, stop=(c == 7)), ["xmT", "G12"], ["ps0"])
                        V(lambda e: e.tensor_tensor(out=gtP[:Lc, :], in0=ps[0][:Lc, 0:8], in1=bif_t[:Lc, :], op=ALU.add), ["ps0", "bif"], [f"gt{pp}"])
                        A(lambda e: e.activation(out=gsP[:Lc, E1, :], in_=gtP[:Lc, 4:8], func=AF.Exp, scale=-1.0), [f"gt{pp}"], [f"gs_e1{pp}"])
                        A(lambda e: e.activation(out=gsP[:Lc, NLF, :], in_=gsP[:Lc, E1, :], func=AF.Ln, bias=1.0), [f"gs_e1{pp}"], [f"gs_nlf{pp}"])
                        T(lambda e: e.matmul(ps[0][:Lc, 8:12], lhsT=tri[:Lc, :Lc], rhs=gsP[:Lc, NLF, :], start=True, stop=True), ["cstf", f"gs_nlf{pp}"], ["ps0"])
                        T(lambda e: e.matmul(ps[0][:, 12:16], lhsT=ones_f[:Lc, :], rhs=gsP[:Lc, NLF, :], start=True, stop=True), ["cstf", f"gs_nlf{pp}"], ["ps0"])
                        A(lambda e: e.activation(out=gsP[:, EG, :], in_=ps[0][:, 12:16], func=AF.Exp, scale=-1.0), ["ps0"], [f"gs_eg{pp}"])
                        A(lambda e: e.activation(out=gsP[:Lc, ENB, :], in_=ps[0][:Lc, 8:12], func=AF.Exp), ["ps0"], [f"gs_enb{pp}"])
                        V(lambda e: e.tensor_tensor(out=gsP[:Lc, TMP, :], in0=ps[0][:Lc, 8:12], in1=gtP[:Lc, 0:4], op=ALU.add), ["ps0", f"gt{pp}"], [f"gs_tmp{pp}"])
                        A(lambda e: e.activation(out=gsP[:Lc, KSC, :], in_=gsP[:Lc, TMP, :], func=AF.Exp, bias=-LN16), [f"gs_tmp{pp}"], [f"gs_ksc{pp}"])
                        V(lambda e: e.tensor_tensor(out=gsP[:Lc, WST, :], in0=gsP[:Lc, KSC, :], in1=gsP[:Lc, EG, :], op=ALU.mult), [f"gs_ksc{pp}", f"gs_eg{pp}"], [f"gs_wst{pp}"])
                        for half in range(2):
                            bank = ps[1 + half]; kb = f"ps{1 + half}"
                            for c4 in range(4):
                                c = half * 4 + c4
                                T(lambda e, c=c, c4=c4, bank=bank: e.matmul(bank[:Lc, c4 * 128:(c4 + 1) * 128], lhsT=xmT[:, c, 3 + o0:3 + o0 + Lc], rhs=bd[:, 2, c, :], start=True, stop=True), ["xmT", "bd"], [kb])
                            A(lambda e, half=half, bank=bank: e.activation(out=vaugP[:Lc, half * 2:half * 2 + 2, 0:256], in_=bank[:Lc, :].rearrange("p (h d) -> p h d", h=2), func=AF.Copy), [kb], [f"vaug{pp}"])
                        for h in range(4):
                            V(lambda e, h=h: e.tensor_scalar(out=vw_P[:Lc, h, :], in0=vaugP[:Lc, h, :], scalar1=gsP[:Lc, WST, h:h + 1], scalar2=None, op0=ALU.mult), [f"vaug{pp}", f"gs_wst{pp}"], [f"vw{pp}"])
                        for half in range(2):
                            bank = ps[1 + half]; kb = f"ps{1 + half}"
                            for c4 in range(4):
                                c = half * 4 + c4
                                T(lambda e, c=c, c4=c4, bank=bank: e.matmul(bank[:Lc, c4 * 128:(c4 + 1) * 128], lhsT=xcT[:, c, o0:o0 + Lc], rhs=bd[:, 1, c, :], start=True, stop=True), ["xcT", "bd"], [kb])
                            A(lambda e, half=half, bank=bank: e.activation(out=kkP[:Lc, half * 512:(half + 1) * 512], in_=bank[:Lc, :], func=AF.Copy), [kb], [f"kk{pp}"])

                    def chunk_heads(ch):
                        pp = ch % 2
                        o0 = ch * Lc
                        sigzP, vaugP, vw_P, kkP, gtP, gsP = sigz2[pp], vaug2[pp], vw2[pp], kk2[pp], gt2[pp], gs2[pp]
                        def hs1_(h):
                            c0 = 2 * h
                            SpTh, hsh, hoh = SpT2[h % 2], hs2[h % 2], ho2[h % 2]
                            T(lambda e, c0=c0: e.matmul(ps[3][:Lc, :Lc], lhsT=kT[:, c0, o0:o0 + Lc], rhs=qT[:, c0, o0:o0 + Lc], start=True, stop=False), ["kT", "qT"], ["ps3"])
                            T(lambda e, c0=c0: e.matmul(ps[3][:Lc, :Lc], lhsT=kT[:, c0 + 1, o0:o0 + Lc], rhs=qT[:, c0 + 1, o0:o0 + Lc], start=False, stop=True), ["kT", "qT"], ["ps3"])
                            V(lambda e, h=h: e.scalar_tensor_tensor(out=SpTh[:Lc, :Lc], in0=ps[3][:Lc, :Lc], scalar=gsP[:Lc, KSC, h:h + 1], in1=maskT[:Lc, :Lc], op0=ALU.mult, op1=ALU.mult), ["ps3", f"gs_ksc{pp}", "cstf"], [f"SpT{h % 2}"])
                            T(lambda e, h=h: e.matmul(ps[4][:Lc, 0:257], lhsT=SpTh[:Lc, :Lc], rhs=vaugP[:Lc, h, :], start=True, stop=False), [f"SpT{h % 2}", f"vaug{pp}"], ["ps4"])
                            T(lambda e, h=h, c0=c0: e.matmul(ps[4][:Lc, 0:257], lhsT=qT[:, c0, o0:o0 + Lc], rhs=CTb[:, c0, :], start=False, stop=False), ["qT", f"CTb{h}"], ["ps4"])
                            T(lambda e, h=h, c0=c0: e.matmul(ps[4][:Lc, 0:257], lhsT=qT[:, c0 + 1, o0:o0 + Lc], rhs=CTb[:, c0 + 1, :], start=False, stop=True), ["qT", f"CTb{h}"], ["ps4"])
                        def hs2_(h):
                            c0 = 2 * h
                            SpTh, hsh, hoh = SpT2[h % 2], hs2[h % 2], ho2[h % 2]
                            V(lambda e, h=h: e.tensor_tensor(out=hsh[:Lc, 0:1], in0=ps[4][:Lc, 256:257], in1=gsP[:Lc, ENB, h:h + 1], op=ALU.max), ["ps4", f"gs_enb{pp}"], [f"hs0_{h % 2}"])
                            V(lambda e, h=h: e.scalar_tensor_tensor(out=hsh[:Lc, 1:2], in0=ps[4][:Lc, 256:257], scalar=-1.0, in1=hsh[:Lc, 0:1], op0=ALU.mult, op1=ALU.max), ["ps4", f"hs0_{h % 2}"], [f"hs1_{h % 2}"])
                            V(lambda e: e.reciprocal(out=hsh[:Lc, 3:4], in_=hsh[:Lc, 1:2]), [f"hs1_{h % 2}"], [f"hs3_{h % 2}"])
                            V(lambda e, h=h: e.scalar_tensor_tensor(out=hoh[:Lc, :], in0=ps[4][:Lc, 0:256], scalar=hsh[:Lc, 3:4], in1=sigzP[:Lc, h * 256:(h + 1) * 256], op0=ALU.mult, op1=ALU.mult), ["ps4", f"hs3_{h % 2}", f"sigz{pp}"], [f"ho{h % 2}"])
                            A(lambda e: e.activation(out=hjunk[:Lc, :], in_=hoh[:Lc, :], func=AF.Square, accum_out=hsh[:Lc, 4:5]), [f"ho{h % 2}"], ["hjunk", f"hs4_{h % 2}"])
                            A(lambda e: e.activation(out=hsh[:Lc, 6:7], in_=hsh[:Lc, 4:5], func=AF.Ln, scale=1.0 / 256.0, bias=EPS), [f"hs4_{h % 2}"], [f"hs6_{h % 2}"])
                            A(lambda e: e.activation(out=hsh[:Lc, 7:8], in_=hsh[:Lc, 6:7], func=AF.Exp, scale=-0.5), [f"hs6_{h % 2}"], [f"hs7_{h % 2}"])
                        def hs3_(h):
                            c0 = 2 * h
                            SpTh, hsh, hoh = SpT2[h % 2], hs2[h % 2], ho2[h % 2]
                            V(lambda e, h=h: e.scalar_tensor_tensor(out=hn[:Lc, h * 256:(h + 1) * 256], in0=hoh[:Lc, :], scalar=hsh[:Lc, 7:8], in1=mlng_t[:Lc, h * 256:(h + 1) * 256], op0=ALU.mult, op1=ALU.mult), [f"ho{h % 2}", f"hs7_{h % 2}", "mlngB"], ["hn"])
                            for dc in range(2):
                                bank = ps[5 + dc]; kb = f"ps{5 + dc}"
                                T(lambda e, h=h, dc=dc, bank=bank: e.matmul(bank[:, 0:257], lhsT=kkP[:Lc, (2 * h + dc) * 128:(2 * h + dc + 1) * 128], rhs=vw_P[:Lc, h, :], start=True, stop=True), [f"kk{pp}", f"vw{pp}"], [kb])
                                V(lambda e, h=h, dc=dc, bank=bank: e.scalar_tensor_tensor(out=CT[:, 2 * h + dc, :], in0=CT[:, 2 * h + dc, :], scalar=gsP[:, EG, h:h + 1], in1=bank[:, 0:257], op0=ALU.mult, op1=ALU.add), [f"CT{h}", f"gs_eg{pp}", kb], [f"CT{h}"])
                                A(lambda e, h=h, dc=dc: e.activation(out=CTb[:, 2 * h + dc, :], in_=CT[:, 2 * h + dc, :], func=AF.Copy), [f"CT{h}"], [f"CTb{h}"])

                        for st_, h_ in ((1, 0), (2, 0), (1, 1), (2, 1), (3, 0), (1, 2), (2, 2), (3, 1), (1, 3), (2, 3), (3, 2), (3, 3)):
                            (hs1_, hs2_, hs3_)[st_ - 1](h_)
                        for c in range(8):
                            T(lambda e, c=c: e.transpose(out=psb[:, c * 64:c * 64 + Lc], in_=hn[:Lc, c * 128:(c + 1) * 128], identity=identb[:Lc, :Lc]), ["hn", "identb"], ["psb"])
                        for c in range(8):
                            V(lambda e, c=c: e.scalar_tensor_tensor(out=hmT[:, c, o0:o0 + Lc], in0=xcT[:, c, o0:o0 + Lc], scalar=mlv_t[:, 5, c:c + 1], in1=psb[:, c * 64:c * 64 + Lc], op0=ALU.mult, op1=ALU.add), ["xcT", "mlv", "psb"], ["hmT"])

                    nch = N // Lc
                    chunk_pre(0)
                    for ch in range(nch):
                        if ch + 1 < nch:
                            chunk_pre(ch + 1)
                        chunk_heads(ch)
                    V(lambda e, N=N: e.tensor_copy(out=xmT[:, :, 0:3], in_=xmT[:, :, N:N + 3]), ["xmT"], ["xmT"])
                    if blk == 0:
                        continue
                    for o in range(8):
                        bank = ps[1 + o % 2]; kb = f"ps{1 + o % 2}"
                        for c in range(8):
                            T(lambda e, c=c, o=o, bank=bank: e.matmul(bank[:, :N], lhsT=Wmlo[:, c, o * 128:(o + 1) * 128], rhs=hmT[:, c, :N], start=(c == 0), stop=(c == 7)), ["Wmlo", "hmT"], [kb])
                        V(lambda e, o=o, bank=bank: e.tensor_tensor(out=tmpf[:, :N], in0=bank[:, :N], in1=sgm[:, o, :N], op=ALU.mult), [kb, "sgm"], ["tmpfB"])
                        V(lambda e, o=o: e.tensor_tensor(out=mixT[:, o, :N], in0=tmpf[:, :N], in1=mixin[:, o, :N], op=ALU.add), ["tmpfB", "mixin"], ["mixTB"])
                    for i in range(ntile):
                        for half in range(2):
                            bank = ps[1 + half]; kb = f"ps{1 + half}"
                            for c in range(8):
                                T(lambda e, c=c, i=i, half=half, bank=bank: e.matmul(bank[:, :], lhsT=mixT[:, c, i * 128:(i + 1) * 128], rhs=Wout[:, c, half * 512:(half + 1) * 512], start=(c == 0), stop=(c == 7)), ["mixTB", "Wout"], [kb])
                            V(lambda e, i=i, half=half, bank=bank: e.tensor_tensor(out=h1t[:, half * 512:(half + 1) * 512], in0=bank[:, :], in1=xt[:, i, half * 512:(half + 1) * 512], op=ALU.add), [kb, f"xtB{i}"], ["h1tB"])
                        r0 = tok0 - 16 + i * 128
                        DM(lambda e, r0=r0: e.dma_start(out=h1[r0:r0 + 128, :], in_=h1t[:, :]), ["h1tB"], ["h1"])
                S.emit(final=(passes[-1] == "B"))

        if "C" in passes:
            conv_batch(100000)
            with ExitStack() as sc_:
                Wpq = sbt(sc_, "Wpq", [128, 8, 2048], BF16)
                n2g_t = sbt(sc_, "n2gC", [128, D], F32)
                fng_t = sbt(sc_, "fngC", [128, D], F32)
                kyb = sbt(sc_, "kyb", [128, 16, 128], BF16)
                sc_outer = sc_
                sc_ = ExitStack()
                sc_.__enter__()
                stg = sbt(sc_, "stgC", [128, 2, 2048], F32)
                load_weight(sc_, "Wpq", w_pq, 1024, 0, 2048, Wpq, 0, stg, "c")
                kyf = sbt(sc_, "kyf", [128, 16, 128], F32)
                DM(lambda e: e.dma_start(out=n2g_t[:], in_=n2g), [], ["n2gC"])
                DM(lambda e: e.dma_start(out=fng_t[:], in_=fng), [], ["fngC"])
                DM(lambda e: e.dma_start(out=kyf[:], in_=keysT), [], ["kyf"])
                V(lambda e: e.tensor_copy(out=kyb[:], in_=kyf[:]), ["kyf"], ["kyb"])
                S.emit()
                sc_.__exit__(None, None, None)
                sc_ = sc_outer
                h1t = [sbt(sc_, f"h1tC{i}", [128, D], F32) for i in range(2)]
                xn2d = [sbt(sc_, f"xn2_{i}", [128, D], BF16) for i in range(2)]
                xn2b = sbt(sc_, "xn2b", [128, D], BF16)
                xn2T = sbt(sc_, "xn2T", [128, 8, 128], BF16)
                qTb = sbt(sc_, "qTb", [128, 16, 128], BF16)
                scq = sbt(sc_, "scq", [128, 16, 128], F32)
                wk = sbt(sc_, "wkC", [128, 256], F32)
                svt = sbt(sc_, "svt", [128, 16, 16], F32)
                sit = sbt(sc_, "sit", [128, 16, 16], U32)
                sif = sbt(sc_, "sif", [128, 16, 16], F32)
                si0s = sbt(sc_, "si0s", [128, 8, 16], F32)
                cand = sbt(sc_, "cand", [128, 8, 256], F32)
                cs = sbt(sc_, "cs", [128, 8, 16], F32)
                cpos = sbt(sc_, "cpos", [128, 8, 16], U32)
                cab = sbt(sc_, "cab", [128, 2, 8, 16], U32)
                cabf = sbt(sc_, "cabf", [128, 2, 8, 16], F32)
                oh = sbt(sc_, "oh", [128, 8, 16, 16], F32)
                e01 = sbt(sc_, "e01", [128, 2, 8, 16], F32)
                eidf = sbt(sc_, "eidf", [128, 128], F32)
                eidi2 = [sbt(sc_, f"eidi{i}", [128, 128], I32) for i in range(2)]
                gex = sbt(sc_, "gex", [128, 8, 16], F32)
                gz = sbt(sc_, "gz", [128, 2, 8], F32)
                gate2 = [sbt(sc_, f"gate{i}", [128, 128], F32) for i in range(2)]
                dots = sbt(sc_, "dots", [128, 128], F32)
                actv = sbt(sc_, "actv", [128, 128], F32)
                wgt2 = [sbt(sc_, f"wgt{i}", [128, 128], F32) for i in range(2)]
                NG = 16
                uvg = [sbt(sc_, f"uvg{i}", [128, 2 * D], BF16) for i in range(NG)]
                prd = [sbt(sc_, f"prd{i}", [128, D], BF16) for i in range(4)]
                dgw = [sbt(sc_, f"dgw{i}", [128, 128], BF16) for i in range(2)]
                hout = sbt(sc_, "hout", [128, D], F32)
                outt = sbt(sc_, "outt", [128, D], F32)
                scrC = dict(tag="nC", junk=sbt(sc_, "junkC", [128, D], F32), ssq=sbt(sc_, "ssqC", [128, 1], F32), rs=sbt(sc_, "rsC", [128, 3], F32))
                iota16 = cstf[:, 2, 0:16]

                def top16(src_ap, ksrc, n, v_ap, i_ap, kv, ki):
                    V(lambda e: e.max(out=v_ap[:, 0:8], in_=src_ap), [ksrc], [kv])
                    V(lambda e: e.max_index(out=i_ap[:, 0:8], in_max=v_ap[:, 0:8], in_values=src_ap), [ksrc, kv], [ki])
                    V(lambda e: e.match_replace(out=wk[:, :n], in_to_replace=v_ap[:, 0:8], in_values=src_ap, imm_value=-1e30), [ksrc, kv], ["wkC"])
                    V(lambda e: e.max(out=v_ap[:, 8:16], in_=wk[:, :n]), ["wkC"], [kv])
                    V(lambda e: e.max_index(out=i_ap[:, 8:16], in_max=v_ap[:, 8:16], in_values=wk[:, :n]), ["wkC", kv], [ki])

                def phase_P1(ti):
                    hb = h1t[ti % 2]; khb = f"h1tC{ti % 2}"
                    xn2 = xn2d[ti % 2]; kxn2 = f"xn2_{ti % 2}"
                    DM(lambda e, ti=ti, hb=hb: e.dma_start(out=hb[:, :], in_=h1[ti * 128:(ti + 1) * 128, :]), ["h1"], [khb])
                    rms_rows(128, hb[:, :], n2g_t[:, :], [(xn2[:, :], kxn2)], khb, "nC", scrC)
                    transpose_rows(128, xn2[:, :], kxn2, xn2T[:, :, :], "xn2T")
                    for g4 in range(4):
                        bank = ps[1 + g4 % 2]; kb = f"ps{1 + g4 % 2}"
                        for hi4 in range(4):
                            hi = g4 * 4 + hi4
                            for k in range(8):
                                T(lambda e, hi=hi, hi4=hi4, k=k, bank=bank: e.matmul(bank[:, hi4 * 128:(hi4 + 1) * 128], lhsT=Wpq[:, k, hi * 128:(hi + 1) * 128], rhs=xn2T[:, k, :], start=(k == 0), stop=(k == 7)), ["Wpq", "xn2T"], [kb])
                        A(lambda e, g4=g4, bank=bank: e.activation(out=qTb[:, g4 * 4:(g4 + 1) * 4, :], in_=bank[:, :].rearrange("p (a t) -> p a t", a=4), func=AF.Copy), [kb], ["qTb"])
                    for g4 in range(4):
                        bank = ps[1 + g4 % 2]; kb = f"ps{1 + g4 % 2}"
                        for hi4 in range(4):
                            hi = g4 * 4 + hi4
                            T(lambda e, hi=hi, hi4=hi4, bank=bank: e.matmul(bank[:, hi4 * 128:(hi4 + 1) * 128], lhsT=qTb[:, hi, :], rhs=kyb[:, hi, :], start=True, stop=True), ["qTb", "kyb"], [kb])
                        A(lambda e, g4=g4, bank=bank: e.activation(out=scq[:, g4 * 4:(g4 + 1) * 4, :], in_=bank[:, :].rearrange("p (a t) -> p a t", a=4), func=AF.Copy), [kb], ["scq"])

                def phase_P2_gen(ti):
                    eidi = eidi2[ti % 2]; keid = f"eidi{ti % 2}"
                    gate = gate2[ti % 2]; kgate = f"gate{ti % 2}"
                    for hi in range(16):
                        top16(scq[:, hi, :], "scq", 128, svt[:, hi, :], sit[:, hi, :], "svt", "sit")
                        yield
                    V(lambda e: e.tensor_copy(out=sif[:], in_=sit[:]), ["sit"], ["sif"])
                    sv4 = svt[:, :, :].rearrange("p (h i) k -> p h i k", i=2)
                    sf4 = sif[:, :, :].rearrange("p (h i) k -> p h i k", i=2)
                    V(lambda e, sf4=sf4: e.tensor_scalar(out=si0s[:], in0=sf4[:, :, 0, :], scalar1=128.0, scalar2=None, op0=ALU.mult), ["sif"], ["si0s"])
                    V(lambda e, sv4=sv4: e.tensor_tensor(out=cand[:, :, :].rearrange("p h (a b) -> p h a b", a=16),
                                                         in0=sv4[:, :, 0, :].unsqueeze(3).to_broadcast([128, 8, 16, 16]),
                                                         in1=sv4[:, :, 1, :].unsqueeze(2).to_broadcast([128, 8, 16, 16]), op=ALU.add), ["svt"], ["cand"])
                    yield
                    for h in range(8):
                        top16(cand[:, h, :], "cand", 256, cs[:, h, :], cpos[:, h, :], "cs", "cpos")
                        yield
                    V(lambda e: e.tensor_single_scalar(out=cab[:, 0, :, :], in_=cpos[:], scalar=4, op=ALU.logical_shift_right), ["cpos"], ["cab"])
                    V(lambda e: e.tensor_single_scalar(out=cab[:, 1, :, :], in_=cpos[:], scalar=15, op=ALU.bitwise_and), ["cpos"], ["cab"])
                    V(lambda e: e.tensor_copy(out=cabf[:], in_=cab[:]), ["cab"], ["cabf"])
                    yield
                    for ab in range(2):
                        srcv = si0s[:, :, :] if ab == 0 else sf4[:, :, 1, :]
                        V(lambda e, ab=ab: e.tensor_tensor(out=oh[:], in0=cabf[:, ab, :, :].unsqueeze(3).to_broadcast([128, 8, 16, 16]),
                                                           in1=iota16.unsqueeze(1).unsqueeze(1).to_broadcast([128, 8, 16, 16]), op=ALU.is_equal), ["cabf", "cstf"], ["oh"])
                        V(lambda e, srcv=srcv: e.tensor_tensor(out=oh[:], in0=oh[:], in1=srcv.unsqueeze(2).to_broadcast([128, 8, 16, 16]), op=ALU.mult), ["oh", "si0s", "sif"], ["oh"])
                        V(lambda e, ab=ab: e.tensor_reduce(out=e01[:, ab, :, :], in_=oh[:], axis=AX.X, op=ALU.add), ["oh"], ["e01"])
                        yield
                    V(lambda e: e.tensor_tensor(out=eidf[:, :].rearrange("p (h k) -> p h k", h=8), in0=e01[:, 0, :, :], in1=e01[:, 1, :, :], op=ALU.add), ["e01"], ["eidf"])
                    V(lambda e, eidi=eidi: e.tensor_copy(out=eidi[:], in_=eidf[:]), ["eidf"], [keid])
                    V(lambda e: e.tensor_tensor(out=gex[:], in0=cs[:], in1=cs[:, :, 0:1].to_broadcast([128, 8, 16]), op=ALU.subtract), ["cs"], ["gex"])
                    A(lambda e: e.activation(out=gex[:], in_=gex[:], func=AF.Exp), ["gex"], ["gex"])
                    V(lambda e: e.tensor_reduce(out=gz[:, 0, :], in_=gex[:], axis=AX.X, op=ALU.add), ["gex"], ["gz0"])
                    V(lambda e: e.reciprocal(out=gz[:, 1, :], in_=gz[:, 0, :]), ["gz0"], ["gz1"])
                    V(lambda e, gate=gate: e.tensor_tensor(out=gate[:, :].rearrange("p (h k) -> p h k", h=8), in0=gex[:], in1=gz[:, 1, :].unsqueeze(2).to_broadcast([128, 8, 16]), op=ALU.mult), ["gex", "gz1"], [kgate])

                djunk2 = [sbt(sc_, f"djunkb{i}", [128, D], BF16) for i in range(2)]
                def phase_P2(ti):
                    for _ in phase_P2_gen(ti):
                        pass

                first_gather = [True]
                gctr = [0]

                def epilogue(ti):
                    acc0 = 5 if ti % 2 == 0 else 3
                    hb = h1t[ti % 2]; khb = f"h1tC{ti % 2}"
                    for half in range(2):
                        V(lambda e, half=half, hb=hb: e.tensor_tensor(out=hout[:, half * 512:(half + 1) * 512], in0=ps[acc0 + half][:, :], in1=hb[:, half * 512:(half + 1) * 512], op=ALU.add), [f"ps{acc0 + half}", khb], ["hout"])
                    rms_rows(128, hout[:, :], fng_t[:, :], [(outt[:, :], "outt")], "hout", "nF", scrC)
                    DM(lambda e, ti=ti: e.dma_start(out=out[ti * 128:(ti + 1) * 128, :], in_=outt[:, :]), ["outt"], ["out"])

                def tile_body(ti):
                    acc0 = 5 if ti % 2 == 0 else 3
                    xn2 = xn2d[ti % 2]; kxn2 = f"xn2_{ti % 2}"
                    eidi = eidi2[ti % 2]; keid = f"eidi{ti % 2}"
                    gate = gate2[ti % 2]; kgate = f"gate{ti % 2}"
                    wgt = wgt2[ti % 2]; kw = f"wgt{ti % 2}"
                    hb = h1t[ti % 2]; khb = f"h1tC{ti % 2}"
                    GS = 2
                    pending = []

                    def vside(g, bufs):
                        for k_, sl in enumerate(range(g * GS, (g + 1) * GS)):
                            b = bufs[k_]
                            dg = dgw[sl % 2]; kdg = f"dgw{sl % 2}"
                            V(lambda e, sl=sl, dg=dg, gate=gate: e.tensor_scalar(out=dg[:, :], in0=identb[:, :], scalar1=actv[:, sl:sl + 1], scalar2=gate[:, sl:sl + 1], op0=ALU.mult, op1=ALU.mult), ["identb", f"actv{g % 4}", kgate], [kdg])
                            for half in range(2):
                                T(lambda e, sl=sl, half=half, dg=dg, b=b: e.matmul(ps[acc0 + half][:, :], lhsT=dg[:, :], rhs=uvg[b][:, D + half * 512:D + (half + 1) * 512], start=(sl == 0), stop=(sl == 127)), [kdg, f"uvg{b}"], [f"ps{acc0 + half}"])

                    for g in range(128 // GS):
                        bufs = []
                        for sl in range(g * GS, (g + 1) * GS):
                            b = gctr[0] % NG
                            gctr[0] += 1
                            bufs.append(b)
                            extra = conv_keys if first_gather[0] else []
                            first_gather[0] = False
                            S.op("pool", lambda e, sl=sl, b=b, eidi=eidi: e.indirect_dma_start(out=uvg[b][:, :], out_offset=None, in_=uvb, in_offset=bass.IndirectOffsetOnAxis(ap=eidi[:, sl:sl + 1], axis=0)),
                                 reads=[keid] + extra, writes=[f"uvg{b}"], dma=True, ring=GRING)
                            pr = prd[sl % 4]; kpr = f"prd{sl % 4}"
                            V(lambda e, sl=sl, b=b, pr=pr: e.tensor_tensor(out=pr[:, :], in0=uvg[b][:, 0:D], in1=xn2[:, :], op=ALU.mult), [f"uvg{b}", kxn2], [kpr])
                            dj = djunk2[sl % 2]
                            A(lambda e, sl=sl, pr=pr, dj=dj: e.activation(out=dj[:, :], in_=pr[:, :], func=AF.Copy, accum_out=dots[:, sl:sl + 1]), [kpr], [f"djunk{sl % 2}", f"dots{g % 4}_{sl % 2}"])
                        gsl = slice(g * GS, (g + 1) * GS)
                        A(lambda e, gsl=gsl: e.activation(out=actv[:, gsl], in_=dots[:, gsl], func=AF.Gelu), [f"dots{g % 4}_0", f"dots{g % 4}_1"], [f"actv{g % 4}"])
                        if pending:
                            vside(*pending.pop(0))
                        pending.append((g, bufs))
                        if g == 1 and ti > 0:
                            epilogue(ti - 1)
                        if g == 1 and ti + 1 < ntileC:
                            phase_P1(ti + 1)
                            p2gen = phase_P2_gen(ti + 1)
                        if g >= 18 and ti + 1 < ntileC and ((g - 17) * 30) // 46 > ((g - 18) * 30) // 46:
                            next(p2gen, None)
                    while pending:
                        vside(*pending.pop(0))
                    if ti + 1 < ntileC:
                        for _ in p2gen:
                            pass

                phase_P1(0)
                phase_P2(0)
                for ti in range(ntileC):
                    tile_body(ti)
                epilogue(ntileC - 1)
                S.emit(final=True)
    return nc


def host_inputs(inp):
    f = lambda a: np.ascontiguousarray(np.asarray(a, dtype=np.float32))
    x = f(inp["x"]); meta = f(inp["meta_tokens"])
    rep = lambda v: np.ascontiguousarray(np.broadcast_to(f(v).reshape(1, -1), (128, f(v).size)))
    pm = lambda v, n: np.ascontiguousarray(f(v).reshape(n, 128).T)
    common = {}
    common["w_in"] = f(inp["w_in"][0]); common["w_glu"] = f(inp["ssm_w_glu"][0]); common["w_sso"] = f(inp["w_ssm_out"][0])
    common["w_mlo"] = f(inp["w_ml_out"][0]); common["w_out"] = f(inp["w_out"][0]); common["w_pq"] = f(inp["peer_w_q"][0])
    common["n1g"] = rep(inp["norm1_g"][0]); common["mlng"] = rep(inp["ml_norm_g"][0]); common["n2g"] = rep(inp["norm2_g"][0]); common["fng"] = rep(inp["final_norm_g"])
    def gp(a):
        return np.ascontiguousarray(f(a).reshape(16, 2, 64).transpose(1, 2, 0).reshape(128, 16))
    ldt = np.broadcast_to(f(inp["ssm_log_dt"][0])[:, None], (32, 64))
    common["s5a"] = np.ascontiguousarray(np.stack([gp(inp["ssm_a_re"][0]), gp(inp["ssm_a_im"][0]), gp(ldt)], axis=1))
    bwide = np.zeros((128, 2, 16, 128), np.float32)
    cwide = np.zeros((128, 2, 16, 128), np.float32)
    for ri, (bk, ck) in enumerate([("ssm_b_re", "ssm_c_re"), ("ssm_b_im", "ssm_c_im")]):
        B = f(inp[bk][0])
        C = f(inp[ck][0])
        for j in range(16):
            for g2 in range(2):
                g = 2 * j + g2
                gl = g % 8
                bwide[g2 * 64:(g2 + 1) * 64, ri, j, gl * 16:(gl + 1) * 16] = B[g]
                cwide[g2 * 64:(g2 + 1) * 64, ri, j, gl * 16:(gl + 1) * 16] = C[g].T
    common["s5bw"] = bwide; common["s5cw"] = cwide
    common["s5v"] = np.ascontiguousarray(np.stack([pm(inp["ssm_d"][0], 4), pm(inp["ssm_b_glu"][0], 4)], axis=1))
    cw_ = f(inp["ml_conv_w"][0])
    common["mlv"] = np.ascontiguousarray(np.stack([pm(cw_[0], 8), pm(cw_[1], 8), pm(cw_[2], 8), pm(cw_[3], 8), pm(inp["ml_conv_b"][0], 8), pm(inp["ml_skip"][0], 8)], axis=1))
    def blockdiag_chunks(w, transpose=False):
        w = f(w)
        o = np.zeros((8, 128, 128), np.float32)
        for n in range(256):
            c = n // 32; r = (n % 32) * 4
            o[c, r:r + 4, r:r + 4] = w[n].T if transpose else w[n]
        return o.transpose(1, 0, 2)
    common["mlbd"] = np.ascontiguousarray(np.stack([blockdiag_chunks(inp["ml_wq"][0]), blockdiag_chunks(inp["ml_wk"][0]), blockdiag_chunks(inp["ml_wv"][0]),
                                                    blockdiag_chunks(inp["ml_wq"][0], True), blockdiag_chunks(inp["ml_wk"][0], True)], axis=1))
    common["mlbdv"] = np.ascontiguousarray(blockdiag_chunks(inp["ml_wv"][0], True))
    common["mlwif"] = np.ascontiguousarray(f(inp["ml_w_if"][0]).reshape(24, 128, 8).transpose(1, 0, 2))
    common["mlbif"] = rep(inp["ml_b_if"][0])
    common["keysT"] = np.ascontiguousarray(f(inp["peer_sub_keys"][0]).reshape(16, 128, 128).transpose(2, 0, 1))
    common["u_tab"] = f(inp["peer_u"][0]); common["v_tab"] = f(inp["peer_v"][0])
    cst = np.zeros((128, 4, 128), np.float32)
    cst[:, 0, :] = np.eye(128, dtype=np.float32)
    m = np.triu(np.ones((64, 64), np.float32))
    cst[0:64, 1, 0:64] = m; cst[64:128, 1, 64:128] = m
    cst[:, 2, :] = np.arange(128, dtype=np.float32)[None, :]
    cst[:, 3, :] = 1.0
    common["cst"] = cst
    maps = []
    for b in range(8):
        d = dict(common)
        d["xh"] = np.ascontiguousarray(np.concatenate([meta, x[b]], axis=0))
        maps.append(d)
    return maps


def kernel(**inputs):
    nc = build()
    maps = host_inputs(inputs)
    res = run_bass_kernel_spmd(nc, maps, core_ids=list(range(8)))
    return np.stack([np.asarray(r["out"], dtype=np.float32) for r in res.results], axis=0)
```

```python
import math
import types
import numpy as np
from contextlib import ExitStack
import concourse.bass as bass
import concourse.mybir as mybir
from concourse.bass_utils import run_bass_kernel_spmd

F32 = mybir.dt.float32
BF16 = mybir.dt.bfloat16
I32 = mybir.dt.int32
U32 = mybir.dt.uint32
AF = mybir.ActivationFunctionType
ALU = mybir.AluOpType
AX = mybir.AxisListType

ENGINES = ("pe", "dve", "act", "pool", "sp")
EPOCH = 20000
N_DMA_SEMS = 24
RING_SIZES = {"cv": 6, "gq": 12}
L = 4112
NMETA = 16
D = 1024
EPS = 1e-6


class Sched:
    def __init__(self, nc, stack):
        self.nc = nc
        self.stack = stack
        self.ops = []
        self.buf = {}
        self.sem_cache = {}
        self.cnt = {e: 0 for e in ENGINES}
        self.dma_cnt = [0] * N_DMA_SEMS
        self.dma_rr = 0
        self.ring_cnt = {}
        self.ring_rr = {}
        self.emitted = 0
        self.fence_tickets = []

    def _sem(self, name):
        if name not in self.sem_cache:
            self.sem_cache[name] = self.stack.enter_context(self.nc.semaphore(name))
        return self.sem_cache[name]

    @staticmethod
    def _snap(fn):
        if fn.__closure__ is None:
            return fn
        cells = []
        for c in fn.__closure__:
            try:
                cells.append(types.CellType(c.cell_contents))
            except ValueError:
                cells.append(c)
        return types.FunctionType(fn.__code__, fn.__globals__, fn.__name__, fn.__defaults__, tuple(cells))

    def op(self, eng, fn, reads=(), writes=(), dma=False, detached=False, ring="dma"):
        fn = self._snap(fn)
        deps = {}
        for k in reads:
            st = self.buf.get(k)
            if st is not None and st["w"] is not None:
                deps[st["w"]] = True
        for k in writes:
            st = self.buf.get(k)
            if st is not None:
                if st["w"] is not None:
                    deps.setdefault(st["w"], False)
                for r_ in st["r"]:
                    deps.setdefault(r_, False)
        oid = len(self.ops)
        self.ops.append(dict(eng=eng, fn=fn, deps=deps, dma=dma, signal=dma, ticket=None, detached=detached, ring=ring))
        for k in reads:
            st = self.buf.setdefault(k, {"w": None, "r": []})
            st["r"].append(oid)
        for k in writes:
            self.buf[k] = {"w": oid, "r": []}
        return oid

    def emit(self, final=False):
        ops = self.ops
        start = self.emitted
        new = ops[start:]
        for o in new:
            live = set()
            for d, raw in o["deps"].items():
                if d < start and not (ops[d]["detached"] and ops[d]["dma"]):
                    continue
                od = ops[d]
                if od["eng"] == o["eng"] and not od["dma"] and not o["dma"] and (o["eng"] == "pe" or (RELAX and not raw)):
                    continue
                od["signal"] = True
                live.add(d)
            o["deps"] = live
        last = {}
        for o in new:
            if not o["dma"] and not o["detached"]:
                last[o["eng"]] = o
        for o in last.values():
            o["signal"] = True
        for o in new:
            if o["dma"] and o["ring"] != "dma":
                rn = o["ring"]
                cntl = self.ring_cnt.setdefault(rn, [0] * RING_SIZES[rn])
                j = self.ring_rr.get(rn, 0) % RING_SIZES[rn]
                self.ring_rr[rn] = self.ring_rr.get(rn, 0) + 1
                o["prev_dma"] = (f"{rn}{j}", cntl[j])
                cntl[j] += 16
                o["ticket"] = (f"{rn}{j}", cntl[j])
            elif o["dma"]:
                j = self.dma_rr % N_DMA_SEMS
                self.dma_rr += 1
                o["prev_dma"] = (f"dma{j}", self.dma_cnt[j])
                self.dma_cnt[j] += 16
                o["ticket"] = (f"dma{j}", self.dma_cnt[j])
            elif o["signal"]:
                e = o["eng"]
                self.cnt[e] += 1
                ep = (self.cnt[e] - 1) // EPOCH
                o["ticket"] = (f"c_{e}_{ep}", (self.cnt[e] - 1) % EPOCH + 1)
        for o in new:
            if o["ticket"] is not None:
                self._sem(o["ticket"][0])
        per_eng = {e: [] for e in ENGINES}
        for o in new:
            per_eng[o["eng"]].append(o)
        fence_in = list(self.fence_tickets)
        fence_out = [o["ticket"] for o in last.values()] + [(f"dma{j}", c) for j, c in enumerate(self.dma_cnt) if c > 0]
        fence_out += [(f"gq{j}", c) for j, c in enumerate(self.ring_cnt.get("gq", [])) if c > 0]

        def replay(engname):
            def body(eng):
                waited = {}
                for s, v in fence_in:
                    eng.wait_ge(self._sem(s), v)
                    waited[s] = max(waited.get(s, 0), v)
                for o in per_eng[engname]:
                    need = {}
                    for d in o["deps"]:
                        s, v = ops[d]["ticket"]
                        if need.get(s, 0) < v:
                            need[s] = v
                    if o["dma"]:
                        s, v = o["prev_dma"]
                        if v > 0 and need.get(s, 0) < v:
                            need[s] = v
                    for s, v in need.items():
                        if waited.get(s, 0) >= v:
                            continue
                        eng.wait_ge(self._sem(s), v)
                        waited[s] = v
                    ins = o["fn"](eng)
                    if o["signal"]:
                        s, v = o["ticket"]
                        ins.then_inc(self._sem(s), 16 if o["dma"] else 1)
                if final:
                    for s, v in fence_out:
                        if waited.get(s, 0) < v:
                            eng.wait_ge(self._sem(s), v)
            return body

        with self.nc.Block() as block:
            block.tensor(replay("pe"))
            block.vector(replay("dve"))
            block.scalar(replay("act"))
            block.gpsimd(replay("pool"))
            block.sync(replay("sp"))
        self.fence_tickets = fence_out
        self.emitted = len(ops)


GRING = "gq"
RELAX = False
TWO_PI = 2.0 * math.pi
CW1 = 6.28125
CW2 = TWO_PI - CW1


def build(passes=("A", "B", "C"), debug=False, nblkA=9, nblkB=17, ntileC=32):
    nc = bass.Bass("TRN2", target_bir_lowering=False)

    def din(name, shape, dt=F32):
        return nc.dram_tensor(name, list(shape), dt, kind="ExternalInput").ap()

    xh = din("xh", [L, D])
    w_in = din("w_in", [D, 4608])
    w_glu = din("w_glu", [512, 512])
    w_sso = din("w_sso", [512, D])
    w_mlo = din("w_mlo", [D, D])
    w_out = din("w_out", [D, D])
    w_pq = din("w_pq", [D, 2048])
    n1g = din("n1g", [128, D])
    mlng = din("mlng", [128, D])
    n2g = din("n2g", [128, D])
    fng = din("fng", [128, D])
    s5a = din("s5a", [128, 3, 16])
    s5bw = din("s5bw", [128, 2, 16, 128])
    s5cw = din("s5cw", [128, 2, 16, 128])
    s5v = din("s5v", [128, 2, 4])
    mlv = din("mlv", [128, 6, 8])
    mlbd = din("mlbd", [128, 5, 8, 128])
    mlbdv = din("mlbdv", [128, 8, 128])
    mlwif = din("mlwif", [128, 24, 8])
    mlbif = din("mlbif", [128, 8])
    keysT = din("keysT", [128, 16, 128])
    u_tab = din("u_tab", [16384, D])
    v_tab = din("v_tab", [16384, D])
    cst = din("cst", [128, 4, 128])
    kind_s = "ExternalOutput" if debug else "Internal"
    mixs = nc.dram_tensor("mixs", [D, L], BF16, kind=kind_s).ap()
    h1 = nc.dram_tensor("h1", [4096, D], F32, kind=kind_s).ap()
    out = nc.dram_tensor("out", [4096, D], F32, kind="ExternalOutput").ap()
    uvb = nc.dram_tensor("uvb", [16384, 2 * D], BF16, kind="Internal").ap()

    with ExitStack() as top:
        S = Sched(nc, top)
        for rn_ in ("cv", GRING):
            if rn_ != "dma":
                for j_ in range(RING_SIZES[rn_]):
                    S._sem(f"{rn_}{j_}")
        V = lambda fn, r, w: S.op("dve", fn, reads=r, writes=w)
        A = lambda fn, r, w: S.op("act", fn, reads=r, writes=w)
        G = lambda fn, r, w: S.op("pool", fn, reads=r, writes=w)
        T = lambda fn, r, w: S.op("pe", fn, reads=r, writes=w)
        DM = lambda fn, r, w: S.op("sp", fn, reads=r, writes=w, dma=True)

        ps = [top.enter_context(nc.psum_tensor(f"ps{i}", [128, 512], F32)) for i in range(3)]
        psS = top.enter_context(nc.psum_tensor("psS", [128, 1024], F32))
        ps += [psS[:, 0:512], psS[:, 512:1024]]
        ps += [top.enter_context(nc.psum_tensor(f"ps{i}", [128, 512], F32)) for i in range(5, 7)]
        psb = top.enter_context(nc.psum_tensor("psb", [128, 1024], BF16))

        def sbt(stack, name, shape, dt):
            return stack.enter_context(nc.sbuf_tensor("sb_" + name, list(shape), dt))

        cstf = sbt(top, "cstf", [128, 4, 128], F32)
        identb = sbt(top, "identb", [128, 128], BF16)
        DM(lambda e: e.dma_start(out=cstf[:], in_=cst), [], ["cstf"])
        V(lambda e: e.tensor_copy(out=identb[:], in_=cstf[:, 0, :]), ["cstf"], ["identb"])
        identf = cstf[:, 0, :]
        maskT = cstf[:, 1, :]
        iota_f = cstf[:, 2, :]
        ones_f = cstf[:, 3, :]

        def load_weight(stack, name, src, rows, c0, c1, dst, dcol0, stg, tagn):
            nk = rows // 128
            w = c1 - c0
            i = 0
            for k in range(nk):
                SW = stg.shape[2]
                for s0 in range(0, w, SW):
                    s1 = min(w, s0 + SW)
                    b = load_weight.ctr % 2
                    load_weight.ctr += 1
                    DM(lambda e, b=b, k=k, s0=s0, s1=s1: e.dma_start(out=stg[:, b, 0:s1 - s0], in_=src[k * 128:(k + 1) * 128, c0 + s0:c0 + s1]),
                       [], [f"stg{b}"])
                    fn = lambda e, b=b, k=k, s0=s0, s1=s1: e.tensor_copy(out=dst[:, k, dcol0 + s0:dcol0 + s1], in_=stg[:, b, 0:s1 - s0])
                    if i % 2 == 0:
                        A(lambda e, b=b, k=k, s0=s0, s1=s1: e.activation(out=dst[:, k, dcol0 + s0:dcol0 + s1], in_=stg[:, b, 0:s1 - s0], func=AF.Copy), [f"stg{b}"], [name])
                    else:
                        V(fn, [f"stg{b}"], [name])
                    i += 1
        load_weight.ctr = 0

        def rms_rows(P, x_ap, g_ap, outs, kx, tag, scr):
            junk, ssq, rs = scr["junk"], scr["ssq"], scr["rs"]
            tag = scr["tag"]
            A(lambda e: e.activation(out=junk[:P, :], in_=x_ap, func=AF.Square, accum_out=ssq[:P, 0:1]), [kx], [tag + "junk", tag + "ssq"])
            A(lambda e: e.activation(out=rs[:P, 1:2], in_=ssq[:P, 0:1], func=AF.Ln, scale=1.0 / D, bias=EPS), [tag + "ssq"], [tag + "rs1"])
            A(lambda e: e.activation(out=rs[:P, 2:3], in_=rs[:P, 1:2], func=AF.Exp, scale=-0.5), [tag + "rs1"], [tag + "rs2"])
            for (o_ap, ko) in outs:
                V(lambda e, o_ap=o_ap: e.scalar_tensor_tensor(out=o_ap, in0=x_ap, scalar=rs[:P, 2:3], in1=g_ap, op0=ALU.mult, op1=ALU.mult),
                  [kx, tag + "rs2"], [ko])

        def transpose_rows(P, xn_bf_ap, kxn, dst_ap, kdst):
            for c in range(8):
                T(lambda e, c=c: e.transpose(out=psb[:, c * 128:c * 128 + P], in_=xn_bf_ap[:, c * 128:(c + 1) * 128], identity=identb[:P, :P]),
                  [kxn, "identb"], ["psb"])
            A(lambda e: e.activation(out=dst_ap, in_=psb[:, :].rearrange("p (c t) -> p c t", c=8)[:, :, :P], func=AF.Copy), ["psb"], [kdst])

        conv_keys = []
        if "C" in passes:
            CW = 1024
            cvf = [sbt(top, f"cvf{i}", [128, CW], F32) for i in range(2)]
            cvb = [sbt(top, f"cvb{i}", [128, CW], BF16) for i in range(2)]
            chunks = [(tsrc, half, c) for (tsrc, half) in ((u_tab, 0), (v_tab, 1)) for c in range(131072 // CW)]
            def cv_in(i):
                tsrc, half, c = chunks[i]
                S.op("pool", lambda e, i=i, tsrc=tsrc, c=c: e.dma_start(out=cvf[i % 2][:, :], in_=tsrc.rearrange("(p r) d -> p (r d)", p=128)[:, c * CW:(c + 1) * CW]),
                     reads=[], writes=[f"cvf{i % 2}"], dma=True, detached=True, ring="cv")
            cvs = {"i": 0}

            def conv_batch(n):
                for _ in range(n):
                    i = cvs["i"]
                    if i >= len(chunks):
                        return
                    if i == 0:
                        cv_in(0)
                    if i + 1 < len(chunks):
                        cv_in(i + 1)
                    tsrc, half, c = chunks[i]
                    S.op("pool", lambda e, i=i: e.tensor_copy(out=cvb[i % 2][:, :], in_=cvf[i % 2][:, :]), reads=[f"cvf{i % 2}"], writes=[f"cvb{i % 2}"], detached=True)
                    S.op("pool", lambda e, i=i, half=half, c=c: e.dma_start(out=uvb.rearrange("(p r) c -> p r c", p=128)[:, c, half * D:(half + 1) * D], in_=cvb[i % 2][:, :]),
                         reads=[f"cvb{i % 2}"], writes=[f"cvk{i}"], dma=True, detached=True, ring="cv")
                    conv_keys.append(f"cvk{i}")
                    cvs["i"] = i + 1
        else:
            def conv_batch(n):
                return

        if "A" in passes:
            with ExitStack() as sa:
                WinA = sbt(sa, "WinA", [128, 8, 1536], BF16)
                Wglu = sbt(sa, "Wglu", [128, 4, 512], BF16)
                Wsso = sbt(sa, "Wsso", [128, 4, 1024], BF16)
                n1g_t = sbt(sa, "n1gA", [128, D], F32)
                sv_ = sbt(sa, "s5v", [128, 2, 4], F32)
                sm = sbt(sa, "s5sm", [128, 24, 16], F32)
                cosT = sbt(sa, "cosT", [128, 16, 128], F32)
                sinT = sbt(sa, "sinT", [128, 16, 128], F32)
                BwT = sbt(sa, "BwT", [128, 2, 16, 128], BF16)
                CwT = sbt(sa, "CwT", [128, 2, 16, 128], BF16)
                DEC = sbt(sa, "DEC", [128, 16, 128], F32)
                sa_outer = sa
                sa = ExitStack()
                sa.__enter__()
                stg = sbt(sa, "stgA", [128, 2, 2048], F32)
                load_weight(sa, "WinA", w_in, 1024, 0, 512, WinA, 0, stg, "a")
                load_weight(sa, "WinA", w_in, 1024, 2560, 3584, WinA, 512, stg, "a")
                load_weight(sa, "Wglu", w_glu, 512, 0, 512, Wglu, 0, stg, "a")
                load_weight(sa, "Wsso", w_sso, 512, 0, 1024, Wsso, 0, stg, "a")
                DM(lambda e: e.dma_start(out=n1g_t[:], in_=n1g), [], ["n1g"])
                pa = sbt(sa, "s5a", [128, 3, 16], F32)
                bw = sbt(sa, "s5bw", [128, 2, 16, 128], F32)
                cw = sbt(sa, "s5cw", [128, 2, 16, 128], F32)
                DM(lambda e: e.dma_start(out=pa[:], in_=s5a), [], ["pa"])
                DM(lambda e: e.dma_start(out=bw[:], in_=s5bw), [], ["bw"])
                DM(lambda e: e.dma_start(out=cw[:], in_=s5cw), [], ["cw"])
                DM(lambda e: e.dma_start(out=sv_[:], in_=s5v), [], ["s5v"])
                DT_, LR, TH, RHO, STH, CTH, AR_, AI_, NR, DEN, FRE, FIM, T1, T2, CR128, SR128, CR16, SR16, TH128, TH16, FIMN = range(21)
                are = pa[:, 0, :]
                aim = pa[:, 1, :]
                A(lambda e: e.activation(out=sm[:, DT_, :], in_=pa[:, 2, :], func=AF.Exp), ["pa"], ["sm_dt"])
                V(lambda e: e.tensor_tensor(out=sm[:, LR, :], in0=are, in1=sm[:, DT_, :], op=ALU.mult), ["pa", "sm_dt"], ["sm_lr"])
                V(lambda e: e.tensor_tensor(out=sm[:, TH, :], in0=aim, in1=sm[:, DT_, :], op=ALU.mult), ["pa", "sm_dt"], ["sm_th"])
                A(lambda e: e.activation(out=sm[:, RHO, :], in_=sm[:, LR, :], func=AF.Exp), ["sm_lr"], ["sm_rho"])
                V(lambda e: e.tensor_scalar(out=sm[:, TH128, :], in0=sm[:, TH, :], scalar1=128.0, scalar2=None, op0=ALU.mult), ["sm_th"], ["sm_th128"])
                V(lambda e: e.tensor_scalar(out=sm[:, TH16, :], in0=sm[:, TH, :], scalar1=16.0, scalar2=None, op0=ALU.mult), ["sm_th"], ["sm_th16"])

                thT = sbt(sa, "thT", [128, 16, 128], F32)
                scs = [sbt(sa, f"scs{i}", [128, 2048], F32) for i in range(4)]
                sci = sbt(sa, "sci", [128, 2048], I32)

                def sincos(x_ap, kx, Fn, sin_ap, ksin, cos_ap, kcos, tag):
                    k_ = scs[0][:, :Fn]; y_ = scs[1][:, :Fn]; s1 = scs[2][:, :Fn]; s2 = scs[3][:, :Fn]; ki = sci[:, :Fn]
                    shp = list(x_ap.shape)
                    def vw(ap):
                        if len(shp) == 3:
                            return ap.rearrange("p (a b) -> p a b", a=shp[1])
                        return ap
                    V(lambda e: e.tensor_scalar(out=vw(k_), in0=x_ap, scalar1=1.0 / TWO_PI, scalar2=None, op0=ALU.mult), [kx], ["scs0"])
                    V(lambda e: e.tensor_copy(out=ki, in_=k_), ["scs0"], ["sci"])
                    V(lambda e: e.tensor_copy(out=k_, in_=ki), ["sci"], ["scs0"])
                    V(lambda e: e.scalar_tensor_tensor(out=vw(y_), in0=vw(k_), scalar=-CW1, in1=x_ap, op0=ALU.mult, op1=ALU.add), ["scs0", kx], ["scs1"])
                    V(lambda e: e.scalar_tensor_tensor(out=y_, in0=k_, scalar=-CW2, in1=y_, op0=ALU.mult, op1=ALU.add), ["scs0", "scs1"], ["scs1"])
                    A(lambda e: e.activation(out=s1, in_=y_, func=AF.Sin, scale=0.5), ["scs1"], ["scs2"])
                    A(lambda e: e.activation(out=s2, in_=y_, func=AF.Sin, scale=0.25), ["scs1"], ["scs3"])
                    V(lambda e: e.tensor_tensor(out=k_, in0=s2, in1=s2, op=ALU.mult), ["scs3"], ["scs0"])
                    V(lambda e: e.tensor_scalar(out=k_, in0=k_, scalar1=-2.0, scalar2=1.0, op0=ALU.mult, op1=ALU.add), ["scs0"], ["scs0"])
                    V(lambda e: e.scalar_tensor_tensor(out=sin_ap, in0=vw(s1), scalar=2.0, in1=vw(k_), op0=ALU.mult, op1=ALU.mult), ["scs2", "scs0"], [ksin])
                    V(lambda e: e.tensor_tensor(out=y_, in0=s1, in1=s1, op=ALU.mult), ["scs2"], ["scs1"])
                    V(lambda e: e.tensor_scalar(out=cos_ap, in0=vw(y_), scalar1=-2.0, scalar2=1.0, op0=ALU.mult, op1=ALU.add), ["scs1"], [kcos])

                sincos(sm[:, TH, :], "sm_th", 16, sm[:, STH, :], "sm_sth", sm[:, CTH, :], "sm_cth", "a")
                sincos(sm[:, TH128, :], "sm_th128", 16, sm[:, SR128, :], "sm_sr128", sm[:, CR128, :], "sm_cr128", "b")
                sincos(sm[:, TH16, :], "sm_th16", 16, sm[:, SR16, :], "sm_sr16", sm[:, CR16, :], "sm_cr16", "c")
                for j in range(16):
                    V(lambda e, j=j: e.tensor_scalar(out=thT[:, j, :], in0=iota_f, scalar1=sm[:, TH, j:j + 1], scalar2=None, op0=ALU.mult), ["cstf", "sm_th"], ["thT"])
                sincos(thT[:, :, :], "thT", 2048, sinT[:, :, :], "sinT", cosT[:, :, :], "cosT", "d")
                V(lambda e: e.tensor_tensor(out=sm[:, AR_, :], in0=sm[:, RHO, :], in1=sm[:, CTH, :], op=ALU.mult), ["sm_rho", "sm_cth"], ["sm_ar"])
                V(lambda e: e.tensor_tensor(out=sm[:, AI_, :], in0=sm[:, RHO, :], in1=sm[:, STH, :], op=ALU.mult), ["sm_rho", "sm_sth"], ["sm_ai"])
                V(lambda e: e.tensor_scalar(out=sm[:, NR, :], in0=sm[:, AR_, :], scalar1=-1.0, scalar2=None, op0=ALU.add), ["sm_ar"], ["sm_nr"])
                V(lambda e: e.tensor_tensor(out=sm[:, T1, :], in0=are, in1=are, op=ALU.mult), ["pa"], ["sm_t1"])
                V(lambda e: e.tensor_tensor(out=sm[:, T2, :], in0=aim, in1=aim, op=ALU.mult), ["pa"], ["sm_t2"])
                V(lambda e: e.tensor_tensor(out=sm[:, DEN, :], in0=sm[:, T1, :], in1=sm[:, T2, :], op=ALU.add), ["sm_t1", "sm_t2"], ["sm_den"])
                V(lambda e: e.reciprocal(out=sm[:, DEN, :], in_=sm[:, DEN, :]), ["sm_den"], ["sm_den"])
                V(lambda e: e.tensor_tensor(out=sm[:, T1, :], in0=sm[:, NR, :], in1=are, op=ALU.mult), ["sm_nr", "pa", "sm_den"], ["sm_t1"])
                V(lambda e: e.tensor_tensor(out=sm[:, T2, :], in0=sm[:, AI_, :], in1=aim, op=ALU.mult), ["sm_ai", "pa", "sm_den"], ["sm_t2"])
                V(lambda e: e.tensor_tensor(out=sm[:, FRE, :], in0=sm[:, T1, :], in1=sm[:, T2, :], op=ALU.add), ["sm_t1", "sm_t2"], ["sm_fre"])
                V(lambda e: e.tensor_tensor(out=sm[:, FRE, :], in0=sm[:, FRE, :], in1=sm[:, DEN, :], op=ALU.mult), ["sm_fre", "sm_den"], ["sm_fre"])
                V(lambda e: e.tensor_tensor(out=sm[:, T1, :], in0=sm[:, AI_, :], in1=are, op=ALU.mult), ["sm_ai", "pa", "sm_fre"], ["sm_t1"])
                V(lambda e: e.tensor_tensor(out=sm[:, T2, :], in0=sm[:, NR, :], in1=aim, op=ALU.mult), ["sm_nr", "pa", "sm_fre"], ["sm_t2"])
                V(lambda e: e.tensor_tensor(out=sm[:, FIM, :], in0=sm[:, T1, :], in1=sm[:, T2, :], op=ALU.subtract), ["sm_t1", "sm_t2"], ["sm_fim"])
                V(lambda e: e.tensor_tensor(out=sm[:, FIM, :], in0=sm[:, FIM, :], in1=sm[:, DEN, :], op=ALU.mult), ["sm_fim", "sm_den"], ["sm_fim"])
                bb = sbt(sa, "bb", [128, 3, 128], F32)
                for j in range(16):
                    V(lambda e, j=j: e.tensor_scalar(out=bb[:, 2, :], in0=bw[:, 1, j, :], scalar1=sm[:, FIM, j:j + 1], scalar2=None, op0=ALU.mult), ["bw", "sm_fim"], ["bb2"])
                    V(lambda e, j=j: e.scalar_tensor_tensor(out=bb[:, 0, :], in0=bw[:, 0, j, :], scalar=sm[:, FRE, j:j + 1], in1=bb[:, 2, :], op0=ALU.mult, op1=ALU.subtract), ["bw", "sm_fre", "bb2"], ["bb0"])
                    V(lambda e, j=j: e.tensor_scalar(out=bb[:, 2, :], in0=bw[:, 1, j, :], scalar1=sm[:, FRE, j:j + 1], scalar2=None, op0=ALU.mult), ["bw", "sm_fre", "bb0"], ["bb2"])
                    V(lambda e, j=j: e.scalar_tensor_tensor(out=bb[:, 1, :], in0=bw[:, 0, j, :], scalar=sm[:, FIM, j:j + 1], in1=bb[:, 2, :], op0=ALU.mult, op1=ALU.add), ["bw", "sm_fim", "bb2"], ["bb1"])
                    for ri in range(2):
                        T(lambda e, ri=ri: e.transpose(out=ps[0][:, ri * 128:(ri + 1) * 128], in_=bb[:, ri, :], identity=identf), [f"bb{ri}", "cstf"], ["ps0"])
                    A(lambda e, j=j: e.activation(out=BwT[:, :, j, :], in_=ps[0][:, 0:256].rearrange("p (r m) -> p r m", r=2), func=AF.Copy), ["ps0"], ["BwT"])
                V(lambda e: e.tensor_copy(out=CwT[:, 0, :, :], in_=cw[:, 0, :, :]), ["cw"], ["CwT"])
                V(lambda e: e.tensor_scalar(out=CwT[:, 1, :, :], in0=cw[:, 1, :, :], scalar1=-1.0, scalar2=None, op0=ALU.mult), ["cw"], ["CwT"])
                for j in range(16):
                    V(lambda e, j=j: e.tensor_scalar(out=DEC[:, j, :], in0=ones_f, scalar1=sm[:, RHO, j:j + 1], scalar2=None, op0=ALU.mult), ["cstf", "sm_rho"], ["DEC"])
                V(lambda e: e.memset(DEC[:, :, 0:1], 0.0), ["DEC"], ["DEC"])

                S.emit()
                sa.__exit__(None, None, None)
                sa = sa_outer
                xt = sbt(sa, "xtA", [128, 4, D], F32)
                xnb = sbt(sa, "xnbA", [128, D], BF16)
                xnT = sbt(sa, "xnTA", [128, 8, 512], BF16)
                uT = sbt(sa, "uT", [128, 4, 512], BF16)
                sgs = sbt(sa, "sgs", [128, 8, 512], BF16)
                yg = sbt(sa, "yg", [128, 4, 512], BF16)
                y2 = sbt(sa, "y2", [128, 4, 512], BF16)
                mixT = sbt(sa, "mixTA", [128, 8, 512], BF16)
                scrA = dict(tag="nA", junk=sbt(sa, "junkA", [128, D], F32), ssq=sbt(sa, "ssqA", [128, 1], F32), rs=sbt(sa, "rsA", [128, 3], F32))
                P1f = sbt(sa, "P1f", [128, 1024], F32)
                P2f = sbt(sa, "P2f", [128, 1024], F32)
                Xf = sbt(sa, "Xf", [128, 2, 4, 128], F32)
                RI = sbt(sa, "RI", [128, 2, 16], F32)
                SbB = [sbt(sa, f"SbB{i}", [128, 2, 4, 128], BF16) for i in range(2)]
                P1 = [P1f[:, i * 256:(i + 1) * 256].rearrange("p (r t) -> p r t", r=2) for i in range(2)]
                P2 = [P2f[:, i * 256:(i + 1) * 256].rearrange("p (r t) -> p r t", r=2) for i in range(2)]
                X_ = [Xf[:, :, i, :] for i in range(2)]
                R_ = sbt(sa, "R_", [128, 2, 16, 128], F32)
                Rinit = sbt(sa, "Rinit", [128, 2, 16], F32)
                rt = sbt(sa, "rt", [128, 4, 16], F32)
                Sb = [sbt(sa, f"Sb{i}", [128, 2, 128], BF16) for i in range(2)]
                yv = sbt(sa, "yv", [128, 128], F32)
                sgt = sbt(sa, "sgt", [128, 512], BF16)
                V(lambda e: e.memset(Rinit[:], 0.0), [], ["Rinit"])
                V(lambda e: e.memset(RI[:], 0.0), [], ["RI"])

                for blk in range(nblkA):
                    conv_batch(10)
                    N = 16 if blk == 0 else 512
                    tok0 = 0 if blk == 0 else 16 + 512 * (blk - 1)
                    ntile = 1 if blk == 0 else 4
                    for i in range(ntile):
                        P = min(128, N)
                        r0 = tok0 + i * 128
                        DM(lambda e, i=i, P=P, r0=r0: e.dma_start(out=xt[:P, i, :], in_=xh[r0:r0 + P, :]), [], [f"xtA{i}"])
                        rms_rows(P, xt[:P, i, :], n1g_t[:P, :], [(xnb[:P, :], "xnbA")], f"xtA{i}", "nA", scrA)
                        transpose_rows(P, xnb[:P, :], "xnbA", xnT[:, :, i * 128:i * 128 + P], "xnTA")
                    for q in range(12):
                        bank = ps[1 + q % 2]; kb = f"ps{1 + q % 2}"
                        for k in range(8):
                            T(lambda e, q=q, k=k, bank=bank: e.matmul(bank[:, :N], lhsT=WinA[:, k, q * 128:(q + 1) * 128], rhs=xnT[:, k, :N], start=(k == 0), stop=(k == 7)),
                              ["WinA", "xnTA"], [kb])
                        if q < 4:
                            A(lambda e, q=q, bank=bank: e.activation(out=uT[:, q, :N], in_=bank[:, :N], func=AF.Copy), [kb], ["uT"])
                        else:
                            A(lambda e, q=q, bank=bank: e.activation(out=sgs[:, q - 4, :N], in_=bank[:, :N], func=AF.Sigmoid), [kb], ["sgs"])
                    nsc = 1 if blk == 0 else 4
                    Ts = 16 if blk == 0 else 128
                    allR = [f"R_{j}" for j in range(16)]
                    for sc in range(nsc):
                        t0 = sc * Ts
                        if Ts == 128:
                            for q in range(4):
                                buv = psS[:, :].rearrange("p (j r t) -> p j r t", j=4, r=2)
                                for jj in range(4):
                                    j = 4 * q + jj
                                    for ri in range(2):
                                        T(lambda e, j=j, jj=jj, ri=ri, q=q: e.matmul(psS[:, jj * 256 + ri * 128:jj * 256 + ri * 128 + 128], lhsT=BwT[:, ri, j, :], rhs=uT[:, q, t0:t0 + 128], start=True, stop=True),
                                          ["BwT", "uT"], ["ps3", "ps4"])
                                cosb = cosT[:, 4 * q:4 * q + 4, :].unsqueeze(2).to_broadcast([128, 4, 2, 128])
                                sinb = sinT[:, 4 * q:4 * q + 4, :].unsqueeze(2).to_broadcast([128, 4, 2, 128])
                                P1v = P1f[:, :].rearrange("p (j r t) -> p j r t", j=4, r=2)
                                P2v = P2f[:, :].rearrange("p (j r t) -> p j r t", j=4, r=2)
                                V(lambda e, buv=buv, cosb=cosb, P1v=P1v: e.tensor_tensor(out=P1v, in0=buv, in1=cosb, op=ALU.mult), ["ps3", "ps4", "cosT"], ["P1"])
                                V(lambda e, buv=buv, sinb=sinb, P2v=P2v: e.tensor_tensor(out=P2v, in0=buv, in1=sinb, op=ALU.mult), ["ps3", "ps4", "sinT"], ["P2"])
                                V(lambda e, P1v=P1v, P2v=P2v: e.tensor_tensor(out=Xf[:, 0, :, :], in0=P1v[:, :, 0, :], in1=P2v[:, :, 1, :], op=ALU.add), ["P1", "P2"], ["X"])
                                V(lambda e, P1v=P1v, P2v=P2v: e.tensor_tensor(out=Xf[:, 1, :, :], in0=P1v[:, :, 1, :], in1=P2v[:, :, 0, :], op=ALU.subtract), ["P1", "P2"], ["X"])
                                V(lambda e, q=q: e.tensor_tensor(out=Xf[:, :, :, 0], in0=Xf[:, :, :, 0], in1=RI[:, :, 4 * q:4 * q + 4], op=ALU.add), ["X", "RI"], ["X"])
                                for ri in range(2):
                                    V(lambda e, ri=ri, q=q: e.tensor_tensor_scan(out=R_[:, ri, 4 * q:4 * q + 4, :].rearrange("p j t -> p (j t)"), data0=DEC[:, 4 * q:4 * q + 4, :].rearrange("p j t -> p (j t)"),
                                                                                data1=Xf[:, ri, :, :].rearrange("p j t -> p (j t)"), initial=0.0, op0=ALU.mult, op1=ALU.add),
                                      ["DEC", "X"], [f"R_{4 * q + jj}" for jj in range(4)])
                                Rv = R_[:, :, 4 * q:4 * q + 4, :]
                                cosb2 = cosT[:, 4 * q:4 * q + 4, :].unsqueeze(1).to_broadcast([128, 2, 4, 128])
                                sinb2 = sinT[:, 4 * q:4 * q + 4, :].unsqueeze(1).to_broadcast([128, 2, 4, 128])
                                Q1v = P1f[:, :].rearrange("p (r j t) -> p r j t", r=2, j=4)
                                Q2v = P2f[:, :].rearrange("p (r j t) -> p r j t", r=2, j=4)
                                kR = [f"R_{4 * q + jj}" for jj in range(4)]
                                V(lambda e, Rv=Rv, cosb2=cosb2, Q1v=Q1v: e.tensor_tensor(out=Q1v, in0=Rv, in1=cosb2, op=ALU.mult), kR + ["cosT"], ["P1"])
                                V(lambda e, Rv=Rv, sinb2=sinb2, Q2v=Q2v: e.tensor_tensor(out=Q2v, in0=Rv, in1=sinb2, op=ALU.mult), kR + ["sinT"], ["P2"])
                                sbq = SbB[q % 2]; ksb = f"SbB{q % 2}"
                                V(lambda e, Q1v=Q1v, Q2v=Q2v, sbq=sbq: e.tensor_tensor(out=sbq[:, 0, :, :], in0=Q1v[:, 0, :, :], in1=Q2v[:, 1, :, :], op=ALU.subtract), ["P1", "P2"], [ksb])
                                V(lambda e, Q1v=Q1v, Q2v=Q2v, sbq=sbq: e.tensor_tensor(out=sbq[:, 1, :, :], in0=Q2v[:, 0, :, :], in1=Q1v[:, 1, :, :], op=ALU.add), ["P1", "P2"], [ksb])
                                for jj in range(4):
                                    j = 4 * q + jj
                                    for ri in range(2):
                                        T(lambda e, j=j, jj=jj, ri=ri, q=q, sbq=sbq: e.matmul(ps[5][:, q * 128:(q + 1) * 128], lhsT=CwT[:, ri, j, :], rhs=sbq[:, ri, jj, :], start=(jj == 0 and ri == 0), stop=(jj == 3 and ri == 1)),
                                          ["CwT", ksb], ["ps5"])
                                V(lambda e, q=q: e.scalar_tensor_tensor(out=yv[:, :], in0=uT[:, q, t0:t0 + 128], scalar=sv_[:, 0, q:q + 1], in1=ps[5][:, q * 128:(q + 1) * 128], op0=ALU.mult, op1=ALU.add),
                                  ["uT", "s5v", "ps5"], ["yv"])
                                A(lambda e, q=q: e.activation(out=yg[:, q, t0:t0 + 128], in_=yv[:, :], func=AF.Gelu), ["yv"], ["yg"])
                        else:
                            for j in range(16):
                                q = j // 4
                                pb = j % 2
                                bank = ps[3 + pb]; kb = f"ps{3 + pb}"
                                buv = bank[:, 0:256].rearrange("p (r t) -> p r t", r=2)
                                T(lambda e, j=j, q=q, bank=bank: e.matmul(bank[:, 0:Ts], lhsT=BwT[:, 0, j, :], rhs=uT[:, q, t0:t0 + Ts], start=True, stop=True), ["BwT", "uT"], [kb])
                                T(lambda e, j=j, q=q, bank=bank: e.matmul(bank[:, 128:128 + Ts], lhsT=BwT[:, 1, j, :], rhs=uT[:, q, t0:t0 + Ts], start=True, stop=True), ["BwT", "uT"], [kb])
                                cosb = cosT[:, j, :Ts].unsqueeze(1).to_broadcast([128, 2, Ts])
                                sinb = sinT[:, j, :Ts].unsqueeze(1).to_broadcast([128, 2, Ts])
                                V(lambda e, buv=buv, pb=pb, cosb=cosb: e.tensor_tensor(out=P1[pb][:, :, :Ts], in0=buv[:, :, :Ts], in1=cosb, op=ALU.mult), [kb, "cosT"], ["P1"])
                                V(lambda e, buv=buv, pb=pb, sinb=sinb: e.tensor_tensor(out=P2[pb][:, :, :Ts], in0=buv[:, :, :Ts], in1=sinb, op=ALU.mult), [kb, "sinT"], ["P2"])
                                V(lambda e, pb=pb: e.tensor_tensor(out=X_[pb][:, 0, :Ts], in0=P1[pb][:, 0, :Ts], in1=P2[pb][:, 1, :Ts], op=ALU.add), ["P1", "P2"], ["X"])
                                V(lambda e, pb=pb: e.tensor_tensor(out=X_[pb][:, 1, :Ts], in0=P1[pb][:, 1, :Ts], in1=P2[pb][:, 0, :Ts], op=ALU.subtract), ["P1", "P2"], ["X"])
                                for ri in range(2):
                                    V(lambda e, pb=pb, ri=ri, j=j: e.tensor_tensor_scan(out=R_[:, ri, j, :Ts], data0=sm[:, RHO, j:j + 1].to_broadcast([128, Ts]), data1=X_[pb][:, ri, :Ts],
                                                                                       initial=Rinit[:, ri, j:j + 1], op0=ALU.mult, op1=ALU.add),
                                      ["sm_rho", "X", "Rinit"], [f"R_{j}"])
                                V(lambda e, pb=pb, j=j, cosb=cosb: e.tensor_tensor(out=P1[pb][:, :, :Ts], in0=R_[:, :, j, :Ts], in1=cosb, op=ALU.mult), [f"R_{j}", "cosT"], ["P1"])
                                V(lambda e, pb=pb, j=j, sinb=sinb: e.tensor_tensor(out=P2[pb][:, :, :Ts], in0=R_[:, :, j, :Ts], in1=sinb, op=ALU.mult), [f"R_{j}", "sinT"], ["P2"])
                                V(lambda e, pb=pb: e.tensor_tensor(out=Sb[pb][:, 0, :Ts], in0=P1[pb][:, 0, :Ts], in1=P2[pb][:, 1, :Ts], op=ALU.subtract), ["P1", "P2"], [f"Sb{pb}"])
                                V(lambda e, pb=pb: e.tensor_tensor(out=Sb[pb][:, 1, :Ts], in0=P2[pb][:, 0, :Ts], in1=P1[pb][:, 1, :Ts], op=ALU.add), ["P1", "P2"], [f"Sb{pb}"])
                                T(lambda e, j=j, q=q, pb=pb: e.matmul(ps[5][:, q * 128:q * 128 + Ts], lhsT=CwT[:, 0, j, :], rhs=Sb[pb][:, 0, :Ts], start=(j % 4 == 0), stop=False), ["CwT", f"Sb{pb}"], ["ps5"])
                                T(lambda e, j=j, q=q, pb=pb: e.matmul(ps[5][:, q * 128:q * 128 + Ts], lhsT=CwT[:, 1, j, :], rhs=Sb[pb][:, 1, :Ts], start=False, stop=(j % 4 == 3)), ["CwT", f"Sb{pb}"], ["ps5"])
                                if j % 4 == 3:
                                    V(lambda e, q=q: e.scalar_tensor_tensor(out=yv[:, :Ts], in0=uT[:, q, t0:t0 + Ts], scalar=sv_[:, 0, q:q + 1], in1=ps[5][:, q * 128:q * 128 + Ts], op0=ALU.mult, op1=ALU.add),
                                      ["uT", "s5v", "ps5"], ["yv"])
                                    A(lambda e, q=q: e.activation(out=yg[:, q, t0:t0 + Ts], in_=yv[:, :Ts], func=AF.Gelu), ["yv"], ["yg"])
                        cR = sm[:, CR16 if Ts == 16 else CR128, :]
                        sR = sm[:, SR16 if Ts == 16 else SR128, :]
                        kc = ["sm_cr16", "sm_sr16"] if Ts == 16 else ["sm_cr128", "sm_sr128"]
                        lre = R_[:, 0, :, Ts - 1]
                        lim = R_[:, 1, :, Ts - 1]
                        V(lambda e, cR=cR, lre=lre: e.tensor_tensor(out=rt[:, 0, :], in0=cR, in1=lre, op=ALU.mult), allR + kc, ["rt0"])
                        V(lambda e, sR=sR, lim=lim: e.tensor_tensor(out=rt[:, 1, :], in0=sR, in1=lim, op=ALU.mult), allR + kc, ["rt1"])
                        V(lambda e, sR=sR, lre=lre: e.tensor_tensor(out=rt[:, 2, :], in0=sR, in1=lre, op=ALU.mult), allR + kc, ["rt2"])
                        V(lambda e, cR=cR, lim=lim: e.tensor_tensor(out=rt[:, 3, :], in0=cR, in1=lim, op=ALU.mult), allR + kc, ["rt3"])
                        V(lambda e: e.tensor_tensor(out=Rinit[:, 0, :], in0=rt[:, 0, :], in1=rt[:, 1, :], op=ALU.subtract), ["rt0", "rt1"], ["Rinit"])
                        V(lambda e: e.tensor_tensor(out=Rinit[:, 1, :], in0=rt[:, 2, :], in1=rt[:, 3, :], op=ALU.add), ["rt2", "rt3"], ["Rinit"])
                        V(lambda e: e.tensor_tensor(out=RI[:, :, :], in0=Rinit[:, :, :], in1=sm[:, RHO, :].unsqueeze(1).to_broadcast([128, 2, 16]), op=ALU.mult), ["Rinit", "sm_rho"], ["RI"])
                    for qo in range(4):
                        bank = ps[1 + qo % 2]; kb = f"ps{1 + qo % 2}"
                        for q in range(4):
                            T(lambda e, q=q, qo=qo, bank=bank: e.matmul(bank[:, :N], lhsT=Wglu[:, q, qo * 128:(qo + 1) * 128], rhs=yg[:, q, :N], start=(q == 0), stop=(q == 3)), ["Wglu", "yg"], [kb])
                        A(lambda e, qo=qo, bank=bank: e.activation(out=sgt[:, :N], in_=bank[:, :N], func=AF.Sigmoid, bias=sv_[:, 1, qo:qo + 1]), [kb, "s5v"], ["sgt"])
                        V(lambda e, qo=qo: e.tensor_tensor(out=y2[:, qo, :N], in0=yg[:, qo, :N], in1=sgt[:, :N], op=ALU.mult), ["yg", "sgt"], ["y2"])
                    for o in range(8):
                        bank = ps[1 + o % 2]; kb = f"ps{1 + o % 2}"
                        for q in range(4):
                            T(lambda e, q=q, o=o, bank=bank: e.matmul(bank[:, :N], lhsT=Wsso[:, q, o * 128:(o + 1) * 128], rhs=y2[:, q, :N], start=(q == 0), stop=(q == 3)), ["Wsso", "y2"], [kb])
                        V(lambda e, o=o, bank=bank: e.tensor_tensor(out=mixT[:, o, :N], in0=bank[:, :N], in1=sgs[:, o, :N], op=ALU.mult), [kb, "sgs"], ["mixTA"])
                    DM(lambda e, tok0=tok0, N=N: e.dma_start(out=mixs.rearrange("(o p) t -> p o t", p=128)[:, :, tok0:tok0 + N], in_=mixT[:, :, :N]), ["mixTA"], ["mixs"])
                S.emit(final=(passes[-1] == "A"))

        if "B" in passes:
            with ExitStack() as sbk:
                WinB = sbt(sbk, "WinB", [128, 8, 3072], BF16)
                Wmlo = sbt(sbk, "Wmlo", [128, 8, 1024], BF16)
                Wout = sbt(sbk, "Wout", [128, 8, 1024], BF16)
                n1g_t = sbt(sbk, "n1gB", [128, D], F32)
                mlng_t = sbt(sbk, "mlngB", [128, D], F32)
                mlv_t = sbt(sbk, "mlv", [128, 6, 8], F32)
                bif_t = sbt(sbk, "bif", [128, 8], F32)
                bd = sbt(sbk, "bd", [128, 3, 8, 128], BF16)
                G12 = sbt(sbk, "G12", [128, 2, 8, 8], BF16)
                sbk_outer = sbk
                sbk = ExitStack()
                sbk.__enter__()
                stg = sbt(sbk, "stgB", [128, 2, 1024], F32)
                load_weight(sbk, "WinB", w_in, 1024, 512, 2560, WinB, 0, stg, "b")
                load_weight(sbk, "WinB", w_in, 1024, 3584, 4608, WinB, 2048, stg, "b")
                load_weight(sbk, "Wmlo", w_mlo, 1024, 0, 1024, Wmlo, 0, stg, "b")
                load_weight(sbk, "Wout", w_out, 1024, 0, 1024, Wout, 0, stg, "b")
                DM(lambda e: e.dma_start(out=n1g_t[:], in_=n1g), [], ["n1gB"])
                DM(lambda e: e.dma_start(out=mlng_t[:], in_=mlng), [], ["mlngB"])
                bdf = sbt(sbk, "bdf", [128, 5, 8, 128], F32)
                bdvf = sbt(sbk, "bdvf", [128, 8, 128], F32)
                wiff = sbt(sbk, "wiff", [128, 24, 8], F32)
                DM(lambda e: e.dma_start(out=mlv_t[:], in_=mlv), [], ["mlv"])
                DM(lambda e: e.dma_start(out=bdf[:], in_=mlbd), [], ["bdf"])
                DM(lambda e: e.dma_start(out=bdvf[:], in_=mlbdv), [], ["bdvf"])
                DM(lambda e: e.dma_start(out=wiff[:], in_=mlwif), [], ["wiff"])
                DM(lambda e: e.dma_start(out=bif_t[:], in_=mlbif), [], ["bif"])
                V(lambda e: e.tensor_copy(out=bd[:], in_=bdf[:, 0:3, :, :]), ["bdf"], ["bd"])
                for c in range(8):
                    T(lambda e, c=c: e.matmul(ps[0][:, c * 8:(c + 1) * 8], lhsT=bdf[:, 3, c, :], rhs=wiff[:, c, :], start=True, stop=False), ["bdf", "wiff"], ["ps0"])
                    T(lambda e, c=c: e.matmul(ps[0][:, c * 8:(c + 1) * 8], lhsT=bdf[:, 4, c, :], rhs=wiff[:, 8 + c, :], start=False, stop=True), ["bdf", "wiff"], ["ps0"])
                    T(lambda e, c=c: e.matmul(ps[0][:, 64 + c * 8:64 + (c + 1) * 8], lhsT=bdvf[:, c, :], rhs=wiff[:, 16 + c, :], start=True, stop=True), ["bdvf", "wiff"], ["ps0"])
                V(lambda e: e.tensor_copy(out=G12[:], in_=ps[0][:, 0:128].rearrange("p (a c n) -> p a c n", a=2, c=8)), ["ps0"], ["G12"])
                tri = cstf[:, 1, :]

                S.emit()
                sbk.__exit__(None, None, None)
                sbk = sbk_outer
                NB = 256
                xt = sbt(sbk, "xtB", [128, 2, D], F32)
                xnb = sbt(sbk, "xnbB", [128, D], BF16)
                xnT = sbt(sbk, "xnTB", [128, 8, NB], BF16)
                xmT = sbt(sbk, "xmT", [128, 8, 3 + NB], BF16)
                xcT = sbt(sbk, "xcT", [128, 8, NB], BF16)
                acc = sbt(sbk, "accB", [128, NB], F32)
                sgm = sbt(sbk, "sgm", [128, 8, NB], BF16)
                qT = sbt(sbk, "qT", [128, 8, NB], BF16)
                kT = sbt(sbk, "kT", [128, 8, NB], BF16)
                mixin = sbt(sbk, "mixin", [128, 8, NB], BF16)
                hmT = sbt(sbk, "hmT", [128, 8, NB], BF16)
                mixT = sbt(sbk, "mixTB", [128, 8, NB], BF16)
                tmpf = sbt(sbk, "tmpfB", [128, NB], F32)
                h1t = sbt(sbk, "h1tB", [128, D], F32)
                scrB = dict(tag="nB", junk=sbt(sbk, "junkB", [128, D], F32), ssq=sbt(sbk, "ssqB", [128, 1], F32), rs=sbt(sbk, "rsB", [128, 3], F32))
                sigz2 = [sbt(sbk, f"sigz{i}", [64, D], BF16) for i in range(2)]
                vaug2 = [sbt(sbk, f"vaug{i}", [64, 4, 257], BF16) for i in range(2)]
                vw2 = [sbt(sbk, f"vw{i}", [64, 4, 257], BF16) for i in range(2)]
                kk2 = [sbt(sbk, f"kk{i}", [64, D], BF16) for i in range(2)]
                gt2 = [sbt(sbk, f"gt{i}", [64, 8], F32) for i in range(2)]
                gs2 = [sbt(sbk, f"gs{i}", [128, 8, 4], F32) for i in range(2)]
                E1, NLF, EG, EB, TMP, KSC, WST, ENB = range(8)
                SpT2 = [sbt(sbk, f"SpT{i}", [64, 64], BF16) for i in range(2)]
                ho2 = [sbt(sbk, f"ho{i}", [64, 256], F32) for i in range(2)]
                hjunk = sbt(sbk, "hjunk", [64, 256], F32)
                hs2 = [sbt(sbk, f"hs{i}", [64, 8], F32) for i in range(2)]
                hn = sbt(sbk, "hn", [64, D], BF16)
                CT = sbt(sbk, "CT", [128, 8, 257], F32)
                CTb = sbt(sbk, "CTb", [128, 8, 257], BF16)
                V(lambda e: e.memset(CT[:], 0.0), [], [f"CT{h_}" for h_ in range(4)])
                V(lambda e: e.memset(CTb[:], 0.0), [], [f"CTb{h_}" for h_ in range(4)])
                V(lambda e: e.memset(xmT[:], 0.0), [], ["xmT"])
                for i_ in range(2):
                    V(lambda e, i_=i_: e.memset(vaug2[i_][:], 1.0), [], [f"vaug{i_}"])
                    V(lambda e, i_=i_: e.memset(gs2[i_][:], 0.0), [], [f"gs{i_}"])
                LN16 = math.log(16.0)

                for blk in range(nblkB):
                    conv_batch(10)
                    N = 16 if blk == 0 else NB
                    tok0 = 0 if blk == 0 else 16 + NB * (blk - 1)
                    ntile = 1 if blk == 0 else 2
                    for i in range(ntile):
                        P = min(128, N)
                        r0 = tok0 + i * 128
                        DM(lambda e, i=i, P=P, r0=r0: e.dma_start(out=xt[:P, i, :], in_=xh[r0:r0 + P, :]), [], [f"xtB{i}"])
                        rms_rows(P, xt[:P, i, :], n1g_t[:P, :], [(xnb[:P, :], "xnbB")], f"xtB{i}", "nB", scrB)
                        transpose_rows(P, xnb[:P, :], "xnbB", xnT[:, :, i * 128:i * 128 + P], "xnTB")
                    DM(lambda e, tok0=tok0, N=N: e.dma_start(out=mixin[:, :, :N], in_=mixs.rearrange("(o p) t -> p o t", p=128)[:, :, tok0:tok0 + N]), ["mixs"], ["mixin"])
                    for c in range(16):
                        bank = ps[1 + c % 2]; kb = f"ps{1 + c % 2}"
                        col0 = c * 128 if c < 8 else 2048 + (c - 8) * 128
                        for k in range(8):
                            T(lambda e, k=k, col0=col0, bank=bank: e.matmul(bank[:, :N], lhsT=WinB[:, k, col0:col0 + 128], rhs=xnT[:, k, :N], start=(k == 0), stop=(k == 7)), ["WinB", "xnTB"], [kb])
                        if c < 8:
                            A(lambda e, c=c, bank=bank: e.activation(out=xmT[:, c, 3:3 + N], in_=bank[:, :N], func=AF.Copy), [kb], ["xmT"])
                        else:
                            A(lambda e, c=c, bank=bank: e.activation(out=sgm[:, c - 8, :N], in_=bank[:, :N], func=AF.Sigmoid), [kb], ["sgm"])
                    for c in range(8):
                        V(lambda e, c=c: e.tensor_scalar(out=acc[:, :N], in0=xmT[:, c, 0:N], scalar1=mlv_t[:, 0, c:c + 1], scalar2=None, op0=ALU.mult), ["xmT", "mlv"], ["accB"])
                        for j in range(1, 4):
                            V(lambda e, c=c, j=j: e.scalar_tensor_tensor(out=acc[:, :N], in0=xmT[:, c, j:j + N], scalar=mlv_t[:, j, c:c + 1], in1=acc[:, :N], op0=ALU.mult, op1=ALU.add), ["xmT", "mlv", "accB"], ["accB"])
                        A(lambda e, c=c: e.activation(out=xcT[:, c, :N], in_=acc[:, :N], func=AF.Silu, bias=mlv_t[:, 4, c:c + 1]), ["accB", "mlv"], ["xcT"])
                    for c in range(16):
                        bank = ps[1 + c % 2]; kb = f"ps{1 + c % 2}"
                        w = c // 8; cc = c % 8
                        T(lambda e, w=w, cc=cc, bank=bank: e.matmul(bank[:, :N], lhsT=bd[:, w, cc, :], rhs=xcT[:, cc, :N], start=True, stop=True), ["bd", "xcT"], [kb])
                        dst = qT if w == 0 else kT
                        A(lambda e, cc=cc, bank=bank, dst=dst: e.activation(out=dst[:, cc, :N], in_=bank[:, :N], func=AF.Copy), [kb], ["qT" if w == 0 else "kT"])
                    Lc = 16 if blk == 0 else 64

                    def chunk_pre(ch):
                        pp = ch % 2
                        sigzP, vaugP, vw_P, kkP, gtP, gsP = sigz2[pp], vaug2[pp], vw2[pp], kk2[pp], gt2[pp], gs2[pp]
                        o0 = ch * Lc
                        for half in range(2):
                            bank = ps[1 + half]; kb = f"ps{1 + half}"
                            for k in range(8):
                                T(lambda e, k=k, half=half, bank=bank: e.matmul(bank[:Lc, :], lhsT=xnT[:, k, o0:o0 + Lc], rhs=WinB[:, k, 1024 + half * 512:1024 + (half + 1) * 512], start=(k == 0), stop=(k == 7)), ["xnTB", "WinB"], [kb])
                            A(lambda e, half=half, bank=bank: e.activation(out=sigzP[:Lc, half * 512:(half + 1) * 512], in_=bank[:Lc, :], func=AF.Sigmoid), [kb], [f"sigz{pp}"])
                        for c in range(8):
                            T(lambda e, c=c: e.matmul(ps[0][:Lc, 0:8], lhsT=xcT[:, c, o0:o0 + Lc], rhs=G12[:, 0, c, :], start=(c == 0), stop=False), ["xcT", "G12"], ["ps0"])
                        for c in range(8):
                            T(lambda e, c=c: e.matmul(ps[0][:Lc, 0:8], lhsT=xmT[:, c, 3 + o0:3 + o0 + Lc], rhs=G12[:, 1, c, :], start=False, stop=(c == 7)), ["xmT", "G12"], ["ps0"])
                        V(lambda e: e.tensor_tensor(out=gtP[:Lc, :], in0=ps[0][:Lc, 0:8], in1=bif_t[:Lc, :], op=ALU.add), ["ps0", "bif"], [f"gt{pp}"])
                        A(lambda e: e.activation(out=gsP[:Lc, E1, :], in_=gtP[:Lc, 4:8], func=AF.Exp, scale=-1.0), [f"gt{pp}"], [f"gs_e1{pp}"])
                        A(lambda e: e.activation(out=gsP[:Lc, NLF, :], in_=gsP[:Lc, E1, :], func=AF.Ln, bias=1.0), [f"gs_e1{pp}"], [f"gs_nlf{pp}"])
                        T(lambda e: e.matmul(ps[0][:Lc, 8:12], lhsT=tri[:Lc, :Lc], rhs=gsP[:Lc, NLF, :], start=True, stop=True), ["cstf", f"gs_nlf{pp}"], ["ps0"])
                        T(lambda e: e.matmul(ps[0][:, 12:16], lhsT=ones_f[:Lc, :], rhs=gsP[:Lc, NLF, :], start=True, stop=True), ["cstf", f"gs_nlf{pp}"], ["ps0"])
                        A(lambda e: e.activation(out=gsP[:, EG, :], in_=ps[0][:, 12:16], func=AF.Exp, scale=-1.0), ["ps0"], [f"gs_eg{pp}"])
                        A(lambda e: e.activation(out=gsP[:Lc, ENB, :], in_=ps[0][:Lc, 8:12], func=AF.Exp), ["ps0"], [f"gs_enb{pp}"])
                        V(lambda e: e.tensor_tensor(out=gsP[:Lc, TMP, :], in0=ps[0][:Lc, 8:12], in1=gtP[:Lc, 0:4], op=ALU.add), ["ps0", f"gt{pp}"], [f"gs_tmp{pp}"])
                        A(lambda e: e.activation(out=gsP[:Lc, KSC, :], in_=gsP[:Lc, TMP, :], func=AF.Exp, bias=-LN16), [f"gs_tmp{pp}"], [f"gs_ksc{pp}"])
                        V(lambda e: e.tensor_tensor(out=gsP[:Lc, WST, :], in0=gsP[:Lc, KSC, :], in1=gsP[:Lc, EG, :], op=ALU.mult), [f"gs_ksc{pp}", f"gs_eg{pp}"], [f"gs_wst{pp}"])
                        for half in range(2):
                            bank = ps[1 + half]; kb = f"ps{1 + half}"
                            for c4 in range(4):
                                c = half * 4 + c4
                                T(lambda e, c=c, c4=c4, bank=bank: e.matmul(bank[:Lc, c4 * 128:(c4 + 1) * 128], lhsT=xmT[:, c, 3 + o0:3 + o0 + Lc], rhs=bd[:, 2, c, :], start=True, stop=True), ["xmT", "bd"], [kb])
                            A(lambda e, half=half, bank=bank: e.activation(out=vaugP[:Lc, half * 2:half * 2 + 2, 0:256], in_=bank[:Lc, :].rearrange("p (h d) -> p h d", h=2), func=AF.Copy), [kb], [f"vaug{pp}"])
                        for h in range(4):
                            V(lambda e, h=h: e.tensor_scalar(out=vw_P[:Lc, h, :], in0=vaugP[:Lc, h, :], scalar1=gsP[:Lc, WST, h:h + 1], scalar2=None, op0=ALU.mult), [f"vaug{pp}", f"gs_wst{pp}"], [f"vw{pp}"])
                        for half in range(2):
                            bank = ps[1 + half]; kb = f"ps{1 + half}"
                            for c4 in range(4):
                                c = half * 4 + c4
                                T(lambda e, c=c, c4=c4, bank=bank: e.matmul(bank[:Lc, c4 * 128:(c4 + 1) * 128], lhsT=xcT[:, c, o0:o0 + Lc], rhs=bd[:, 1, c, :], start=True, stop=True), ["xcT", "bd"], [kb])
                            A(lambda e, half=half, bank=bank: e.activation(out=kkP[:Lc, half * 512:(half + 1) * 512], in_=bank[:Lc, :], func=AF.Copy), [kb], [f"kk{pp}"])

                    def chunk_heads(ch):
                        pp = ch % 2
                        o0 = ch * Lc
                        sigzP, vaugP, vw_P, kkP, gtP, gsP = sigz2[pp], vaug2[pp], vw2[pp], kk2[pp], gt2[pp], gs2[pp]
                        def hs1_(h):
                            c0 = 2 * h
                            SpTh, hsh, hoh = SpT2[h % 2], hs2[h % 2], ho2[h % 2]
                            T(lambda e, c0=c0: e.matmul(ps[3][:Lc, :Lc], lhsT=kT[:, c0, o0:o0 + Lc], rhs=qT[:, c0, o0:o0 + Lc], start=True, stop=False), ["kT", "qT"], ["ps3"])
                            T(lambda e, c0=c0: e.matmul(ps[3][:Lc, :Lc], lhsT=kT[:, c0 + 1, o0:o0 + Lc], rhs=qT[:, c0 + 1, o0:o0 + Lc], start=False, stop=True), ["kT", "qT"], ["ps3"])
                            V(lambda e, h=h: e.scalar_tensor_tensor(out=SpTh[:Lc, :Lc], in0=ps[3][:Lc, :Lc], scalar=gsP[:Lc, KSC, h:h + 1], in1=maskT[:Lc, :Lc], op0=ALU.mult, op1=ALU.mult), ["ps3", f"gs_ksc{pp}", "cstf"], [f"SpT{h % 2}"])
                            T(lambda e, h=h: e.matmul(ps[4][:Lc, 0:257], lhsT=SpTh[:Lc, :Lc], rhs=vaugP[:Lc, h, :], start=True, stop=False), [f"SpT{h % 2}", f"vaug{pp}"], ["ps4"])
                            T(lambda e, h=h, c0=c0: e.matmul(ps[4][:Lc, 0:257], lhsT=qT[:, c0, o0:o0 + Lc], rhs=CTb[:, c0, :], start=False, stop=False), ["qT", f"CTb{h}"], ["ps4"])
                            T(lambda e, h=h, c0=c0: e.matmul(ps[4][:Lc, 0:257], lhsT=qT[:, c0 + 1, o0:o0 + Lc], rhs=CTb[:, c0 + 1, :], start=False, stop=True), ["qT", f"CTb{h}"], ["ps4"])
                        def hs2_(h):
                            c0 = 2 * h
                            SpTh, hsh, hoh = SpT2[h % 2], hs2[h % 2], ho2[h % 2]
                            V(lambda e, h=h: e.tensor_tensor(out=hsh[:Lc, 0:1], in0=ps[4][:Lc, 256:257], in1=gsP[:Lc, ENB, h:h + 1], op=ALU.max), ["ps4", f"gs_enb{pp}"], [f"hs0_{h % 2}"])
                            V(lambda e, h=h: e.scalar_tensor_tensor(out=hsh[:Lc, 1:2], in0=ps[4][:Lc, 256:257], scalar=-1.0, in1=hsh[:Lc, 0:1], op0=ALU.mult, op1=ALU.max), ["ps4", f"hs0_{h % 2}"], [f"hs1_{h % 2}"])
                            V(lambda e: e.reciprocal(out=hsh[:Lc, 3:4], in_=hsh[:Lc, 1:2]), [f"hs1_{h % 2}"], [f"hs3_{h % 2}"])
                            V(lambda e, h=h: e.scalar_tensor_tensor(out=hoh[:Lc, :], in0=ps[4][:Lc, 0:256], scalar=hsh[:Lc, 3:4], in1=sigzP[:Lc, h * 256:(h + 1) * 256], op0=ALU.mult, op1=ALU.mult), ["ps4", f"hs3_{h % 2}", f"sigz{pp}"], [f"ho{h % 2}"])
                            A(lambda e: e.activation(out=hjunk[:Lc, :], in_=hoh[:Lc, :], func=AF.Square, accum_out=hsh[:Lc, 4:5]), [f"ho{h % 2}"], ["hjunk", f"hs4_{h % 2}"])
                            A(lambda e: e.activation(out=hsh[:Lc, 6:7], in_=hsh[:Lc, 4:5], func=AF.Ln, scale=1.0 / 256.0, bias=EPS), [f"hs4_{h % 2}"], [f"hs6_{h % 2}"])
                            A(lambda e: e.activation(out=hsh[:Lc, 7:8], in_=hsh[:Lc, 6:7], func=AF.Exp, scale=-0.5), [f"hs6_{h % 2}"], [f"hs7_{h % 2}"])
                        def hs3_(h):
                            c0 = 2 * h
                            SpTh, hsh, hoh = SpT2[h % 2], hs2[h % 2], ho2[h % 2]
                            V(lambda e, h=h: e.scalar_tensor_tensor(out=hn[:Lc, h * 256:(h + 1) * 256], in0=hoh[:Lc, :], scalar=hsh[:Lc, 7:8], in1=mlng_t[:Lc, h * 256:(h + 1) * 256], op0=ALU.mult, op1=ALU.mult), [f"ho{h % 2}", f"hs7_{h % 2}", "mlngB"], ["hn"])
                            for dc in range(2):
                                bank = ps[5 + dc]; kb = f"ps{5 + dc}"
                                T(lambda e, h=h, dc=dc, bank=bank: e.matmul(bank[:, 0:257], lhsT=kkP[:Lc, (2 * h + dc) * 128:(2 * h + dc + 1) * 128], rhs=vw_P[:Lc, h, :], start=True, stop=True), [f"kk{pp}", f"vw{pp}"], [kb])
                                V(lambda e, h=h, dc=dc, bank=bank: e.scalar_tensor_tensor(out=CT[:, 2 * h + dc, :], in0=CT[:, 2 * h + dc, :], scalar=gsP[:, EG, h:h + 1], in1=bank[:, 0:257], op0=ALU.mult, op1=ALU.add), [f"CT{h}", f"gs_eg{pp}", kb], [f"CT{h}"])
                                A(lambda e, h=h, dc=dc: e.activation(out=CTb[:, 2 * h + dc, :], in_=CT[:, 2 * h + dc, :], func=AF.Copy), [f"CT{h}"], [f"CTb{h}"])

                        for st_, h_ in ((1, 0), (2, 0), (1, 1), (2, 1), (3, 0), (1, 2), (2, 2), (3, 1), (1, 3), (2, 3), (3, 2), (3, 3)):
                            (hs1_, hs2_, hs3_)[st_ - 1](h_)
                        for c in range(8):
                            T(lambda e, c=c: e.transpose(out=psb[:, c * 64:c * 64 + Lc], in_=hn[:Lc, c * 128:(c + 1) * 128], identity=identb[:Lc, :Lc]), ["hn", "identb"], ["psb"])
                        for c in range(8):
                            V(lambda e, c=c: e.scalar_tensor_tensor(out=hmT[:, c, o0:o0 + Lc], in0=xcT[:, c, o0:o0 + Lc], scalar=mlv_t[:, 5, c:c + 1], in1=psb[:, c * 64:c * 64 + Lc], op0=ALU.mult, op1=ALU.add), ["xcT", "mlv", "psb"], ["hmT"])

                    nch = N // Lc
                    chunk_pre(0)
                    for ch in range(nch):
                        if ch + 1 < nch:
                            chunk_pre(ch + 1)
                        chunk_heads(ch)
                    V(lambda e, N=N: e.tensor_copy(out=xmT[:, :, 0:3], in_=xmT[:, :, N:N + 3]), ["xmT"], ["xmT"])
                    if blk == 0:
                        continue
                    for o in range(8):
                        bank = ps[1 + o % 2]; kb = f"ps{1 + o % 2}"
                        for c in range(8):
                            T(lambda e, c=c, o=o, bank=bank: e.matmul(bank[:, :N], lhsT=Wmlo[:, c, o * 128:(o + 1) * 128], rhs=hmT[:, c, :N], start=(c == 0), stop=(c == 7)), ["Wmlo", "hmT"], [kb])
                        V(lambda e, o=o, bank=bank: e.tensor_tensor(out=tmpf[:, :N], in0=bank[:, :N], in1=sgm[:, o, :N], op=ALU.mult), [kb, "sgm"], ["tmpfB"])
                        V(lambda e, o=o: e.tensor_tensor(out=mixT[:, o, :N], in0=tmpf[:, :N], in1=mixin[:, o, :N], op=ALU.add), ["tmpfB", "mixin"], ["mixTB"])
                    for i in range(ntile):
                        for half in range(2):
                            bank = ps[1 + half]; kb = f"ps{1 + half}"
                            for c in range(8):
                                T(lambda e, c=c, i=i, half=half, bank=bank: e.matmul(bank[:, :], lhsT=mixT[:, c, i * 128:(i + 1) * 128], rhs=Wout[:, c, half * 512:(half + 1) * 512], start=(c == 0), stop=(c == 7)), ["mixTB", "Wout"], [kb])
                            V(lambda e, i=i, half=half, bank=bank: e.tensor_tensor(out=h1t[:, half * 512:(half + 1) * 512], in0=bank[:, :], in1=xt[:, i, half * 512:(half + 1) * 512], op=ALU.add), [kb, f"xtB{i}"], ["h1tB"])
                        r0 = tok0 - 16 + i * 128
                        DM(lambda e, r0=r0: e.dma_start(out=h1[r0:r0 + 128, :], in_=h1t[:, :]), ["h1tB"], ["h1"])
                S.emit(final=(passes[-1] == "B"))

        if "C" in passes:
            conv_batch(100000)
            with ExitStack() as sc_:
                Wpq = sbt(sc_, "Wpq", [128, 8, 2048], BF16)
                n2g_t = sbt(sc_, "n2gC", [128, D], F32)
                fng_t = sbt(sc_, "fngC", [128, D], F32)
                kyb = sbt(sc_, "kyb", [128, 16, 128], BF16)
                sc_outer = sc_
                sc_ = ExitStack()
                sc_.__enter__()
                stg = sbt(sc_, "stgC", [128, 2, 2048], F32)
                load_weight(sc_, "Wpq", w_pq, 1024, 0, 2048, Wpq, 0, stg, "c")
                kyf = sbt(sc_, "kyf", [128, 16, 128], F32)
                DM(lambda e: e.dma_start(out=n2g_t[:], in_=n2g), [], ["n2gC"])
                DM(lambda e: e.dma_start(out=fng_t[:], in_=fng), [], ["fngC"])
                DM(lambda e: e.dma_start(out=kyf[:], in_=keysT), [], ["kyf"])
                V(lambda e: e.tensor_copy(out=kyb[:], in_=kyf[:]), ["kyf"], ["kyb"])
                S.emit()
                sc_.__exit__(None, None, None)
                sc_ = sc_outer
                h1t = [sbt(sc_, f"h1tC{i}", [128, D], F32) for i in range(2)]
                xn2d = [sbt(sc_, f"xn2_{i}", [128, D], BF16) for i in range(2)]
                xn2b = sbt(sc_, "xn2b", [128, D], BF16)
                xn2T = sbt(sc_, "xn2T", [128, 8, 128], BF16)
                qTb = sbt(sc_, "qTb", [128, 16, 128], BF16)
                scq = sbt(sc_, "scq", [128, 16, 128], F32)
                wk = sbt(sc_, "wkC", [128, 256], F32)
                svt = sbt(sc_, "svt", [128, 16, 16], F32)
                sit = sbt(sc_, "sit", [128, 16, 16], U32)
                sif = sbt(sc_, "sif", [128, 16, 16], F32)
                si0s = sbt(sc_, "si0s", [128, 8, 16], F32)
                cand = sbt(sc_, "cand", [128, 8, 256], F32)
                cs = sbt(sc_, "cs", [128, 8, 16], F32)
                cpos = sbt(sc_, "cpos", [128, 8, 16], U32)
                cab = sbt(sc_, "cab", [128, 2, 8, 16], U32)
                cabf = sbt(sc_, "cabf", [128, 2, 8, 16], F32)
                oh = sbt(sc_, "oh", [128, 8, 16, 16], F32)
                e01 = sbt(sc_, "e01", [128, 2, 8, 16], F32)
                eidf = sbt(sc_, "eidf", [128, 128], F32)
                eidi2 = [sbt(sc_, f"eidi{i}", [128, 128], I32) for i in range(2)]
                gex = sbt(sc_, "gex", [128, 8, 16], F32)
                gz = sbt(sc_, "gz", [128, 2, 8], F32)
                gate2 = [sbt(sc_, f"gate{i}", [128, 128], F32) for i in range(2)]
                dots = sbt(sc_, "dots", [128, 128], F32)
                actv = sbt(sc_, "actv", [128, 128], F32)
                wgt2 = [sbt(sc_, f"wgt{i}", [128, 128], F32) for i in range(2)]
                NG = 16
                uvg = [sbt(sc_, f"uvg{i}", [128, 2 * D], BF16) for i in range(NG)]
                prd = [sbt(sc_, f"prd{i}", [128, D], BF16) for i in range(4)]
                dgw = [sbt(sc_, f"dgw{i}", [128, 128], BF16) for i in range(2)]
                hout = sbt(sc_, "hout", [128, D], F32)
                outt = sbt(sc_, "outt", [128, D], F32)
                scrC = dict(tag="nC", junk=sbt(sc_, "junkC", [128, D], F32), ssq=sbt(sc_, "ssqC", [128, 1], F32), rs=sbt(sc_, "rsC", [128, 3], F32))
                iota16 = cstf[:, 2, 0:16]

                def top16(src_ap, ksrc, n, v_ap, i_ap, kv, ki):
                    V(lambda e: e.max(out=v_ap[:, 0:8], in_=src_ap), [ksrc], [kv])
                    V(lambda e: e.max_index(out=i_ap[:, 0:8], in_max=v_ap[:, 0:8], in_values=src_ap), [ksrc, kv], [ki])
                    V(lambda e: e.match_replace(out=wk[:, :n], in_to_replace=v_ap[:, 0:8], in_values=src_ap, imm_value=-1e30), [ksrc, kv], ["wkC"])
                    V(lambda e: e.max(out=v_ap[:, 8:16], in_=wk[:, :n]), ["wkC"], [kv])
                    V(lambda e: e.max_index(out=i_ap[:, 8:16], in_max=v_ap[:, 8:16], in_values=wk[:, :n]), ["wkC", kv], [ki])

                def phase_P1(ti):
                    hb = h1t[ti % 2]; khb = f"h1tC{ti % 2}"
                    xn2 = xn2d[ti % 2]; kxn2 = f"xn2_{ti % 2}"
                    DM(lambda e, ti=ti, hb=hb: e.dma_start(out=hb[:, :], in_=h1[ti * 128:(ti + 1) * 128, :]), ["h1"], [khb])
                    rms_rows(128, hb[:, :], n2g_t[:, :], [(xn2[:, :], kxn2)], khb, "nC", scrC)
                    transpose_rows(128, xn2[:, :], kxn2, xn2T[:, :, :], "xn2T")
                    for g4 in range(4):
                        bank = ps[1 + g4 % 2]; kb = f"ps{1 + g4 % 2}"
                        for hi4 in range(4):
                            hi = g4 * 4 + hi4
                            for k in range(8):
                                T(lambda e, hi=hi, hi4=hi4, k=k, bank=bank: e.matmul(bank[:, hi4 * 128:(hi4 + 1) * 128], lhsT=Wpq[:, k, hi * 128:(hi + 1) * 128], rhs=xn2T[:, k, :], start=(k == 0), stop=(k == 7)), ["Wpq", "xn2T"], [kb])
                        A(lambda e, g4=g4, bank=bank: e.activation(out=qTb[:, g4 * 4:(g4 + 1) * 4, :], in_=bank[:, :].rearrange("p (a t) -> p a t", a=4), func=AF.Copy), [kb], ["qTb"])
                    for g4 in range(4):
                        bank = ps[1 + g4 % 2]; kb = f"ps{1 + g4 % 2}"
                        for hi4 in range(4):
                            hi = g4 * 4 + hi4
                            T(lambda e, hi=hi, hi4=hi4, bank=bank: e.matmul(bank[:, hi4 * 128:(hi4 + 1) * 128], lhsT=qTb[:, hi, :], rhs=kyb[:, hi, :], start=True, stop=True), ["qTb", "kyb"], [kb])
                        A(lambda e, g4=g4, bank=bank: e.activation(out=scq[:, g4 * 4:(g4 + 1) * 4, :], in_=bank[:, :].rearrange("p (a t) -> p a t", a=4), func=AF.Copy), [kb], ["scq"])

                def phase_P2_gen(ti):
                    eidi = eidi2[ti % 2]; keid = f"eidi{ti % 2}"
                    gate = gate2[ti % 2]; kgate = f"gate{ti % 2}"
                    for hi in range(16):
                        top16(scq[:, hi, :], "scq", 128, svt[:, hi, :], sit[:, hi, :], "svt", "sit")
                        yield
                    V(lambda e: e.tensor_copy(out=sif[:], in_=sit[:]), ["sit"], ["sif"])
                    sv4 = svt[:, :, :].rearrange("p (h i) k -> p h i k", i=2)
                    sf4 = sif[:, :, :].rearrange("p (h i) k -> p h i k", i=2)
                    V(lambda e, sf4=sf4: e.tensor_scalar(out=si0s[:], in0=sf4[:, :, 0, :], scalar1=128.0, scalar2=None, op0=ALU.mult), ["sif"], ["si0s"])
                    V(lambda e, sv4=sv4: e.tensor_tensor(out=cand[:, :, :].rearrange("p h (a b) -> p h a b", a=16),
                                                         in0=sv4[:, :, 0, :].unsqueeze(3).to_broadcast([128, 8, 16, 16]),
                                                         in1=sv4[:, :, 1, :].unsqueeze(2).to_broadcast([128, 8, 16, 16]), op=ALU.add), ["svt"], ["cand"])
                    yield
                    for h in range(8):
                        top16(cand[:, h, :], "cand", 256, cs[:, h, :], cpos[:, h, :], "cs", "cpos")
                        yield
                    V(lambda e: e.tensor_single_scalar(out=cab[:, 0, :, :], in_=cpos[:], scalar=4, op=ALU.logical_shift_right), ["cpos"], ["cab"])
                    V(lambda e: e.tensor_single_scalar(out=cab[:, 1, :, :], in_=cpos[:], scalar=15, op=ALU.bitwise_and), ["cpos"], ["cab"])
                    V(lambda e: e.tensor_copy(out=cabf[:], in_=cab[:]), ["cab"], ["cabf"])
                    yield
                    for ab in range(2):
                        srcv = si0s[:, :, :] if ab == 0 else sf4[:, :, 1, :]
                        V(lambda e, ab=ab: e.tensor_tensor(out=oh[:], in0=cabf[:, ab, :, :].unsqueeze(3).to_broadcast([128, 8, 16, 16]),
                                                           in1=iota16.unsqueeze(1).unsqueeze(1).to_broadcast([128, 8, 16, 16]), op=ALU.is_equal), ["cabf", "cstf"], ["oh"])
                        V(lambda e, srcv=srcv: e.tensor_tensor(out=oh[:], in0=oh[:], in1=srcv.unsqueeze(2).to_broadcast([128, 8, 16, 16]), op=ALU.mult), ["oh", "si0s", "sif"], ["oh"])
                        V(lambda e, ab=ab: e.tensor_reduce(out=e01[:, ab, :, :], in_=oh[:], axis=AX.X, op=ALU.add), ["oh"], ["e01"])
                        yield
                    V(lambda e: e.tensor_tensor(out=eidf[:, :].rearrange("p (h k) -> p h k", h=8), in0=e01[:, 0, :, :], in1=e01[:, 1, :, :], op=ALU.add), ["e01"], ["eidf"])
                    V(lambda e, eidi=eidi: e.tensor_copy(out=eidi[:], in_=eidf[:]), ["eidf"], [keid])
                    V(lambda e: e.tensor_tensor(out=gex[:], in0=cs[:], in1=cs[:, :, 0:1].to_broadcast([128, 8, 16]), op=ALU.subtract), ["cs"], ["gex"])
                    A(lambda e: e.activation(out=gex[:], in_=gex[:], func=AF.Exp), ["gex"], ["gex"])
                    V(lambda e: e.tensor_reduce(out=gz[:, 0, :], in_=gex[:], axis=AX.X, op=ALU.add), ["gex"], ["gz0"])
                    V(lambda e: e.reciprocal(out=gz[:, 1, :], in_=gz[:, 0, :]), ["gz0"], ["gz1"])
                    V(lambda e, gate=gate: e.tensor_tensor(out=gate[:, :].rearrange("p (h k) -> p h k", h=8), in0=gex[:], in1=gz[:, 1, :].unsqueeze(2).to_broadcast([128, 8, 16]), op=ALU.mult), ["gex", "gz1"], [kgate])

                djunk2 = [sbt(sc_, f"djunkb{i}", [128, D], BF16) for i in range(2)]
                def phase_P2(ti):
                    for _ in phase_P2_gen(ti):
                        pass

                first_gather = [True]
                gctr = [0]

                def epilogue(ti):
                    acc0 = 5 if ti % 2 == 0 else 3
                    hb = h1t[ti % 2]; khb = f"h1tC{ti % 2}"
                    for half in range(2):
                        V(lambda e, half=half, hb=hb: e.tensor_tensor(out=hout[:, half * 512:(half + 1) * 512], in0=ps[acc0 + half][:, :], in1=hb[:, half * 512:(half + 1) * 512], op=ALU.add), [f"ps{acc0 + half}", khb], ["hout"])
                    rms_rows(128, hout[:, :], fng_t[:, :], [(outt[:, :], "outt")], "hout", "nF", scrC)
                    DM(lambda e, ti=ti: e.dma_start(out=out[ti * 128:(ti + 1) * 128, :], in_=outt[:, :]), ["outt"], ["out"])

                def tile_body(ti):
                    acc0 = 5 if ti % 2 == 0 else 3
                    xn2 = xn2d[ti % 2]; kxn2 = f"xn2_{ti % 2}"
                    eidi = eidi2[ti % 2]; keid = f"eidi{ti % 2}"
                    gate = gate2[ti % 2]; kgate = f"gate{ti % 2}"
                    wgt = wgt2[ti % 2]; kw = f"wgt{ti % 2}"
                    hb = h1t[ti % 2]; khb = f"h1tC{ti % 2}"
                    GS = 2
                    pending = []

                    def vside(g, bufs):
                        for k_, sl in enumerate(range(g * GS, (g + 1) * GS)):
                            b = bufs[k_]
                            dg = dgw[sl % 2]; kdg = f"dgw{sl % 2}"
                            V(lambda e, sl=sl, dg=dg, gate=gate: e.tensor_scalar(out=dg[:, :], in0=identb[:, :], scalar1=actv[:, sl:sl + 1], scalar2=gate[:, sl:sl + 1], op0=ALU.mult, op1=ALU.mult), ["identb", f"actv{g % 4}", kgate], [kdg])
                            for half in range(2):
                                T(lambda e, sl=sl, half=half, dg=dg, b=b: e.matmul(ps[acc0 + half][:, :], lhsT=dg[:, :], rhs=uvg[b][:, D + half * 512:D + (half + 1) * 512], start=(sl == 0), stop=(sl == 127)), [kdg, f"uvg{b}"], [f"ps{acc0 + half}"])

                    for g in range(128 // GS):
                        bufs = []
                        for sl in range(g * GS, (g + 1) * GS):
                            b = gctr[0] % NG
                            gctr[0] += 1
                            bufs.append(b)
                            extra = conv_keys if first_gather[0] else []
                            first_gather[0] = False
                            S.op("pool", lambda e, sl=sl, b=b, eidi=eidi: e.indirect_dma_start(out=uvg[b][:, :], out_offset=None, in_=uvb, in_offset=bass.IndirectOffsetOnAxis(ap=eidi[:, sl:sl + 1], axis=0)),
                                 reads=[keid] + extra, writes=[f"uvg{b}"], dma=True, ring=GRING)
                            pr = prd[sl % 4]; kpr = f"prd{sl % 4}"
                            V(lambda e, sl=sl, b=b, pr=pr: e.tensor_tensor(out=pr[:, :], in0=uvg[b][:, 0:D], in1=xn2[:, :], op=ALU.mult), [f"uvg{b}", kxn2], [kpr])
                            dj = djunk2[sl % 2]
                            A(lambda e, sl=sl, pr=pr, dj=dj: e.activation(out=dj[:, :], in_=pr[:, :], func=AF.Copy, accum_out=dots[:, sl:sl + 1]), [kpr], [f"djunk{sl % 2}", f"dots{g % 4}_{sl % 2}"])
                        gsl = slice(g * GS, (g + 1) * GS)
                        A(lambda e, gsl=gsl: e.activation(out=actv[:, gsl], in_=dots[:, gsl], func=AF.Gelu), [f"dots{g % 4}_0", f"dots{g % 4}_1"], [f"actv{g % 4}"])
                        if pending:
                            vside(*pending.pop(0))
                        pending.append((g, bufs))
                        if g == 1 and ti > 0:
                            epilogue(ti - 1)
                        if g == 1 and ti + 1 < ntileC:
                            phase_P1(ti + 1)
                            p2gen = phase_P2_gen(ti + 1)
                        if g >= 18 and ti + 1 < ntileC and ((g - 17) * 30) // 46 > ((g - 18) * 30) // 46:
                            next(p2gen, None)
                    while pending:
                        vside(*pending.pop(0))
                    if ti + 1 < ntileC:
                        for _ in p2gen:
                            pass

                phase_P1(0)
                phase_P2(0)
                for ti in range(ntileC):
                    tile_body(ti)
                epilogue(ntileC - 1)
                S.emit(final=True)
    return nc


def host_inputs(inp):
    f = lambda a: np.ascontiguousarray(np.asarray(a, dtype=np.float32))
    x = f(inp["x"]); meta = f(inp["meta_tokens"])
    rep = lambda v: np.ascontiguousarray(np.broadcast_to(f(v).reshape(1, -1), (128, f(v).size)))
    pm = lambda v, n: np.ascontiguousarray(f(v).reshape(n, 128).T)
    common = {}
    common["w_in"] = f(inp["w_in"][0]); common["w_glu"] = f(inp["ssm_w_glu"][0]); common["w_sso"] = f(inp["w_ssm_out"][0])
    common["w_mlo"] = f(inp["w_ml_out"][0]); common["w_out"] = f(inp["w_out"][0]); common["w_pq"] = f(inp["peer_w_q"][0])
    common["n1g"] = rep(inp["norm1_g"][0]); common["mlng"] = rep(inp["ml_norm_g"][0]); common["n2g"] = rep(inp["norm2_g"][0]); common["fng"] = rep(inp["final_norm_g"])
    def gp(a):
        return np.ascontiguousarray(f(a).reshape(16, 2, 64).transpose(1, 2, 0).reshape(128, 16))
    ldt = np.broadcast_to(f(inp["ssm_log_dt"][0])[:, None], (32, 64))
    common["s5a"] = np.ascontiguousarray(np.stack([gp(inp["ssm_a_re"][0]), gp(inp["ssm_a_im"][0]), gp(ldt)], axis=1))
    bwide = np.zeros((128, 2, 16, 128), np.float32)
    cwide = np.zeros((128, 2, 16, 128), np.float32)
    for ri, (bk, ck) in enumerate([("ssm_b_re", "ssm_c_re"), ("ssm_b_im", "ssm_c_im")]):
        B = f(inp[bk][0])
        C = f(inp[ck][0])
        for j in range(16):
            for g2 in range(2):
                g = 2 * j + g2
                gl = g % 8
                bwide[g2 * 64:(g2 + 1) * 64, ri, j, gl * 16:(gl + 1) * 16] = B[g]
                cwide[g2 * 64:(g2 + 1) * 64, ri, j, gl * 16:(gl + 1) * 16] = C[g].T
    common["s5bw"] = bwide; common["s5cw"] = cwide
    common["s5v"] = np.ascontiguousarray(np.stack([pm(inp["ssm_d"][0], 4), pm(inp["ssm_b_glu"][0], 4)], axis=1))
    cw_ = f(inp["ml_conv_w"][0])
    common["mlv"] = np.ascontiguousarray(np.stack([pm(cw_[0], 8), pm(cw_[1], 8), pm(cw_[2], 8), pm(cw_[3], 8), pm(inp["ml_conv_b"][0], 8), pm(inp["ml_skip"][0], 8)], axis=1))
    def blockdiag_chunks(w, transpose=False):
        w = f(w)
        o = np.zeros((8, 128, 128), np.float32)
        for n in range(256):
            c = n // 32; r = (n % 32) * 4
            o[c, r:r + 4, r:r + 4] = w[n].T if transpose else w[n]
        return o.transpose(1, 0, 2)
    common["mlbd"] = np.ascontiguousarray(np.stack([blockdiag_chunks(inp["ml_wq"][0]), blockdiag_chunks(inp["ml_wk"][0]), blockdiag_chunks(inp["ml_wv"][0]),
                                                    blockdiag_chunks(inp["ml_wq"][0], True), blockdiag_chunks(inp["ml_wk"][0], True)], axis=1))
    common["mlbdv"] = np.ascontiguousarray(blockdiag_chunks(inp["ml_wv"][0], True))
    common["mlwif"] = np.ascontiguousarray(f(inp["ml_w_if"][0]).reshape(24, 128, 8).transpose(1, 0, 2))
    common["mlbif"] = rep(inp["ml_b_if"][0])
    common["keysT"] = np.ascontiguousarray(f(inp["peer_sub_keys"][0]).reshape(16, 128, 128).transpose(2, 0, 1))
    common["u_tab"] = f(inp["peer_u"][0]); common["v_tab"] = f(inp["peer_v"][0])
    cst = np.zeros((128, 4, 128), np.float32)
    cst[:, 0, :] = np.eye(128, dtype=np.float32)
    m = np.triu(np.ones((64, 64), np.float32))
    cst[0:64, 1, 0:64] = m; cst[64:128, 1, 64:128] = m
    cst[:, 2, :] = np.arange(128, dtype=np.float32)[None, :]
    cst[:, 3, :] = 1.0
    common["cst"] = cst
    maps = []
    for b in range(8):
        d = dict(common)
        d["xh"] = np.ascontiguousarray(np.concatenate([meta, x[b]], axis=0))
        maps.append(d)
    return maps


def kernel(**inputs):
    nc = build()
    maps = host_inputs(inputs)
    res = run_bass_kernel_spmd(nc, maps, core_ids=list(range(8)))
    return np.stack([np.asarray(r["out"], dtype=np.float32) for r in res.results], axis=0)
```
